# Optimizing a Trainium2 kernel written in Bass

```python
import math
import jax, jax.numpy as jnp
from jax import lax
import numpy as np

D_MODEL = 1024
BATCH = 1
SEQ = 16384
DEPTH = 2
DEC_BATCH = 2
DEC_SEQ = 16384
PAST_LEN = 128

HEAD_DIM = 64
H_A = 8
D_A = H_A * HEAD_DIM
H_B = 8
H_KV = 2
GROUP = H_B // H_KV
D_B = H_B * HEAD_DIM
D_KV = H_KV * HEAD_DIM
D_MIX = D_A + D_B
W_IN_COLS = 3 * D_A + D_B + 2 * D_KV
GRID_W = 64
KH_MAX = 8
KW = 16
WINDOW = 128
BLOCK = 128
D_FF = 2816
CONV_W = 3
EPS = 1e-6

kernel_name = "hymba_natten_swa_convffn_encoder"


def rmsnorm(x, g):
    xf = x.astype(jnp.float32)
    inv = lax.rsqrt(jnp.mean(xf * xf, axis=-1, keepdims=True) + EPS)
    return (xf * inv).astype(x.dtype) * g


def neighbourhood_attention(q, k, v, rpb):
    B, T = q.shape[0], q.shape[1]
    rows = T // GRID_W
    kh = min(KH_MAX, rows)
    scale = 1.0 / math.sqrt(HEAD_DIM)
    q = q.reshape(B, rows, GRID_W, H_A, HEAD_DIM)
    k = k.reshape(B, rows, GRID_W, H_A, HEAD_DIM)
    v = v.reshape(B, rows, GRID_W, H_A, HEAD_DIM)
    cols = np.arange(GRID_W)
    col_start = np.clip(cols - KW // 2, 0, GRID_W - KW)
    col_idx = col_start[:, None] + np.arange(KW)
    dc = jnp.asarray(col_idx - cols[:, None] + (KW - 1))

    def row_step(r):
        rs = jnp.clip(r - kh // 2, 0, rows - kh)
        q_r = lax.dynamic_index_in_dim(q, r, axis=1, keepdims=False)
        k_w = lax.dynamic_slice_in_dim(k, rs, kh, axis=1)[:, :, col_idx]
        v_w = lax.dynamic_slice_in_dim(v, rs, kh, axis=1)[:, :, col_idx]
        dr = rs + jnp.arange(kh) - r + (KH_MAX - 1)
        bias = rpb[:, dr[:, None, None], dc[None]]
        bias = bias.transpose(0, 2, 1, 3).astype(jnp.float32)
        s = jnp.einsum('bqhd,brqwhd->bhqrw', q_r, k_w).astype(jnp.float32) * scale + bias[None]
        p = jax.nn.softmax(s.reshape(B, H_A, GRID_W, kh * KW), axis=-1)
        p = p.reshape(B, H_A, GRID_W, kh, KW).astype(v.dtype)
        return jnp.einsum('bhqrw,brqwhd->bqhd', p, v_w)

    out = lax.map(row_step, jnp.arange(rows))
    return out.transpose(1, 0, 2, 3, 4).reshape(B, T, D_A)


def sliding_window_attention(q, k, v, sinks):
    B, T = q.shape[0], q.shape[1]
    nb = T // BLOCK
    scale = 1.0 / math.sqrt(HEAD_DIM)
    slopes = jnp.exp2(-8.0 * jnp.arange(1, H_B + 1, dtype=jnp.float32) / H_B)
    slope = slopes.reshape(H_KV, GROUP)[None, :, :, None, None]
    sink = sinks.astype(jnp.float32).reshape(H_KV, GROUP)[None, :, :, None]
    q = q.reshape(B, nb, BLOCK, H_KV, GROUP, HEAD_DIM)
    pad = ((0, 0), (BLOCK, BLOCK), (0, 0), (0, 0))
    kp = jnp.pad(k, pad)
    vp = jnp.pad(v, pad)

    def block_step(i):
        q_i = lax.dynamic_index_in_dim(q, i, axis=1, keepdims=False)
        k_i = lax.dynamic_slice_in_dim(kp, i * BLOCK, 3 * BLOCK, axis=1)
        v_i = lax.dynamic_slice_in_dim(vp, i * BLOCK, 3 * BLOCK, axis=1)
        t = i * BLOCK + jnp.arange(BLOCK)
        s_pos = (i - 1) * BLOCK + jnp.arange(3 * BLOCK)
        dist = jnp.abs(t[:, None] - s_pos[None, :])
        valid = (dist <= WINDOW) & (s_pos >= 0)[None, :] & (s_pos < T)[None, :]
        logits = jnp.einsum('bqkgd,bskd->bkgqs', q_i, k_i).astype(jnp.float32) * scale
        logits = logits - slope * dist.astype(jnp.float32)
        logits = jnp.where(valid, logits, -jnp.inf)
        m = jnp.maximum(jnp.max(logits, axis=-1), sink)
        e = jnp.exp(logits - m[..., None])
        denom = jnp.sum(e, axis=-1) + jnp.exp(sink - m)
        p = (e / denom[..., None]).astype(v.dtype)
        return jnp.einsum('bkgqs,bskd->bqkgd', p, v_i).reshape(B, BLOCK, D_B)

    out = lax.map(block_step, jnp.arange(nb))
    return out.transpose(1, 0, 2, 3).reshape(B, T, D_B)


def dwconv3(u, w, b):
    up = jnp.pad(u, ((0, 0), (1, 1), (0, 0)))
    return up[:, :-2] * w[0] + up[:, 1:-1] * w[1] + up[:, 2:] * w[2] + b


def trunk(x, norm_mix, w_in, rpb, sinks, norm_grp, w_out, norm_ffn, w_up, conv_w, conv_b, w_down, norm_final):
    B, T = x.shape[0], x.shape[1]
    for l in range(DEPTH):
        h = rmsnorm(x, norm_mix[l])
        proj = h @ w_in[l]
        o = 0
        qa = proj[..., o:o + D_A].reshape(B, T, H_A, HEAD_DIM); o += D_A
        ka = proj[..., o:o + D_A].reshape(B, T, H_A, HEAD_DIM); o += D_A
        va = proj[..., o:o + D_A].reshape(B, T, H_A, HEAD_DIM); o += D_A
        qb = proj[..., o:o + D_B].reshape(B, T, H_B, HEAD_DIM); o += D_B
        kb = proj[..., o:o + D_KV].reshape(B, T, H_KV, HEAD_DIM); o += D_KV
        vb = proj[..., o:o + D_KV].reshape(B, T, H_KV, HEAD_DIM)
        out_a = neighbourhood_attention(qa, ka, va, rpb[l])
        out_b = sliding_window_attention(qb, kb, vb, sinks[l])
        out_a = rmsnorm(out_a, norm_grp[l, :D_A])
        out_b = rmsnorm(out_b, norm_grp[l, D_A:])
        x = x + jnp.concatenate([out_a, out_b], axis=-1) @ w_out[l]
        h = rmsnorm(x, norm_ffn[l])
        u = dwconv3(h @ w_up[l], conv_w[l], conv_b[l])
        gate, val = u[..., :D_FF], u[..., D_FF:]
        x = x + (jax.nn.silu(gate) * val) @ w_down[l]
    return rmsnorm(x, norm_final)


def setup_inputs(seed: int = 0) -> dict:
    key = jax.random.key(seed)
    ks = jax.random.split(key, 16)
    f32 = jnp.float32
    nrm = lambda k, shape, s: jax.random.normal(k, shape, f32) * s
    return {
        "x_prompt": nrm(ks[0], (BATCH, SEQ, D_MODEL), 1.0),
        "x_sample": nrm(ks[1], (DEC_BATCH, DEC_SEQ, D_MODEL), 1.0),
        "norm_mix": 1.0 + nrm(ks[2], (DEPTH, D_MODEL), 0.02),
        "w_in": nrm(ks[3], (DEPTH, D_MODEL, W_IN_COLS), D_MODEL ** -0.5),
        "rpb": nrm(ks[4], (DEPTH, H_A, 2 * KH_MAX - 1, 2 * KW - 1), 0.5),
        "sinks": nrm(ks[5], (DEPTH, H_B), 0.5),
        "norm_grp": 1.0 + nrm(ks[6], (DEPTH, D_MIX), 0.02),
        "w_out": nrm(ks[7], (DEPTH, D_MIX, D_MODEL), D_MIX ** -0.5),
        "norm_ffn": 1.0 + nrm(ks[8], (DEPTH, D_MODEL), 0.02),
        "w_up": nrm(ks[9], (DEPTH, D_MODEL, 2 * D_FF), D_MODEL ** -0.5),
        "conv_w": nrm(ks[10], (DEPTH, CONV_W, 2 * D_FF), CONV_W ** -0.5),
        "conv_b": nrm(ks[11], (DEPTH, 2 * D_FF), 0.02),
        "w_down": nrm(ks[12], (DEPTH, D_FF, D_MODEL), D_FF ** -0.5),
        "norm_final": 1.0 + nrm(ks[13], (D_MODEL,), 0.02),
    }


def reference(x_prompt, x_sample, norm_mix, w_in, rpb, sinks, norm_grp, w_out, norm_ffn, w_up, conv_w, conv_b, w_down, norm_final):
    y_prompt = trunk(x_prompt, norm_mix, w_in, rpb, sinks, norm_grp, w_out, norm_ffn, w_up, conv_w, conv_b, w_down, norm_final)
    y_sample = trunk(x_sample, norm_mix, w_in, rpb, sinks, norm_grp, w_out, norm_ffn, w_up, conv_w, conv_b, w_down, norm_final)
    return (y_prompt, y_sample)
```

```python
import numpy as np
from contextlib import ExitStack
import concourse.bass as bass
import concourse.mybir as mybir
from concourse.bass_utils import run_bass_kernel_spmd

F32 = mybir.dt.float32
BF16 = mybir.dt.bfloat16
AF = mybir.ActivationFunctionType
ALU = mybir.AluOpType
D = 1024
DFF = 2816
EPS = 1e-6
NEG = -30000.0
HALO = 6


class Cfg:
    def __init__(self, ncores, nseq, seq_blocks):
        self.NC, self.NSEQ, self.SB = ncores, nseq, seq_blocks
        self.TOTB = nseq * seq_blocks
        assert self.TOTB % ncores == 0
        self.BPC = self.TOTB // ncores
        self.NLB = self.BPC + 2 * HALO
        self.NTOK = self.NLB * 128
        pb = set()
        for c in range(ncores):
            for s in range(nseq + 1):
                o = s * seq_blocks - c * self.BPC
                if 0 <= o <= self.BPC:
                    pb.add(o + HALO)
        self.PB = sorted(pb)
        N = self.NLB
        self.rA = [(0, N), (3, N - 3)]
        self.rBC = [(2, N - 2), (5, N - 5)]
        self.rD = [(3, N - 3), (6, N - 6)]
        self.win = []
        for l in range(2):
            a0, a1 = self.rA[l]
            w = {}
            for lb in range(*self.rBC[l]):
                r0 = 2 * lb
                cand = (r0 - 4, 5)
                if (lb + 1) in self.PB:
                    cand = (r0 - 6, 6)
                elif lb in self.PB:
                    cand = (r0 - 4, 6)
                if cand[0] < 2 * a0 or cand[0] + 2 * cand[1] > 2 * a1:
                    cand = (r0 - 4, 5)
                assert cand[0] >= 2 * a0 and cand[0] + 2 * cand[1] <= 2 * a1
                w[lb] = cand
            self.win.append(w)
        self.tiles = []
        for l in range(2):
            d0, d1 = self.rD[l]
            cuts = sorted({d0, d1} | {p for p in self.PB if d0 < p < d1})
            tl = []
            for s0, s1 in zip(cuts[:-1], cuts[1:]):
                n = s1 - s0
                sizes = []
                while n > 0:
                    if n % 3 == 0 or n >= 5:
                        sizes.append(3); n -= 3
                    elif n >= 2:
                        sizes.append(2); n -= 2
                    else:
                        sizes.append(1); n -= 1
                b = s0
                for sz in sizes:
                    tl.append((b, sz)); b += sz
            self.tiles.append(tl)
        self.goff = {}
        o = 0
        for l in range(2):
            for nm, w in (("mix", 8), ("grp", 8), ("ffn", 8), ("cw0", 44), ("cw1", 44), ("cw2", 44), ("cb", 44)):
                self.goff[f"{nm}{l}"] = o; o += w
        self.goff["sink"] = o; o += 16
        self.goff["keep"] = o; o += len(self.PB)
        self.NG = o


def _vcol(v, n):
    return np.ascontiguousarray(np.asarray(v, np.float32).reshape(n, 128).T)


def build_tt(rpb_l):
    rpb_l = np.asarray(rpb_l, np.float32)
    tt = np.full((2, 64, 8, 16, 64), NEG, np.float32)
    qc = np.arange(64)
    cs = np.clip(qc - 8, 0, 48)
    kc = np.arange(64)
    valid = (kc[None, :] >= cs[:, None]) & (kc[None, :] < cs[:, None] + 16)
    dcc = np.clip(kc[None, :] - qc[:, None] + 15, 0, 30)
    for a in range(2):
        for s in range(16):
            off = 8 - s + a
            if abs(off) > 7:
                continue
            for hidx in range(8):
                e, pair = hidx // 4, hidx % 4
                h = 2 * pair + e
                slab = np.where(valid, rpb_l[h, off + 7][dcc], np.float32(NEG))
                tt[a, :, hidx, s, :] = slab.T
    return tt.reshape(128, 8, 1024)


def build_bsw():
    s = np.arange(128)[:, None, None, None]
    c = np.arange(3)[None, :, None, None]
    h = np.arange(8)[None, None, :, None]
    t = np.arange(128)[None, None, None, :]
    dist = np.abs(t - (s + 128 * (c - 1)))
    slope = np.exp2(-(h + 1.0))
    val = -(slope * dist)
    return np.where(dist <= 128, val, NEG).astype(np.float32)


def na_masks(cfg, core):
    NLB = cfg.NLB
    m = np.full((2, NLB, 6, 2, 2), NEG, np.float32)
    grow0 = 2 * (core * cfg.BPC - HALO)
    RPS = 2 * cfg.SB
    for l in range(2):
        for lb, (w0, nch) in cfg.win[l].items():
            for p in range(2):
                gq = grow0 + 2 * lb + p
                real = 0 <= gq < 2 * cfg.TOTB
                oks = np.zeros((6, 2), bool)
                for c in range(nch):
                    for a in range(2):
                        gk = grow0 + w0 + 2 * c + a
                        if real:
                            sq = gq // RPS
                            rs = int(np.clip(gq % RPS - 4, 0, RPS - 8)) + sq * RPS
                            oks[c, a] = rs <= gk < rs + 8
                        else:
                            oks[c, a] = -4 <= gk - gq <= 3
                if not oks.any():
                    for c in range(nch):
                        for a in range(2):
                            gk = grow0 + w0 + 2 * c + a
                            oks[c, a] = -4 <= gk - gq <= 3
                m[l, lb, :, p, :] = np.where(oks, 0.0, NEG)
    mm = np.transpose(m, (4, 0, 1, 2, 3)).reshape(2, -1)
    return np.ascontiguousarray(np.repeat(mm, 64, axis=0)).astype(np.float32)


def sw_masks(cfg, core):
    NLB = cfg.NLB
    m = np.zeros((NLB, 3), np.float32)
    for lb in range(NLB):
        gb = core * cfg.BPC - HALO + lb
        real = 0 <= gb < cfg.TOTB
        for c in range(3):
            gk = gb + c - 1
            if real:
                ok = (0 <= gk < cfg.TOTB) and (gk // cfg.SB == gb // cfg.SB)
            else:
                ok = True
            m[lb, c] = 0.0 if ok else NEG
    return np.ascontiguousarray(np.broadcast_to(m.reshape(1, -1), (128, NLB * 3))).astype(np.float32)


def host_inputs(cfg, xcat, p):
    shared = {
        "w_in": np.ascontiguousarray(p["w_in"], np.float32),
        "w_out": np.ascontiguousarray(p["w_out"], np.float32),
        "w_up": np.ascontiguousarray(p["w_up"], np.float32),
        "w_down": np.ascontiguousarray(p["w_down"], np.float32),
        "tt": np.stack([build_tt(p["rpb"][l]) for l in range(2)]),
        "bsw": build_bsw(),
        "ident": np.eye(128, dtype=np.float32),
        "gfin": np.ascontiguousarray(np.broadcast_to(np.asarray(p["norm_final"], np.float32)[None, :], (128, D))),
    }
    maps = []
    for c in range(cfg.NC):
        g = np.zeros((128, cfg.NG), np.float32)
        for l in range(2):
            g[:, cfg.goff[f"mix{l}"]:][:, :8] = _vcol(p["norm_mix"][l], 8)
            g[:, cfg.goff[f"grp{l}"]:][:, :8] = _vcol(p["norm_grp"][l], 8)
            g[:, cfg.goff[f"ffn{l}"]:][:, :8] = _vcol(p["norm_ffn"][l], 8)
            for j in range(3):
                g[:, cfg.goff[f"cw{j}{l}"]:][:, :44] = _vcol(p["conv_w"][l][j], 44)
            g[:, cfg.goff[f"cb{l}"]:][:, :44] = _vcol(p["conv_b"][l], 44)
        g[:, cfg.goff["sink"]:][:, :16] = np.asarray(p["sinks"], np.float32).reshape(1, 16)
        for j, pb in enumerate(cfg.PB):
            gb = c * cfg.BPC - HALO + pb
            realb = (0 <= gb <= cfg.TOTB) and (gb % cfg.SB == 0)
            g[:, cfg.goff["keep"] + j] = 0.0 if realb else 1.0
        xin = np.zeros((cfg.NTOK, D), np.float32)
        t0 = (c * cfg.BPC - HALO) * 128
        lo, hi = max(t0, 0), min(t0 + cfg.NTOK, xcat.shape[0])
        xin[lo - t0:hi - t0] = xcat[lo:hi]
        d = dict(shared)
        d.update({"xin": xin, "gcols": g, "mbna": na_masks(cfg, c), "mbsw": sw_masks(cfg, c)})
        maps.append(d)
    return maps


class Eng:
    def __init__(self, nc, eng, name):
        self.e = eng
        self.sem = nc.alloc_semaphore("pg_" + name)
        self.n = 0
        self.seen = {}

    def wait(self, *evs):
        for ev in evs:
            if ev is None:
                continue
            if isinstance(ev, list):
                self.wait(*ev)
                continue
            sem, v = ev
            k = id(sem)
            if self.seen.get(k, 0) >= v:
                continue
            self.seen[k] = v
            self.e.wait_ge(sem, v)

    def sig(self, ins):
        self.n += 1
        ins.then_inc(self.sem, 1)
        return (self.sem, self.n)


class DSem:
    def __init__(self, nc, name):
        self.sem = nc.alloc_semaphore("dm_" + name)
        self.n = 0

    def add(self, ins):
        ins.then_inc(self.sem, 16)
        self.n += 16
        return (self.sem, self.n)

    def ev(self):
        return (self.sem, self.n) if self.n else None


class Rot:
    def __init__(self, banks):
        self.b = banks
        self.free = [None] * len(banks)
        self.i = 0

    def get(self):
        j = self.i % len(self.b)
        self.i += 1
        return j, self.b[j], self.free[j]

    def rel(self, j, ev):
        self.free[j] = ev


def build_program(cfg):
    nc = bass.Bass("TRN2", target_bir_lowering=False)
    NLB, NTOK, BPC = cfg.NLB, cfg.NTOK, cfg.BPC
    G = cfg.goff

    def din(name, shape):
        return nc.dram_tensor(name, list(shape), F32, kind="ExternalInput").ap()

    xin = din("xin", [NTOK, D])
    gcols_d = din("gcols", [128, cfg.NG])
    gfin_d = din("gfin", [128, D])
    tt_d = din("tt", [2, 128, 8, 1024])
    bsw_d = din("bsw", [128, 3, 8, 128])
    mbna_d = din("mbna", [128, 2 * NLB * 12])
    mbsw_d = din("mbsw", [128, NLB * 3])
    ident_d = din("ident", [128, 128])
    w_in_d = din("w_in", [2, D, 2304])
    w_out_d = din("w_out", [2, D, D])
    w_up_d = din("w_up", [2, D, 2 * DFF])
    w_down_d = din("w_down", [2, DFF, D])
    y_d = nc.dram_tensor("y", [BPC * 128, D], F32, kind="ExternalOutput").ap()

    qk_scr = nc.dram_tensor("qk_scr", [13 * 128, NTOK], BF16).ap()
    v_scr = nc.dram_tensor("v_scr", [NTOK, 768], BF16).ap()
    xmid_scr = nc.dram_tensor("xmid_scr", [NTOK, D], F32).ap()
    hmT_scr = nc.dram_tensor("hmT_scr", [D, NTOK + 2], BF16).ap()
    x1_scr = nc.dram_tensor("x1_scr", [NTOK, D], F32).ap()

    PE = Eng(nc, nc.tensor, "pe")
    ACT = Eng(nc, nc.scalar, "act")
    DVE = Eng(nc, nc.vector, "dve")
    POOL = Eng(nc, nc.gpsimd, "pool")
    SP = Eng(nc, nc.sync, "sp")
    ENGS = [PE, ACT, DVE, POOL, SP]
    sync, scalar, vector, gpsimd, tensor = nc.sync, nc.scalar, nc.vector, nc.gpsimd, nc.tensor

    ds = {n: DSem(nc, n) for n in ["c", "w0", "w1", "w2", "w3", "x0", "x1", "q0", "q1", "b0", "b1", "s0", "s1",
                                   "h0", "h1", "m0", "m1", "m2", "o0", "o1"]}
    qk_v = qk_scr.rearrange("(c p) t -> p c t", p=128)
    hmT_v = hmT_scr.rearrange("(k p) t -> p k t", p=128)

    with ExitStack() as top:
        uid = [0]

        def sb(es, name, shape, dty):
            uid[0] += 1
            return es.enter_context(nc.sbuf_tensor(f"{name}_u{uid[0]}", list(shape), dty))

        def ps(es, name, shape, dty):
            uid[0] += 1
            return es.enter_context(nc.psum_tensor(f"{name}_u{uid[0]}", list(shape), dty))

        gc = sb(top, "gc", [128, cfg.NG], F32)
        mbna = sb(top, "mbna_sb", [128, 2 * NLB * 12], F32)
        mbsw = sb(top, "mbsw_sb", [128, NLB * 3], F32)
        idf = sb(top, "idf", [128, 128], F32)
        idb = sb(top, "idb", [128, 128], BF16)
        ones = sb(top, "ones", [128, 128], BF16)
        esink = sb(top, "esink", [128, 16], F32)
        dummy = sb(top, "k_dummy", [128, 2], F32)

        sync.dma_start(out=gc[:], in_=gcols_d[:, :]).then_inc(ds["c"].sem, 16)
        sync.dma_start(out=mbna[:], in_=mbna_d[:, :]).then_inc(ds["c"].sem, 16)
        sync.dma_start(out=mbsw[:], in_=mbsw_d[:, :]).then_inc(ds["c"].sem, 16)
        sync.dma_start(out=idf[:], in_=ident_d[:, :]).then_inc(ds["c"].sem, 16)
        ds["c"].n = 64
        ev_c = ds["c"].ev()
        DVE.wait(ev_c)
        vector.tensor_copy(out=idb[:], in_=idf[:])
        ev_const = DVE.sig(vector.memset(ones[:], 1.0))
        ACT.wait(ev_c)
        ev_sink = ACT.sig(scalar.activation(out=esink[:], in_=gc[:, G["sink"]:G["sink"] + 16], func=AF.Exp))
        esrow = sb(top, "esrow", [1, 2048], BF16)
        epsc = sb(top, "epsc", [128, 2], F32)
        vector.memset(epsc[:], EPS)
        DVE.wait(ev_sink, ev_const)
        for hh in range(16):
            ins = vector.tensor_scalar(out=esrow[0:1, hh * 128:(hh + 1) * 128], in0=ones[0:1, :], scalar1=esink[0:1, hh:hh + 1], scalar2=None, op0=ALU.mult)
        ev_esrow = DVE.sig(ins)
        for E in ENGS:
            E.wait(ev_c, ev_const, ev_sink, ev_esrow)

        state = {"phase_ev": None}

        def phase_begin():
            for E in ENGS:
                E.wait(state["phase_ev"])

        def phase_end(store_evs):
            POOL.wait(*store_evs)
            state["phase_ev"] = POOL.sig(gpsimd.memset(dummy[:], 0.0))

        def phase_A(l, xsrc):
            a0, a1 = cfg.rA[l]
            groups = [list(range(b, min(b + 4, a1))) for b in range(a0, a1, 4)]
            phase_begin()
            with ExitStack() as es:
                wres = sb(es, "a_wres", [128, 8, 2432], BF16)
                stg = [sb(es, f"a_stg{i}", [128, 2304], F32) for i in range(2)]
                xg = [sb(es, f"a_xg{i}", [128, 4, D], F32) for i in range(2)]
                hb = [sb(es, f"a_hb{i}", [128, D], BF16) for i in range(2)]
                hT = [sb(es, f"a_hT{i}", [128, 8, 512], BF16) for i in range(2)]
                qko = [sb(es, f"a_qko{i}", [128, 13, 512], BF16) for i in range(2)]
                vo = [sb(es, f"a_vo{i}", [128, 4, 768], BF16) for i in range(2)]
                junk = sb(es, "a_junk", [128, D], BF16)
                st = sb(es, "a_st", [128, 12], F32)
                tp = ps(es, "a_tp", [128, 8, 128], BF16)
                rot = Rot([ps(es, f"a_pm{i}", [128, 512], F32) for i in range(6)])

                stg_free = [None, None]
                ldw = [ds["w0"], ds["w1"]]
                for k in range(8):
                    s = k % 2
                    SP.wait(stg_free[s])
                    ev = ldw[s].add(sync.dma_start(out=stg[s][:], in_=w_in_d[l, k * 128:(k + 1) * 128, :]))
                    DVE.wait(ev)
                    gk = gc[:, G[f"mix{l}"] + k:G[f"mix{l}"] + k + 1]
                    src, dst = stg[s], wres
                    vector.tensor_scalar(out=dst[:, k, 0:512], in0=src[:, 0:512], scalar1=gk, scalar2=0.125, op0=ALU.mult, op1=ALU.mult)
                    vector.tensor_scalar(out=dst[:, k, 512:1024], in0=src[:, 512:1024], scalar1=gk, scalar2=None, op0=ALU.mult)
                    vector.tensor_scalar(out=dst[:, k, 1024:1536].rearrange("p (j a c) -> p j a c", j=4, a=2),
                                         in0=src[:, 1536:2048].rearrange("p (a j c) -> p j a c", a=2, j=4),
                                         scalar1=gk, scalar2=0.125, op0=ALU.mult, op1=ALU.mult)
                    vector.tensor_scalar(out=dst[:, k, 1536:1664], in0=src[:, 2048:2176], scalar1=gk, scalar2=None, op0=ALU.mult)
                    vector.tensor_scalar(out=dst[:, k, 1664:2176], in0=src[:, 1024:1536], scalar1=gk, scalar2=None, op0=ALU.mult)
                    dv = dst[:, k, 2176:2432].rearrange("p (g d c) -> p g d c", g=2, d=2)
                    sv = src[:, 2176:2304].rearrange("p (g c) -> p g c", g=2)
                    vector.tensor_scalar(out=dv[:, :, 0, :], in0=sv, scalar1=gk, scalar2=None, op0=ALU.mult)
                    ins = vector.tensor_scalar(out=dv[:, :, 1, :], in0=sv, scalar1=gk, scalar2=None, op0=ALU.mult)
                    stg_free[s] = DVE.sig(ins)
                wready = stg_free[1]
                PE.wait(wready)

                ldx = [ds["x0"], ds["x1"]]
                stq = [ds["q0"], ds["q1"]]
                xg_free = [None, None]; hb_free = [None, None]; hT_free = [None, None]
                out_free = [None, None]
                stat_free = [None] * 4
                store_evs = []
                A = {"bcount": 0, "tp_free": None}
                def stage1_begin(gi, blks):
                    s = gi % 2
                    nb = len(blks); T = 128 * nb; tok0 = blks[0] * 128
                    SP.wait(xg_free[s])
                    ev_x = ldx[s].add(sync.dma_start(out=xg[s][:, 0:nb, :],
                                                     in_=xsrc[tok0:tok0 + T, :].rearrange("(b p) d -> p b d", p=128)))
                    return {"s": s, "nb": nb, "T": T, "tok0": tok0, "evs_hT": [], "ev_x": ev_x, "done": 0}

                def stage1_block(cx, part="both"):
                    if cx is None or cx["done"] >= cx["nb"]:
                        return
                    if part in ("norm", "both") and cx.get("pend") is None:
                        stage1_norm(cx)
                    if part in ("tr", "both") and cx.get("pend") is not None:
                        stage1_tr(cx)

                def stage1_norm(cx):
                    bi = cx["done"]
                    s, nb, ev_x = cx["s"], cx["nb"], cx["ev_x"]
                    if True:
                        q = A["bcount"] % 4; sc = q * 3; hs = A["bcount"] % 2; A["bcount"] += 1
                        ACT.wait(ev_x, stat_free[q])
                        e = ACT.sig(scalar.activation(out=junk[:], in_=xg[s][:, bi, :], func=AF.Square, accum_out=st[:, sc:sc + 1]))
                        ACT.wait(e)
                        e = ACT.sig(scalar.activation(out=st[:, sc + 1:sc + 2], in_=st[:, sc:sc + 1], func=AF.Sqrt, bias=EPS, scale=1.0 / D))
                        DVE.wait(e)
                        e = DVE.sig(vector.reciprocal(out=st[:, sc + 2:sc + 3], in_=st[:, sc + 1:sc + 2]))
                        DVE.wait(e, hb_free[hs], ev_x)
                        e_h = DVE.sig(vector.tensor_scalar(out=hb[hs][:], in0=xg[s][:, bi, :], scalar1=st[:, sc + 2:sc + 3], scalar2=None, op0=ALU.mult))
                        stat_free[q] = e_h
                        if bi == nb - 1:
                            xg_free[s] = e_h
                        cx["pend"] = (bi, hs, e_h)

                def stage1_tr(cx):
                    bi, hs, e_h = cx["pend"]
                    cx["pend"] = None
                    cx["done"] += 1
                    s, evs_hT = cx["s"], cx["evs_hT"]
                    if True:
                        PE.wait(e_h, A["tp_free"])
                        for k in range(8):
                            ins = tensor.transpose(out=tp[:, k, :], in_=hb[hs][:, k * 128:(k + 1) * 128], identity=idb[:])
                        e_t = PE.sig(ins)
                        hb_free[hs] = e_t
                        ACT.wait(e_t, hT_free[s])
                        e_c = ACT.sig(scalar.copy(out=hT[s][:, :, bi * 128:(bi + 1) * 128], in_=tp[:]))
                        A["tp_free"] = e_c
                        evs_hT.append(e_c)

                def stage2(gi, blks, ctx, nxt):
                    s, nb, T, tok0, evs_hT = ctx["s"], ctx["nb"], ctx["T"], ctx["tok0"], ctx["evs_hT"]
                    evac = []
                    n_ev = 0
                    for cc in range(13):
                        if cc in (0, 3, 6, 9):
                            stage1_block(nxt, "norm")
                        elif cc in (2, 5, 8, 11):
                            stage1_block(nxt, "tr")
                        j, bank, fr = rot.get()
                        PE.wait(fr, *evs_hT)
                        for k in range(8):
                            ins = tensor.matmul(bank[:, 0:T], lhsT=wres[:, k, cc * 128:(cc + 1) * 128], rhs=hT[s][:, k, 0:T],
                                                start=(k == 0), stop=(k == 7))
                        e_m = PE.sig(ins)
                        E = ACT if n_ev % 2 == 0 else DVE
                        n_ev += 1
                        E.wait(e_m, out_free[s])
                        if E is ACT:
                            e_e = E.sig(scalar.copy(out=qko[s][:, cc, 0:T], in_=bank[:, 0:T]))
                        else:
                            e_e = E.sig(vector.tensor_copy(out=qko[s][:, cc, 0:T], in_=bank[:, 0:T]))
                        rot.rel(j, e_e)
                        evac.append(e_e)
                    for bi in range(nb):
                        for (c0, c1, w) in ((1664, 2176, 512), (2176, 2432, 256)):
                            j, bank, fr = rot.get()
                            PE.wait(fr)
                            for k in range(8):
                                ins = tensor.matmul(bank[:, 0:w], lhsT=hT[s][:, k, bi * 128:(bi + 1) * 128], rhs=wres[:, k, c0:c1],
                                                    start=(k == 0), stop=(k == 7))
                            e_m = PE.sig(ins)
                            E = ACT if n_ev % 2 == 0 else DVE
                            n_ev += 1
                            E.wait(e_m, out_free[s])
                            o0 = 0 if w == 512 else 512
                            if E is ACT:
                                e_e = E.sig(scalar.copy(out=vo[s][:, bi, o0:o0 + w], in_=bank[:, 0:w]))
                            else:
                                e_e = E.sig(vector.tensor_copy(out=vo[s][:, bi, o0:o0 + w], in_=bank[:, 0:w]))
                            rot.rel(j, e_e)
                            evac.append(e_e)
                    hT_free[s] = e_m
                    POOL.wait(*evac)
                    stq[s].add(gpsimd.dma_start(out=qk_v[:, :, tok0:tok0 + T], in_=qko[s][:, :, 0:T]))
                    stq[s].add(gpsimd.dma_start(out=v_scr[tok0:tok0 + T, :].rearrange("(b p) f -> p b f", p=128), in_=vo[s][:, 0:nb, :]))
                    out_free[s] = stq[s].ev()

                ctx_next = stage1_begin(0, groups[0])
                while ctx_next["done"] < ctx_next["nb"]:
                    stage1_block(ctx_next)
                for gi, blks in enumerate(groups):
                    ctx = ctx_next
                    ctx_next = stage1_begin(gi + 1, groups[gi + 1]) if gi + 1 < len(groups) else None
                    stage2(gi, blks, ctx, ctx_next)
                    while ctx_next is not None and ctx_next["done"] < ctx_next["nb"]:
                        stage1_block(ctx_next)
                phase_end([stq[0].ev(), stq[1].ev()])

        def phase_BC(l, xsrc):
            b0, b1 = cfg.rBC[l]
            phase_begin()
            with ExitStack() as es:
                wo = sb(es, "b_wo", [128, 8, D], BF16)
                ttb = sb(es, "b_tt", [128, 8, 1024], BF16)
                bswb = sb(es, "b_bsw", [128, 3, 8, 128], BF16)
                stg = [sb(es, f"b_stg{i}", [128, 1024], F32) for i in range(2)]
                kaT = [sb(es, f"b_kaT{i}", [128, 4, 768], BF16) for i in range(2)]
                qaT = [sb(es, f"b_qaT{i}", [128, 4, 128], BF16) for i in range(2)]
                qbT = [sb(es, f"b_qbT{i}", [128, 4, 128], BF16) for i in range(2)]
                kbT = [sb(es, f"b_kbT{i}", [128, 384], BF16) for i in range(2)]
                vaw = [sb(es, f"b_vaw{i}", [128, 6, 512], BF16) for i in range(2)]
                vbw = [sb(es, f"b_vbw{i}", [128, 3, 256], BF16) for i in range(2)]
                xb = [sb(es, f"b_xb{i}", [128, D], F32) for i in range(3)]
                xb_free = [None, None, None]
                ldxb = [ds["m0"], ds["m1"], ds["m2"]]
                PTn = sb(es, "b_PTn", [128, 6, 8, 128], BF16)
                PTs = sb(es, "b_PTs", [128, 3, 8, 128], BF16)
                rec_n = sb(es, "b_recn", [128, 8, 128], F32)
                rawPn = sb(es, "b_rawPn", [128, 2, 512], F32)
                rawDn = sb(es, "b_rawDn", [128, 2, 512], F32)
                rawPs = sb(es, "b_rawPs", [128, 2, 512], F32)
                rawDs = sb(es, "b_rawDs", [128, 2, 512], F32)
                rec_s = sb(es, "b_recs", [128, 8, 128], F32)
                o_un = sb(es, "b_oun", [128, 8, 128], F32)
                sqb = sb(es, "b_sqb", [128, 8, 128], BF16)
                ab_s = sb(es, "b_abs", [128, 256], F32)
                ab = sb(es, "b_ab", [128, 256], F32)
                o_bf = sb(es, "b_obf", [128, 8, 128], BF16)
                xmid = [sb(es, f"b_xmid{i}", [128, D], F32) for i in range(2)]
                junk = sb(es, "b_junk", [128, D], BF16)
                hm = sb(es, "b_hm", [128, D], BF16)
                hmT = [sb(es, f"b_hmT{i}", [128, 8, 128], BF16) for i in range(2)]
                st = sb(es, "b_st", [128, 4], F32)
                S_rot = Rot([ps(es, f"b_S{i}", [128, 512], F32) for i in range(2)])
                X = ps(es, "b_X", [128, 512], F32)
                P = [ps(es, f"b_P{i}", [128, 512], F32) for i in range(2)]
                Dk = [ps(es, f"b_D{i}", [128, 512], F32) for i in range(2)]
                tp = ps(es, "b_tp", [128, 8, 128], BF16)

                stg_free = [None, None]
                ldw = [ds["w0"], ds["w1"]]
                jobs = []
                for k in range(8):
                    jobs.append(("wo", k))
                for h in range(8):
                    jobs.append(("tt", h))
                for c in range(3):
                    jobs.append(("bsw", c))
                for n, (kind, k) in enumerate(jobs):
                    s = n % 2
                    SP.wait(stg_free[s])
                    if kind == "wo":
                        src_ap = w_out_d[l, k * 128:(k + 1) * 128, :]
                    elif kind == "tt":
                        src_ap = tt_d[l, :, k, :]
                    else:
                        src_ap = bsw_d[:, k, :, :].rearrange("p h q -> p (h q)")
                    ev = ldw[s].add(sync.dma_start(out=stg[s][:], in_=src_ap))
                    DVE.wait(ev)
                    if kind == "wo":
                        gk = gc[:, G[f"grp{l}"] + k:G[f"grp{l}"] + k + 1]
                        ins = vector.tensor_scalar(out=wo[:, k, :], in0=stg[s][:], scalar1=gk, scalar2=None, op0=ALU.mult)
                    elif kind == "tt":
                        ins = vector.tensor_copy(out=ttb[:, k, :], in_=stg[s][:])
                    else:
                        ins = vector.tensor_copy(out=bswb[:, k, :, :].rearrange("p h q -> p (h q)"), in_=stg[s][:])
                    stg_free[s] = DVE.sig(ins)
                wready = [stg_free[0], stg_free[1]]
                PE.wait(wready)

                ldb = [ds["b0"], ds["b1"]]
                stb = [ds["s0"], ds["s1"]]
                blk_free = [None, None]
                out_free = [None, None]
                fr = {"PTn": None, "PTs": None, "PD": None, "oun": None, "sqb": None, "obf": None, "hm": None, "tp": None, "X": None, "rawn": None, "raws": None}

                def make_out_pieces(i, lb, s, tok0, e_sw, e_pvs, ev_x, xs):
                    c = {}

                    def piece0():
                        ACT.wait(e_sw, fr["sqb"])
                        c["e_sq"] = ACT.sig(scalar.activation(out=sqb[:].rearrange("p k q -> p (k q)"), in_=o_un[:].rearrange("p k q -> p (k q)"), func=AF.Square))
                        PE.wait(c["e_sq"], fr["X"])
                        for grp in range(2):
                            for kk in range(4):
                                ins = tensor.matmul(X[:, grp * 128:(grp + 1) * 128], lhsT=ones[:], rhs=sqb[:, grp * 4 + kk, :],
                                                    start=(kk == 0), stop=(kk == 3), skip_group_check=True)
                        e_ss = PE.sig(ins)
                        fr["sqb"] = e_ss
                        ACT.wait(e_ss)
                        c["e_q"] = ACT.sig(scalar.activation(out=ab_s[:], in_=X[:, 0:256], func=AF.Ln, bias=epsc[:, 0:1], scale=1.0 / 512))
                        ACT.wait(c["e_q"])
                        e_ab = ACT.sig(scalar.activation(out=ab[:], in_=ab_s[:], func=AF.Exp, scale=-0.5))
                        POOL.wait(e_ab, e_sw, fr["obf"])
                        DVE.wait(e_ab, e_sw, fr["obf"])
                        evs_ = []
                        for k in range(8):
                            EE, eng_ = (DVE, vector) if k < 4 else (POOL, gpsimd)
                            ins = eng_.tensor_tensor(out=o_bf[:, k, :], in0=o_un[:, k, :], in1=ab[:, (k // 4) * 128:(k // 4 + 1) * 128], op=ALU.mult)
                            if k % 4 == 3:
                                evs_.append(EE.sig(ins))
                        c["e_ob"] = evs_
                        fr["oun"] = [c["e_sq"]] + evs_

                    def piece1(hf):
                        PE.wait(c["e_ob"], c["e_q"], c.get("e_x0"))
                        for k in range(8):
                            ins = tensor.matmul(X[:, 0:512], lhsT=o_bf[:, k, :], rhs=wo[:, k, hf * 512:(hf + 1) * 512],
                                                start=(k == 0), stop=(k == 7), skip_group_check=True)
                        e_op = PE.sig(ins)
                        DVE.wait(e_op, out_free[s], ev_x)
                        e_add = DVE.sig(vector.tensor_tensor(out=xmid[s][:, hf * 512:(hf + 1) * 512], in0=X[:, 0:512],
                                                             in1=xb[xs][:, hf * 512:(hf + 1) * 512], op=ALU.add))
                        if hf == 0:
                            c["e_x0"] = e_add
                        else:
                            fr["obf"] = e_op
                            c["e_xm"] = e_add
                            fr["X"] = e_add
                            xb_free[xs] = e_add

                    def piece2():
                        ACT.wait(c["e_xm"])
                        e = ACT.sig(scalar.activation(out=junk[:], in_=xmid[s][:], func=AF.Square, accum_out=st[:, 0:1]))
                        ACT.wait(e)
                        e = ACT.sig(scalar.activation(out=st[:, 1:2], in_=st[:, 0:1], func=AF.Ln, bias=epsc[:, 0:1], scale=1.0 / D))
                        ACT.wait(e)
                        e = ACT.sig(scalar.activation(out=st[:, 2:3], in_=st[:, 1:2], func=AF.Exp, scale=-0.5))
                        DVE.wait(e, fr["hm"])
                        c["e_hm"] = DVE.sig(vector.tensor_scalar(out=hm[:], in0=xmid[s][:], scalar1=st[:, 2:3], scalar2=None, op0=ALU.mult))

                    def piece3():
                        PE.wait(c["e_hm"], fr["tp"])
                        for k in range(8):
                            ins = tensor.transpose(out=tp[:, k, :], in_=hm[:, k * 128:(k + 1) * 128], identity=idb[:])
                        e_t = PE.sig(ins)
                        fr["hm"] = e_t
                        ACT.wait(e_t, out_free[s])
                        e_c = ACT.sig(scalar.copy(out=hmT[s][:], in_=tp[:]))
                        fr["tp"] = e_c
                        POOL.wait(c["e_xm"], e_c, c["e_hm"])
                        stb[s].add(gpsimd.dma_start(out=xmid_scr[tok0:tok0 + 128, :], in_=xmid[s][:]))
                        stb[s].add(gpsimd.dma_start(out=hmT_v[:, :, 1 + tok0:1 + tok0 + 128], in_=hmT[s][:]))
                        out_free[s] = stb[s].ev()

                    return [piece0, lambda: piece1(0), lambda: piece1(1), piece2, piece3]

                pending = None
                pending3 = None
                for i, lb in enumerate(range(b0, b1)):
                    s = i % 2
                    tok0 = lb * 128
                    w0, nch = cfg.win[l][lb]
                    kc0 = 64 * w0; kc1 = 64 * (w0 + 2 * nch)
                    SP.wait(blk_free[s])
                    L = ldb[s]
                    L.add(sync.dma_start(out=kaT[s][:, :, 0:kc1 - kc0], in_=qk_v[:, 4:8, kc0:kc1]))
                    L.add(sync.dma_start(out=qaT[s][:], in_=qk_v[:, 0:4, tok0:tok0 + 128]))
                    L.add(sync.dma_start(out=qbT[s][:], in_=qk_v[:, 8:12, tok0:tok0 + 128]))
                    L.add(sync.dma_start(out=kbT[s][:], in_=qk_scr[12 * 128:13 * 128, tok0 - 128:tok0 + 256]))
                    L.add(sync.dma_start(out=vaw[s][:, 0:nch, :], in_=v_scr[kc0:kc1, 0:512].rearrange("(c p) f -> p c f", p=128)))
                    ev_ld = L.add(sync.dma_start(out=vbw[s][:], in_=v_scr[tok0 - 128:tok0 + 256, 512:768].rearrange("(c p) f -> p c f", p=128)))
                    xs = i % 3
                    SP.wait(xb_free[xs])
                    ev_x = ldxb[xs].add(sync.dma_start(out=xb[xs][:], in_=xsrc[tok0:tok0 + 128, :]))

                    units = [(e, c) for e in range(2) for c in range(nch)]
                    ev_exp = {}

                    def na_S(u):
                        e, c = units[u]
                        j, S, frb = S_rot.get()
                        PE.wait(frb, ev_ld)
                        s0 = 8 - (w0 - 2 * lb + 2 * c)
                        for pair in range(4):
                            tensor.matmul(S[:, pair * 128:(pair + 1) * 128],
                                          lhsT=kaT[s][64 * e:64 * e + 64, pair, c * 128:(c + 1) * 128],
                                          rhs=qaT[s][64 * e:64 * e + 64, pair, :],
                                          start=(pair == 0), stop=False, skip_group_check=True)
                        ins = tensor.matmul(S[:, 0:512], lhsT=idb[:], rhs=ttb[:, e * 4:(e + 1) * 4, s0 * 64:s0 * 64 + 128],
                                            start=False, stop=True, skip_group_check=True)
                        e_s = PE.sig(ins)
                        ACT.wait(e_s, fr["PTn"])
                        Sv = S[:, 0:512].rearrange("p (h a q) -> p h a q", h=4, a=2)
                        for p in range(2):
                            col = ((l * NLB + lb) * 6 + c) * 2 + p
                            ins = scalar.activation(out=PTn[:, c, e * 4:(e + 1) * 4, p * 64:(p + 1) * 64], in_=Sv[:, :, p, :],
                                                    func=AF.Exp, bias=mbna[:, col:col + 1])
                        e_x = ACT.sig(ins)
                        S_rot.rel(j, e_x)
                        ev_exp[u] = e_x

                    def na_PV(u):
                        e, c = units[u]
                        PE.wait(ev_exp[u], fr["PD"])
                        for pair in range(4):
                            bank = P[pair // 2]
                            col0 = (pair % 2) * 256 + e * 128
                            st_flag = (e == 0 and c == 0 and pair % 2 == 0)
                            tensor.matmul(bank[:, col0:col0 + 128], lhsT=vaw[s][:, c, pair * 128:(pair + 1) * 128],
                                          rhs=PTn[:, c, e * 4 + pair, :], start=st_flag, stop=(c == nch - 1), skip_group_check=True)
                        return tensor.matmul(Dk[e][:, 0:512], lhsT=ones[:], rhs=PTn[:, c, e * 4:(e + 1) * 4, :],
                                             start=(c == 0), stop=(c == nch - 1), skip_group_check=True)

                    nu = len(units)
                    LAG = 4
                    for u in range(nu):
                        na_S(u)
                        if u >= LAG:
                            na_PV(u - LAG)
                        if pending3 is not None and u == 1:
                            pending3[0]()
                        if pending3 is not None and u == 7:
                            pending3[1]()
                            pending3 = None
                        if pending is not None and u == 8:
                            pending[0]()
                    for u in range(nu - LAG, nu):
                        ins = na_PV(u)
                    e_pvn = PE.sig(ins)
                    fr["PTn"] = e_pvn
                    if pending is not None:
                        pending[1]()
                    ACT.wait(e_pvn, fr["rawn"])
                    scalar.copy(out=rawPn[:, 0, :], in_=P[0][:, 0:512])
                    scalar.copy(out=rawDn[:, 0, :], in_=Dk[0][:, 0:512])
                    scalar.copy(out=rawPn[:, 1, :], in_=P[1][:, 0:512])
                    e_ca = ACT.sig(scalar.copy(out=rawDn[:, 1, :], in_=Dk[1][:, 0:512]))
                    e_cd = e_ca
                    fr["PD"] = [e_ca]
                    DVE.wait(e_ca, e_cd)
                    e_r = DVE.sig(vector.reciprocal(out=rec_n[:].rearrange("p h q -> p (h q)"), in_=rawDn[:].rearrange("p e q -> p (e q)")))
                    POOL.wait(e_r, e_ca, e_cd, fr["oun"])
                    DVE.wait(e_r, fr["oun"])
                    evs_ = []
                    for e in range(2):
                        EE, eng_ = (DVE, vector) if e == 0 else (POOL, gpsimd)
                        for bk in range(2):
                            src = rawPn[64 * e:64 * e + 64, bk, :].rearrange("p (m t q) -> p m t q", m=2, t=2)[:, :, e, :]
                            ins = eng_.tensor_tensor(out=o_un[64 * e:64 * e + 64, 2 * bk:2 * bk + 2, :], in0=src,
                                                     in1=rec_n[64 * e:64 * e + 64, e * 4 + 2 * bk:e * 4 + 2 * bk + 2, :], op=ALU.mult)
                        evs_.append(EE.sig(ins))
                    e_na = evs_
                    fr["rawn"] = e_na

                    sunits = [(g, c) for g in range(2) for c in range(3)]
                    sw_exp = {}

                    def sw_S(u):
                        g, c = sunits[u]
                        j, S, frb = S_rot.get()
                        PE.wait(frb, ev_ld)
                        tensor.matmul(S[:, 0:512], lhsT=kbT[s][64 * g:64 * g + 64, c * 128:(c + 1) * 128],
                                      rhs=qbT[s][64 * g:64 * g + 64, :, :], start=True, stop=False, skip_group_check=True)
                        ins = tensor.matmul(S[:, 0:512], lhsT=idb[:], rhs=bswb[:, c, g * 4:(g + 1) * 4, :], start=False, stop=True, skip_group_check=True)
                        e_s = PE.sig(ins)
                        ACT.wait(e_s, fr["PTs"])
                        col = lb * 3 + c
                        e_x = ACT.sig(scalar.activation(out=PTs[:, c, g * 4:(g + 1) * 4, :], in_=S[:, 0:512].rearrange("p (h q) -> p h q", h=4),
                                                        func=AF.Exp, bias=mbsw[:, col:col + 1]))
                        S_rot.rel(j, e_x)
                        sw_exp[u] = e_x

                    def sw_PV(u):
                        g, c = sunits[u]
                        PE.wait(sw_exp[u], fr["PD"])
                        tensor.matmul(P[g][:, 0:512], lhsT=vbw[s][:, c, g * 128:(g + 1) * 128], rhs=PTs[:, c, g * 4:(g + 1) * 4, :],
                                      start=(c == 0), stop=(c == 2), skip_group_check=True)
                        ins = tensor.matmul(Dk[g][:, 0:512], lhsT=ones[:], rhs=PTs[:, c, g * 4:(g + 1) * 4, :],
                                            start=(c == 0), stop=False, skip_group_check=True)
                        if c == 2:
                            o0 = (l * 8 + 4 * g) * 128
                            ins = tensor.matmul(Dk[g][:, 0:512], lhsT=ones[0:1, :], rhs=esrow[0:1, o0:o0 + 512],
                                                start=False, stop=True, skip_group_check=True)
                        return ins

                    for u in range(6):
                        sw_S(u)
                        if pending is not None and u == 1:
                            pending[2]()
                    for u in range(6):
                        ins = sw_PV(u)
                    e_pvs = PE.sig(ins)
                    fr["PTs"] = e_pvs
                    DVE.wait(e_pvs, fr["raws"])
                    vector.tensor_copy(out=rawPs[:, 0, :], in_=P[0][:, 0:512])
                    vector.tensor_copy(out=rawDs[:, 0, :], in_=Dk[0][:, 0:512])
                    vector.tensor_copy(out=rawPs[:, 1, :], in_=P[1][:, 0:512])
                    e_cd = DVE.sig(vector.tensor_copy(out=rawDs[:, 1, :], in_=Dk[1][:, 0:512]))
                    e_ca = e_cd
                    pd_sw = [e_cd]
                    DVE.wait(e_ca, e_cd)
                    e_r = DVE.sig(vector.reciprocal(out=rec_s[:].rearrange("p h q -> p (h q)"), in_=rawDs[:].rearrange("p g q -> p (g q)")))
                    POOL.wait(e_r, e_ca, e_cd, e_na)
                    DVE.wait(e_r, e_na)
                    rsv = rec_s[:].rearrange("p (g m t) q -> p g m t q", g=2, m=2)
                    evs_ = []
                    for hf in range(2):
                        EE, eng_ = (DVE, vector) if hf == 0 else (POOL, gpsimd)
                        for g in range(2):
                            src = rawPs[64 * hf:64 * hf + 64, g, :].rearrange("p (m t q) -> p m t q", m=2, t=2)[:, :, hf, :]
                            ins = eng_.tensor_tensor(out=o_un[64 * hf:64 * hf + 64, 4 + 2 * g:6 + 2 * g, :], in0=src,
                                                     in1=rsv[64 * hf:64 * hf + 64, g, :, hf, :], op=ALU.mult)
                        evs_.append(EE.sig(ins))
                    e_sw = evs_
                    fr["raws"] = e_sw
                    fr["PD"] = pd_sw
                    blk_free[s] = e_pvs
                    if pending is not None:
                        pending3 = (pending[3], pending[4])
                    pending = make_out_pieces(i, lb, s, tok0, e_sw, e_pvs, ev_x, xs)
                if pending3 is not None:
                    pending3[0]()
                    pending3[1]()
                for pc in pending:
                    pc()
                phase_end([stb[0].ev(), stb[1].ev()])

        def phase_D(l):
            tiles = cfg.tiles[l]
            phase_begin()
            with ExitStack() as es:
                wu = sb(es, "d_wu", [128, 8, 2 * DFF], BF16)
                wd = sb(es, "d_wd", [128, 22, D], BF16)
                with ExitStack() as es2:
                    NSTG = 4
                    stg = [sb(es2, f"d_stg{i}", [128, 2816], F32) for i in range(NSTG)]
                    stg_free = [None] * NSTG
                    ldw = [ds["w0"], ds["w1"], ds["w2"], ds["w3"]]
                    jobs = [("u", k, q) for k in range(8) for q in range(2)] + [("d", k, 0) for k in range(11)]
                    for n, (kind, k, q) in enumerate(jobs):
                        s = n % NSTG
                        EE = DVE if n % 2 == 0 else ACT
                        SP.wait(stg_free[s])
                        if kind == "u":
                            ev = ldw[s].add(sync.dma_start(out=stg[s][:, 0:2816], in_=w_up_d[l, k * 128:(k + 1) * 128, q * 2816:(q + 1) * 2816]))
                            EE.wait(ev)
                            gk = gc[:, G[f"ffn{l}"] + k:G[f"ffn{l}"] + k + 1]
                            if EE is DVE:
                                ins = vector.tensor_scalar(out=wu[:, k, q * 2816:(q + 1) * 2816], in0=stg[s][:, 0:2816], scalar1=gk, scalar2=None, op0=ALU.mult)
                            else:
                                ins = scalar.activation(out=wu[:, k, q * 2816:(q + 1) * 2816], in_=stg[s][:, 0:2816], func=AF.Identity, scale=gk)
                        else:
                            ev = ldw[s].add(sync.dma_start(out=stg[s][:, 0:2048].rearrange("p (c n) -> p c n", c=2),
                                                           in_=w_down_d[l, k * 256:(k + 1) * 256, :].rearrange("(c p) n -> p c n", p=128)))
                            EE.wait(ev)
                            dst = wd[:, 2 * k:2 * k + 2, :].rearrange("p c n -> p (c n)")
                            if EE is DVE:
                                ins = vector.tensor_copy(out=dst, in_=stg[s][:, 0:2048])
                            else:
                                ins = scalar.copy(out=dst, in_=stg[s][:, 0:2048])
                        stg_free[s] = EE.sig(ins)
                    wready = list(stg_free)
                    for E in ENGS:
                        E.wait(wready)
                hTt = [sb(es, f"d_hT{i}", [128, 8, 386], BF16) for i in range(2)]
                actT = sb(es, "d_act", [128, 22, 384], BF16)
                tg = [sb(es, f"d_tg{i}", [128, 384], F32) for i in range(2)]
                tv = [sb(es, f"d_tv{i}", [128, 384], F32) for i in range(2)]
                xm = [sb(es, f"d_xm{i}", [128, D], F32) for i in range(3)]
                xo = [sb(es, f"d_xo{i}", [128, D], F32) for i in range(2)]
                junk = sb(es, "d_junk", [128, D], BF16)
                st = sb(es, "d_st", [128, 4], F32)
                gfin = sb(es, "d_gfin", [128, D], F32) if l == 1 else None
                U = [ps(es, f"d_U{i}", [128, 512], F32) for i in range(4)]
                U_free = [None] * 4
                Y_rot = Rot([ps(es, f"d_Y{i}", [128, 512], F32) for i in range(4)])
                ev_gf = None
                if l == 1:
                    ds["c"].add(sync.dma_start(out=gfin[:], in_=gfin_d[:, :]))
                    ev_gf = ds["c"].ev()

                ldh = [ds["h0"], ds["h1"]]
                ldm = [ds["m0"], ds["m1"], ds["m2"]]
                sto = [ds["o0"], ds["o1"]]
                hTt_free = [None, None]
                xm_free = [None] * 3
                xo_free = [None, None]
                t_free = [None, None]
                xmc = 0; xoc = 0
                for t, (tb0, nb) in enumerate(tiles):
                    s = t % 2
                    T = 128 * nb; tok0 = tb0 * 128
                    SP.wait(hTt_free[s])
                    ev_h = ldh[s].add(sync.dma_start(out=hTt[s][:, :, 0:T + 2], in_=hmT_v[:, :, tok0:tok0 + T + 2]))
                    ev_fix = None
                    if tb0 in cfg.PB:
                        j = cfg.PB.index(tb0)
                        DVE.wait(ev_h)
                        ev_fix = DVE.sig(vector.tensor_scalar(out=hTt[s][:, :, 0:1], in0=hTt[s][:, :, 0:1],
                                                              scalar1=gc[:, G["keep"] + j:G["keep"] + j + 1], scalar2=None, op0=ALU.mult))
                    if (tb0 + nb) in cfg.PB:
                        j = cfg.PB.index(tb0 + nb)
                        DVE.wait(ev_h)
                        ev_fix = DVE.sig(vector.tensor_scalar(out=hTt[s][:, :, T + 1:T + 2], in0=hTt[s][:, :, T + 1:T + 2],
                                                              scalar1=gc[:, G["keep"] + j:G["keep"] + j + 1], scalar2=None, op0=ALU.mult))
                    xm_ev = []
                    xm_slot = []
                    for sbk in range(nb):
                        xs = xmc % 3; xmc += 1
                        SP.wait(xm_free[xs])
                        xm_ev.append(ldm[xs].add(sync.dma_start(out=xm[xs][:], in_=xmid_scr[tok0 + sbk * 128:tok0 + (sbk + 1) * 128, :])))
                        xm_slot.append(xs)
                    e_a = None
                    e_acts = []
                    for jj in range(22):
                        pr = jj % 2
                        e3 = {}
                        for wh, colbase, bi_, tt_ in (("g", jj * 128, 2 * pr, tg[pr]), ("v", DFF + jj * 128, 2 * pr + 1, tv[pr])):
                            bank = U[bi_]
                            PE.wait(U_free[bi_], ev_h, ev_fix)
                            for k in range(8):
                                ins = tensor.matmul(bank[:, 0:T + 2], lhsT=wu[:, k, colbase:colbase + 128], rhs=hTt[s][:, k, 0:T + 2],
                                                    start=(k == 0), stop=(k == 7))
                            e_u = PE.sig(ins)
                            m = colbase // 128
                            ACT.wait(e_u, t_free[pr])
                            e1 = ACT.sig(scalar.activation(out=tt_[:, 0:T], in_=bank[:, 1:T + 1], func=AF.Identity,
                                                           bias=gc[:, G[f"cb{l}"] + m:G[f"cb{l}"] + m + 1],
                                                           scale=gc[:, G[f"cw1{l}"] + m:G[f"cw1{l}"] + m + 1]))
                            DVE.wait(e1)
                            e2 = DVE.sig(vector.scalar_tensor_tensor(out=tt_[:, 0:T], in0=bank[:, 0:T], scalar=gc[:, G[f"cw0{l}"] + m:G[f"cw0{l}"] + m + 1],
                                                                     in1=tt_[:, 0:T], op0=ALU.mult, op1=ALU.add))
                            DVE.wait(e2)
                            e3[wh] = DVE.sig(vector.scalar_tensor_tensor(out=tt_[:, 0:T], in0=bank[:, 2:T + 2], scalar=gc[:, G[f"cw2{l}"] + m:G[f"cw2{l}"] + m + 1],
                                                                         in1=tt_[:, 0:T], op0=ALU.mult, op1=ALU.add))
                            U_free[bi_] = e3[wh]
                        ACT.wait(e3["g"])
                        e_s = ACT.sig(scalar.activation(out=tg[pr][:, 0:T], in_=tg[pr][:, 0:T], func=AF.Silu))
                        POOL.wait(e_s, e3["v"])
                        e_a = POOL.sig(gpsimd.tensor_tensor(out=actT[:, jj, 0:T], in0=tg[pr][:, 0:T], in1=tv[pr][:, 0:T], op=ALU.mult))
                        t_free[pr] = e_a
                        e_acts.append(e_a)
                    hTt_free[s] = e_u
                    for sbk in range(nb):
                        os_ = xoc % 2; xoc += 1
                        xs = xm_slot[sbk]
                        tok = tok0 + sbk * 128
                        for hf in range(2):
                            j, Y, fr = Y_rot.get()
                            PE.wait(fr)
                            for jj in range(22):
                                PE.wait(e_acts[jj])
                                ins = tensor.matmul(Y[:, 0:512], lhsT=actT[:, jj, sbk * 128:(sbk + 1) * 128], rhs=wd[:, jj, hf * 512:(hf + 1) * 512],
                                                    start=(jj == 0), stop=(jj == 21))
                            e_y = PE.sig(ins)
                            DVE.wait(e_y, xm_ev[sbk], xo_free[os_])
                            e_o = DVE.sig(vector.tensor_tensor(out=xo[os_][:, hf * 512:(hf + 1) * 512], in0=Y[:, 0:512],
                                                               in1=xm[xs][:, hf * 512:(hf + 1) * 512], op=ALU.add))
                            Y_rot.rel(j, e_o)
                        xm_free[xs] = e_o
                        if l == 0:
                            POOL.wait(e_o)
                            sto[os_].add(gpsimd.dma_start(out=x1_scr[tok:tok + 128, :], in_=xo[os_][:]))
                        else:
                            ACT.wait(e_o)
                            e = ACT.sig(scalar.activation(out=junk[:], in_=xo[os_][:], func=AF.Square, accum_out=st[:, 0:1]))
                            ACT.wait(e)
                            e = ACT.sig(scalar.activation(out=st[:, 1:2], in_=st[:, 0:1], func=AF.Sqrt, bias=EPS, scale=1.0 / D))
                            DVE.wait(e)
                            e = DVE.sig(vector.reciprocal(out=st[:, 2:3], in_=st[:, 1:2]))
                            DVE.wait(e)
                            e = DVE.sig(vector.tensor_scalar(out=xo[os_][:], in0=xo[os_][:], scalar1=st[:, 2:3], scalar2=None, op0=ALU.mult))
                            POOL.wait(e, ev_gf)
                            e = POOL.sig(gpsimd.tensor_tensor(out=xo[os_][:], in0=xo[os_][:], in1=gfin[:], op=ALU.mult))
                            POOL.wait(e)
                            yt = tok - HALO * 128
                            sto[os_].add(gpsimd.dma_start(out=y_d[yt:yt + 128, :], in_=xo[os_][:]))
                        xo_free[os_] = sto[os_].ev()
                phase_end([sto[0].ev(), sto[1].ev()])

        phase_A(0, xin)
        phase_BC(0, xin)
        phase_D(0)
        phase_A(1, x1_scr)
        phase_BC(1, x1_scr)
        phase_D(1)
        for E in ENGS:
            E.wait(state["phase_ev"])
    return nc


def run_cfg(cfg, xcat, params):
    nc = build_program(cfg)
    maps = host_inputs(cfg, xcat, params)
    res = run_bass_kernel_spmd(nc, maps, core_ids=list(range(cfg.NC)))
    return np.concatenate([np.asarray(r["y"]) for r in res.results], axis=0)


def kernel(x_prompt, x_sample, norm_mix, w_in, rpb, sinks, norm_grp, w_out, norm_ffn, w_up, conv_w, conv_b, w_down, norm_final):
    xp = np.asarray(x_prompt, np.float32)
    xs = np.asarray(x_sample, np.float32)
    T = xp.shape[1]
    xcat = np.concatenate([xp.reshape(-1, D), xs.reshape(-1, D)], axis=0)
    nseq = xp.shape[0] + xs.shape[0]
    cfg = Cfg(8, nseq, T // 128)
    params = dict(norm_mix=np.asarray(norm_mix), w_in=np.asarray(w_in), rpb=np.asarray(rpb), sinks=np.asarray(sinks),
                  norm_grp=np.asarray(norm_grp), w_out=np.asarray(w_out), norm_ffn=np.asarray(norm_ffn), w_up=np.asarray(w_up),
                  conv_w=np.asarray(conv_w), conv_b=np.asarray(conv_b), w_down=np.asarray(w_down), norm_final=np.asarray(norm_final))
    y = run_cfg(cfg, xcat, params)
    y = y.reshape(nseq, T, D)
    return (np.ascontiguousarray(y[:xp.shape[0]]), np.ascontiguousarray(y[xp.shape[0]:]))
```

```python
import numpy as np
from contextlib import ExitStack
import concourse.bass as bass
import concourse.mybir as mybir
from concourse.bass_utils import run_bass_kernel_spmd

F32 = mybir.dt.float32
BF16 = mybir.dt.bfloat16
AF = mybir.ActivationFunctionType
ALU = mybir.AluOpType
D = 1024
DFF = 2816
EPS = 1e-6
NEG = -30000.0
HALO = 6


class Cfg:
    def __init__(self, ncores, nseq, seq_blocks):
        self.NC, self.NSEQ, self.SB = ncores, nseq, seq_blocks
        self.TOTB = nseq * seq_blocks
        assert self.TOTB % ncores == 0
        self.BPC = self.TOTB // ncores
        self.NLB = self.BPC + 2 * HALO
        self.NTOK = self.NLB * 128
        pb = set()
        for c in range(ncores):
            for s in range(nseq + 1):
                o = s * seq_blocks - c * self.BPC
                if 0 <= o <= self.BPC:
                    pb.add(o + HALO)
        self.PB = sorted(pb)
        N = self.NLB
        self.rA = [(0, N), (3, N - 3)]
        self.rBC = [(2, N - 2), (5, N - 5)]
        self.rD = [(3, N - 3), (6, N - 6)]
        self.win = []
        for l in range(2):
            a0, a1 = self.rA[l]
            w = {}
            for lb in range(*self.rBC[l]):
                r0 = 2 * lb
                cand = (r0 - 4, 5)
                if (lb + 1) in self.PB:
                    cand = (r0 - 6, 6)
                elif lb in self.PB:
                    cand = (r0 - 4, 6)
                if cand[0] < 2 * a0 or cand[0] + 2 * cand[1] > 2 * a1:
                    cand = (r0 - 4, 5)
                assert cand[0] >= 2 * a0 and cand[0] + 2 * cand[1] <= 2 * a1
                w[lb] = cand
            self.win.append(w)
        self.tiles = []
        for l in range(2):
            d0, d1 = self.rD[l]
            cuts = sorted({d0, d1} | {p for p in self.PB if d0 < p < d1})
            tl = []
            for s0, s1 in zip(cuts[:-1], cuts[1:]):
                n = s1 - s0
                sizes = []
                while n > 0:
                    if n % 3 == 0 or n >= 5:
                        sizes.append(3); n -= 3
                    elif n >= 2:
                        sizes.append(2); n -= 2
                    else:
                        sizes.append(1); n -= 1
                b = s0
                for sz in sizes:
                    tl.append((b, sz)); b += sz
            self.tiles.append(tl)
        self.goff = {}
        o = 0
        for l in range(2):
            for nm, w in (("mix", 8), ("grp", 8), ("ffn", 8), ("cw0", 44), ("cw1", 44), ("cw2", 44), ("cb", 44)):
                self.goff[f"{nm}{l}"] = o; o += w
        self.goff["sink"] = o; o += 16
        self.goff["keep"] = o; o += len(self.PB)
        self.NG = o


def _vcol(v, n):
    return np.ascontiguousarray(np.asarray(v, np.float32).reshape(n, 128).T)


def build_tt(rpb_l):
    rpb_l = np.asarray(rpb_l, np.float32)
    tt = np.full((2, 64, 8, 16, 64), NEG, np.float32)
    qc = np.arange(64)
    cs = np.clip(qc - 8, 0, 48)
    kc = np.arange(64)
    valid = (kc[None, :] >= cs[:, None]) & (kc[None, :] < cs[:, None] + 16)
    dcc = np.clip(kc[None, :] - qc[:, None] + 15, 0, 30)
    for a in range(2):
        for s in range(16):
            off = 8 - s + a
            if abs(off) > 7:
                continue
            for hidx in range(8):
                e, pair = hidx // 4, hidx % 4
                h = 2 * pair + e
                slab = np.where(valid, rpb_l[h, off + 7][dcc], np.float32(NEG))
                tt[a, :, hidx, s, :] = slab.T
    return tt.reshape(128, 8, 1024)


def build_bsw():
    s = np.arange(128)[:, None, None, None]
    c = np.arange(3)[None, :, None, None]
    h = np.arange(8)[None, None, :, None]
    t = np.arange(128)[None, None, None, :]
    dist = np.abs(t - (s + 128 * (c - 1)))
    slope = np.exp2(-(h + 1.0))
    val = -(slope * dist)
    return np.where(dist <= 128, val, NEG).astype(np.float32)


def na_masks(cfg, core):
    NLB = cfg.NLB
    m = np.full((2, NLB, 6, 2, 2), NEG, np.float32)
    grow0 = 2 * (core * cfg.BPC - HALO)
    RPS = 2 * cfg.SB
    for l in range(2):
        for lb, (w0, nch) in cfg.win[l].items():
            for p in range(2):
                gq = grow0 + 2 * lb + p
                real = 0 <= gq < 2 * cfg.TOTB
                oks = np.zeros((6, 2), bool)
                for c in range(nch):
                    for a in range(2):
                        gk = grow0 + w0 + 2 * c + a
                        if real:
                            sq = gq // RPS
                            rs = int(np.clip(gq % RPS - 4, 0, RPS - 8)) + sq * RPS
                            oks[c, a] = rs <= gk < rs + 8
                        else:
                            oks[c, a] = -4 <= gk - gq <= 3
                if not oks.any():
                    for c in range(nch):
                        for a in range(2):
                            gk = grow0 + w0 + 2 * c + a
                            oks[c, a] = -4 <= gk - gq <= 3
                m[l, lb, :, p, :] = np.where(oks, 0.0, NEG)
    mm = np.transpose(m, (4, 0, 1, 2, 3)).reshape(2, -1)
    return np.ascontiguousarray(np.repeat(mm, 64, axis=0)).astype(np.float32)


def sw_masks(cfg, core):
    NLB = cfg.NLB
    m = np.zeros((NLB, 3), np.float32)
    for lb in range(NLB):
        gb = core * cfg.BPC - HALO + lb
        real = 0 <= gb < cfg.TOTB
        for c in range(3):
            gk = gb + c - 1
            if real:
                ok = (0 <= gk < cfg.TOTB) and (gk // cfg.SB == gb // cfg.SB)
            else:
                ok = True
            m[lb, c] = 0.0 if ok else NEG
    return np.ascontiguousarray(np.broadcast_to(m.reshape(1, -1), (128, NLB * 3))).astype(np.float32)


def host_inputs(cfg, xcat, p):
    shared = {
        "w_in": np.ascontiguousarray(p["w_in"], np.float32),
        "w_out": np.ascontiguousarray(p["w_out"], np.float32),
        "w_up": np.ascontiguousarray(p["w_up"], np.float32),
        "w_down": np.ascontiguousarray(p["w_down"], np.float32),
        "tt": np.stack([build_tt(p["rpb"][l]) for l in range(2)]),
        "bsw": build_bsw(),
        "ident": np.eye(128, dtype=np.float32),
        "gfin": np.ascontiguousarray(np.broadcast_to(np.asarray(p["norm_final"], np.float32)[None, :], (128, D))),
    }
    maps = []
    for c in range(cfg.NC):
        g = np.zeros((128, cfg.NG), np.float32)
        for l in range(2):
            g[:, cfg.goff[f"mix{l}"]:][:, :8] = _vcol(p["norm_mix"][l], 8)
            g[:, cfg.goff[f"grp{l}"]:][:, :8] = _vcol(p["norm_grp"][l], 8)
            g[:, cfg.goff[f"ffn{l}"]:][:, :8] = _vcol(p["norm_ffn"][l], 8)
            for j in range(3):
                g[:, cfg.goff[f"cw{j}{l}"]:][:, :44] = _vcol(p["conv_w"][l][j], 44)
            g[:, cfg.goff[f"cb{l}"]:][:, :44] = _vcol(p["conv_b"][l], 44)
        g[:, cfg.goff["sink"]:][:, :16] = np.asarray(p["sinks"], np.float32).reshape(1, 16)
        for j, pb in enumerate(cfg.PB):
            gb = c * cfg.BPC - HALO + pb
            realb = (0 <= gb <= cfg.TOTB) and (gb % cfg.SB == 0)
            g[:, cfg.goff["keep"] + j] = 0.0 if realb else 1.0
        xin = np.zeros((cfg.NTOK, D), np.float32)
        t0 = (c * cfg.BPC - HALO) * 128
        lo, hi = max(t0, 0), min(t0 + cfg.NTOK, xcat.shape[0])
        xin[lo - t0:hi - t0] = xcat[lo:hi]
        d = dict(shared)
        d.update({"xin": xin, "gcols": g, "mbna": na_masks(cfg, c), "mbsw": sw_masks(cfg, c)})
        maps.append(d)
    return maps


class Eng:
    def __init__(self, nc, eng, name):
        self.e = eng
        self.sem = nc.alloc_semaphore("pg_" + name)
        self.n = 0
        self.seen = {}

    def wait(self, *evs):
        for ev in evs:
            if ev is None:
                continue
            if isinstance(ev, list):
                self.wait(*ev)
                continue
            sem, v = ev
            k = id(sem)
            if self.seen.get(k, 0) >= v:
                continue
            self.seen[k] = v
            self.e.wait_ge(sem, v)

    def sig(self, ins):
        self.n += 1
        ins.then_inc(self.sem, 1)
        return (self.sem, self.n)


class DSem:
    def __init__(self, nc, name):
        self.sem = nc.alloc_semaphore("dm_" + name)
        self.n = 0

    def add(self, ins):
        ins.then_inc(self.sem, 16)
        self.n += 16
        return (self.sem, self.n)

    def ev(self):
        return (self.sem, self.n) if self.n else None


class Rot:
    def __init__(self, banks):
        self.b = banks
        self.free = [None] * len(banks)
        self.i = 0

    def get(self):
        j = self.i % len(self.b)
        self.i += 1
        return j, self.b[j], self.free[j]

    def rel(self, j, ev):
        self.free[j] = ev


def build_program(cfg):
    nc = bass.Bass("TRN2", target_bir_lowering=False)
    NLB, NTOK, BPC = cfg.NLB, cfg.NTOK, cfg.BPC
    G = cfg.goff

    def din(name, shape):
        return nc.dram_tensor(name, list(shape), F32, kind="ExternalInput").ap()

    xin = din("xin", [NTOK, D])
    gcols_d = din("gcols", [128, cfg.NG])
    gfin_d = din("gfin", [128, D])
    tt_d = din("tt", [2, 128, 8, 1024])
    bsw_d = din("bsw", [128, 3, 8, 128])
    mbna_d = din("mbna", [128, 2 * NLB * 12])
    mbsw_d = din("mbsw", [128, NLB * 3])
    ident_d = din("ident", [128, 128])
    w_in_d = din("w_in", [2, D, 2304])
    w_out_d = din("w_out", [2, D, D])
    w_up_d = din("w_up", [2, D, 2 * DFF])
    w_down_d = din("w_down", [2, DFF, D])
    y_d = nc.dram_tensor("y", [BPC * 128, D], F32, kind="ExternalOutput").ap()

    qk_scr = nc.dram_tensor("qk_scr", [13 * 128, NTOK], BF16).ap()
    v_scr = nc.dram_tensor("v_scr", [NTOK, 768], BF16).ap()
    xmid_scr = nc.dram_tensor("xmid_scr", [NTOK, D], F32).ap()
    hmT_scr = nc.dram_tensor("hmT_scr", [D, NTOK + 2], BF16).ap()
    x1_scr = nc.dram_tensor("x1_scr", [NTOK, D], F32).ap()

    PE = Eng(nc, nc.tensor, "pe")
    ACT = Eng(nc, nc.scalar, "act")
    DVE = Eng(nc, nc.vector, "dve")
    POOL = Eng(nc, nc.gpsimd, "pool")
    SP = Eng(nc, nc.sync, "sp")
    ENGS = [PE, ACT, DVE, POOL, SP]
    sync, scalar, vector, gpsimd, tensor = nc.sync, nc.scalar, nc.vector, nc.gpsimd, nc.tensor

    ds = {n: DSem(nc, n) for n in ["c", "w0", "w1", "w2", "w3", "x0", "x1", "q0", "q1", "b0", "b1", "s0", "s1",
                                   "h0", "h1", "m0", "m1", "m2", "o0", "o1"]}
    qk_v = qk_scr.rearrange("(c p) t -> p c t", p=128)
    hmT_v = hmT_scr.rearrange("(k p) t -> p k t", p=128)

    with ExitStack() as top:
        uid = [0]

        def sb(es, name, shape, dty):
            uid[0] += 1
            return es.enter_context(nc.sbuf_tensor(f"{name}_u{uid[0]}", list(shape), dty))

        def ps(es, name, shape, dty):
            uid[0] += 1
            return es.enter_context(nc.psum_tensor(f"{name}_u{uid[0]}", list(shape), dty))

        gc = sb(top, "gc", [128, cfg.NG], F32)
        mbna = sb(top, "mbna_sb", [128, 2 * NLB * 12], F32)
        mbsw = sb(top, "mbsw_sb", [128, NLB * 3], F32)
        idf = sb(top, "idf", [128, 128], F32)
        idb = sb(top, "idb", [128, 128], BF16)
        ones = sb(top, "ones", [128, 128], BF16)
        esink = sb(top, "esink", [128, 16], F32)
        dummy = sb(top, "k_dummy", [128, 2], F32)

        sync.dma_start(out=gc[:], in_=gcols_d[:, :]).then_inc(ds["c"].sem, 16)
        sync.dma_start(out=mbna[:], in_=mbna_d[:, :]).then_inc(ds["c"].sem, 16)
        sync.dma_start(out=mbsw[:], in_=mbsw_d[:, :]).then_inc(ds["c"].sem, 16)
        sync.dma_start(out=idf[:], in_=ident_d[:, :]).then_inc(ds["c"].sem, 16)
        ds["c"].n = 64
        ev_c = ds["c"].ev()
        DVE.wait(ev_c)
        vector.tensor_copy(out=idb[:], in_=idf[:])
        ev_const = DVE.sig(vector.memset(ones[:], 1.0))
        ACT.wait(ev_c)
        ev_sink = ACT.sig(scalar.activation(out=esink[:], in_=gc[:, G["sink"]:G["sink"] + 16], func=AF.Exp))
        esrow = sb(top, "esrow", [1, 2048], BF16)
        epsc = sb(top, "epsc", [128, 2], F32)
        vector.memset(epsc[:], EPS)
        DVE.wait(ev_sink, ev_const)
        for hh in range(16):
            ins = vector.tensor_scalar(out=esrow[0:1, hh * 128:(hh + 1) * 128], in0=ones[0:1, :], scalar1=esink[0:1, hh:hh + 1], scalar2=None, op0=ALU.mult)
        ev_esrow = DVE.sig(ins)
        for E in ENGS:
            E.wait(ev_c, ev_const, ev_sink, ev_esrow)

        state = {"phase_ev": None}

        def phase_begin():
            for E in ENGS:
                E.wait(state["phase_ev"])

        def phase_end(store_evs):
            POOL.wait(*store_evs)
            state["phase_ev"] = POOL.sig(gpsimd.memset(dummy[:], 0.0))

        def phase_A(l, xsrc):
            a0, a1 = cfg.rA[l]
            groups = [list(range(b, min(b + 4, a1))) for b in range(a0, a1, 4)]
            phase_begin()
            with ExitStack() as es:
                wres = sb(es, "a_wres", [128, 8, 2432], BF16)
                stg = [sb(es, f"a_stg{i}", [128, 2304], F32) for i in range(2)]
                xg = [sb(es, f"a_xg{i}", [128, 4, D], F32) for i in range(2)]
                hb = [sb(es, f"a_hb{i}", [128, D], BF16) for i in range(2)]
                hT = [sb(es, f"a_hT{i}", [128, 8, 512], BF16) for i in range(2)]
                qko = [sb(es, f"a_qko{i}", [128, 13, 512], BF16) for i in range(2)]
                vo = [sb(es, f"a_vo{i}", [128, 4, 768], BF16) for i in range(2)]
                junk = sb(es, "a_junk", [128, D], BF16)
                st = sb(es, "a_st", [128, 12], F32)
                tp = ps(es, "a_tp", [128, 8, 128], BF16)
                rot = Rot([ps(es, f"a_pm{i}", [128, 512], F32) for i in range(6)])

                stg_free = [None, None]
                ldw = [ds["w0"], ds["w1"]]
                for k in range(8):
                    s = k % 2
                    SP.wait(stg_free[s])
                    ev = ldw[s].add(sync.dma_start(out=stg[s][:], in_=w_in_d[l, k * 128:(k + 1) * 128, :]))
                    DVE.wait(ev)
                    gk = gc[:, G[f"mix{l}"] + k:G[f"mix{l}"] + k + 1]
                    src, dst = stg[s], wres
                    vector.tensor_scalar(out=dst[:, k, 0:512], in0=src[:, 0:512], scalar1=gk, scalar2=0.125, op0=ALU.mult, op1=ALU.mult)
                    vector.tensor_scalar(out=dst[:, k, 512:1024], in0=src[:, 512:1024], scalar1=gk, scalar2=None, op0=ALU.mult)
                    vector.tensor_scalar(out=dst[:, k, 1024:1536].rearrange("p (j a c) -> p j a c", j=4, a=2),
                                         in0=src[:, 1536:2048].rearrange("p (a j c) -> p j a c", a=2, j=4),
                                         scalar1=gk, scalar2=0.125, op0=ALU.mult, op1=ALU.mult)
                    vector.tensor_scalar(out=dst[:, k, 1536:1664], in0=src[:, 2048:2176], scalar1=gk, scalar2=None, op0=ALU.mult)
                    vector.tensor_scalar(out=dst[:, k, 1664:2176], in0=src[:, 1024:1536], scalar1=gk, scalar2=None, op0=ALU.mult)
                    dv = dst[:, k, 2176:2432].rearrange("p (g d c) -> p g d c", g=2, d=2)
                    sv = src[:, 2176:2304].rearrange("p (g c) -> p g c", g=2)
                    vector.tensor_scalar(out=dv[:, :, 0, :], in0=sv, scalar1=gk, scalar2=None, op0=ALU.mult)
                    ins = vector.tensor_scalar(out=dv[:, :, 1, :], in0=sv, scalar1=gk, scalar2=None, op0=ALU.mult)
                    stg_free[s] = DVE.sig(ins)
                wready = stg_free[1]
                PE.wait(wready)

                ldx = [ds["x0"], ds["x1"]]
                stq = [ds["q0"], ds["q1"]]
                xg_free = [None, None]; hb_free = [None, None]; hT_free = [None, None]
                out_free = [None, None]
                stat_free = [None] * 4
                store_evs = []
                A = {"bcount": 0, "tp_free": None}
                def stage1_begin(gi, blks):
                    s = gi % 2
                    nb = len(blks); T = 128 * nb; tok0 = blks[0] * 128
                    SP.wait(xg_free[s])
                    ev_x = ldx[s].add(sync.dma_start(out=xg[s][:, 0:nb, :],
                                                     in_=xsrc[tok0:tok0 + T, :].rearrange("(b p) d -> p b d", p=128)))
                    return {"s": s, "nb": nb, "T": T, "tok0": tok0, "evs_hT": [], "ev_x": ev_x, "done": 0}

                def stage1_block(cx, part="both"):
                    if cx is None or cx["done"] >= cx["nb"]:
                        return
                    if part in ("norm", "both") and cx.get("pend") is None:
                        stage1_norm(cx)
                    if part in ("tr", "both") and cx.get("pend") is not None:
                        stage1_tr(cx)

                def stage1_norm(cx):
                    bi = cx["done"]
                    s, nb, ev_x = cx["s"], cx["nb"], cx["ev_x"]
                    if True:
                        q = A["bcount"] % 4; sc = q * 3; hs = A["bcount"] % 2; A["bcount"] += 1
                        ACT.wait(ev_x, stat_free[q])
                        e = ACT.sig(scalar.activation(out=junk[:], in_=xg[s][:, bi, :], func=AF.Square, accum_out=st[:, sc:sc + 1]))
                        ACT.wait(e)
                        e = ACT.sig(scalar.activation(out=st[:, sc + 1:sc + 2], in_=st[:, sc:sc + 1], func=AF.Sqrt, bias=EPS, scale=1.0 / D))
                        DVE.wait(e)
                        e = DVE.sig(vector.reciprocal(out=st[:, sc + 2:sc + 3], in_=st[:, sc + 1:sc + 2]))
                        DVE.wait(e, hb_free[hs], ev_x)
                        e_h = DVE.sig(vector.tensor_scalar(out=hb[hs][:], in0=xg[s][:, bi, :], scalar1=st[:, sc + 2:sc + 3], scalar2=None, op0=ALU.mult))
                        stat_free[q] = e_h
                        if bi == nb - 1:
                            xg_free[s] = e_h
                        cx["pend"] = (bi, hs, e_h)

                def stage1_tr(cx):
                    bi, hs, e_h = cx["pend"]
                    cx["pend"] = None
                    cx["done"] += 1
                    s, evs_hT = cx["s"], cx["evs_hT"]
                    if True:
                        PE.wait(e_h, A["tp_free"])
                        for k in range(8):
                            ins = tensor.transpose(out=tp[:, k, :], in_=hb[hs][:, k * 128:(k + 1) * 128], identity=idb[:])
                        e_t = PE.sig(ins)
                        hb_free[hs] = e_t
                        ACT.wait(e_t, hT_free[s])
                        e_c = ACT.sig(scalar.copy(out=hT[s][:, :, bi * 128:(bi + 1) * 128], in_=tp[:]))
                        A["tp_free"] = e_c
                        evs_hT.append(e_c)

                def stage2(gi, blks, ctx, nxt):
                    s, nb, T, tok0, evs_hT = ctx["s"], ctx["nb"], ctx["T"], ctx["tok0"], ctx["evs_hT"]
                    evac = []
                    n_ev = 0
                    for cc in range(13):
                        if cc in (0, 3, 6, 9):
                            stage1_block(nxt, "norm")
                        elif cc in (2, 5, 8, 11):
                            stage1_block(nxt, "tr")
                        j, bank, fr = rot.get()
                        PE.wait(fr, *evs_hT)
                        for k in range(8):
                            ins = tensor.matmul(bank[:, 0:T], lhsT=wres[:, k, cc * 128:(cc + 1) * 128], rhs=hT[s][:, k, 0:T],
                                                start=(k == 0), stop=(k == 7))
                        e_m = PE.sig(ins)
                        E = ACT if n_ev % 2 == 0 else DVE
                        n_ev += 1
                        E.wait(e_m, out_free[s])
                        if E is ACT:
                            e_e = E.sig(scalar.copy(out=qko[s][:, cc, 0:T], in_=bank[:, 0:T]))
                        else:
                            e_e = E.sig(vector.tensor_copy(out=qko[s][:, cc, 0:T], in_=bank[:, 0:T]))
                        rot.rel(j, e_e)
                        evac.append(e_e)
                    for bi in range(nb):
                        for (c0, c1, w) in ((1664, 2176, 512), (2176, 2432, 256)):
                            j, bank, fr = rot.get()
                            PE.wait(fr)
                            for k in range(8):
                                ins = tensor.matmul(bank[:, 0:w], lhsT=hT[s][:, k, bi * 128:(bi + 1) * 128], rhs=wres[:, k, c0:c1],
                                                    start=(k == 0), stop=(k == 7))
                            e_m = PE.sig(ins)
                            E = ACT if n_ev % 2 == 0 else DVE
                            n_ev += 1
                            E.wait(e_m, out_free[s])
                            o0 = 0 if w == 512 else 512
                            if E is ACT:
                                e_e = E.sig(scalar.copy(out=vo[s][:, bi, o0:o0 + w], in_=bank[:, 0:w]))
                            else:
                                e_e = E.sig(vector.tensor_copy(out=vo[s][:, bi, o0:o0 + w], in_=bank[:, 0:w]))
                            rot.rel(j, e_e)
                            evac.append(e_e)
                    hT_free[s] = e_m
                    POOL.wait(*evac)
                    stq[s].add(gpsimd.dma_start(out=qk_v[:, :, tok0:tok0 + T], in_=qko[s][:, :, 0:T]))
                    stq[s].add(gpsimd.dma_start(out=v_scr[tok0:tok0 + T, :].rearrange("(b p) f -> p b f", p=128), in_=vo[s][:, 0:nb, :]))
                    out_free[s] = stq[s].ev()

                ctx_next = stage1_begin(0, groups[0])
                while ctx_next["done"] < ctx_next["nb"]:
                    stage1_block(ctx_next)
                for gi, blks in enumerate(groups):
                    ctx = ctx_next
                    ctx_next = stage1_begin(gi + 1, groups[gi + 1]) if gi + 1 < len(groups) else None
                    stage2(gi, blks, ctx, ctx_next)
                    while ctx_next is not None and ctx_next["done"] < ctx_next["nb"]:
                        stage1_block(ctx_next)
                phase_end([stq[0].ev(), stq[1].ev()])

        def phase_BC(l, xsrc):
            b0, b1 = cfg.rBC[l]
            phase_begin()
            with ExitStack() as es:
                wo = sb(es, "b_wo", [128, 8, D], BF16)
                ttb = sb(es, "b_tt", [128, 8, 1024], BF16)
                bswb = sb(es, "b_bsw", [128, 3, 8, 128], BF16)
                stg = [sb(es, f"b_stg{i}", [128, 1024], F32) for i in range(2)]
                kaT = [sb(es, f"b_kaT{i}", [128, 4, 768], BF16) for i in range(2)]
                qaT = [sb(es, f"b_qaT{i}", [128, 4, 128], BF16) for i in range(2)]
                qbT = [sb(es, f"b_qbT{i}", [128, 4, 128], BF16) for i in range(2)]
                kbT = [sb(es, f"b_kbT{i}", [128, 384], BF16) for i in range(2)]
                vaw = [sb(es, f"b_vaw{i}", [128, 6, 512], BF16) for i in range(2)]
                vbw = [sb(es, f"b_vbw{i}", [128, 3, 256], BF16) for i in range(2)]
                xb = [sb(es, f"b_xb{i}", [128, D], F32) for i in range(3)]
                xb_free = [None, None, None]
                ldxb = [ds["m0"], ds["m1"], ds["m2"]]
                PTn = sb(es, "b_PTn", [128, 6, 8, 128], BF16)
                PTs = sb(es, "b_PTs", [128, 3, 8, 128], BF16)
                rec_n = sb(es, "b_recn", [128, 8, 128], F32)
                rawPn = sb(es, "b_rawPn", [128, 2, 512], F32)
                rawDn = sb(es, "b_rawDn", [128, 2, 512], F32)
                rawPs = sb(es, "b_rawPs", [128, 2, 512], F32)
                rawDs = sb(es, "b_rawDs", [128, 2, 512], F32)
                rec_s = sb(es, "b_recs", [128, 8, 128], F32)
                o_un = sb(es, "b_oun", [128, 8, 128], F32)
                sqb = sb(es, "b_sqb", [128, 8, 128], BF16)
                ab_s = sb(es, "b_abs", [128, 256], F32)
                ab = sb(es, "b_ab", [128, 256], F32)
                o_bf = sb(es, "b_obf", [128, 8, 128], BF16)
                xmid = [sb(es, f"b_xmid{i}", [128, D], F32) for i in range(2)]
                junk = sb(es, "b_junk", [128, D], BF16)
                hm = sb(es, "b_hm", [128, D], BF16)
                hmT = [sb(es, f"b_hmT{i}", [128, 8, 128], BF16) for i in range(2)]
                st = sb(es, "b_st", [128, 4], F32)
                S_rot = Rot([ps(es, f"b_S{i}", [128, 512], F32) for i in range(3)])
                X = ps(es, "b_X", [128, 512], F32)
                tpv = X[:, 0:512].bitcast(BF16).rearrange("p (k q) -> p k q", k=8)
                P = [ps(es, f"b_P{i}", [128, 512], F32) for i in range(2)]
                Dk = [ps(es, f"b_D{i}", [128, 512], F32) for i in range(2)]

                stg_free = [None, None]
                ldw = [ds["w0"], ds["w1"]]
                jobs = []
                for k in range(8):
                    jobs.append(("wo", k))
                for h in range(8):
                    jobs.append(("tt", h))
                for c in range(3):
                    jobs.append(("bsw", c))
                for n, (kind, k) in enumerate(jobs):
                    s = n % 2
                    SP.wait(stg_free[s])
                    if kind == "wo":
                        src_ap = w_out_d[l, k * 128:(k + 1) * 128, :]
                    elif kind == "tt":
                        src_ap = tt_d[l, :, k, :]
                    else:
                        src_ap = bsw_d[:, k, :, :].rearrange("p h q -> p (h q)")
                    ev = ldw[s].add(sync.dma_start(out=stg[s][:], in_=src_ap))
                    DVE.wait(ev)
                    if kind == "wo":
                        gk = gc[:, G[f"grp{l}"] + k:G[f"grp{l}"] + k + 1]
                        ins = vector.tensor_scalar(out=wo[:, k, :], in0=stg[s][:], scalar1=gk, scalar2=None, op0=ALU.mult)
                    elif kind == "tt":
                        ins = vector.tensor_copy(out=ttb[:, k, :], in_=stg[s][:])
                    else:
                        ins = vector.tensor_copy(out=bswb[:, k, :, :].rearrange("p h q -> p (h q)"), in_=stg[s][:])
                    stg_free[s] = DVE.sig(ins)
                wready = [stg_free[0], stg_free[1]]
                PE.wait(wready)

                ldb = [ds["b0"], ds["b1"]]
                stb = [ds["s0"], ds["s1"]]
                blk_free = [None, None]
                out_free = [None, None]
                fr = {"PTn": None, "PTs": None, "PD": None, "oun": None, "sqb": None, "obf": None, "hm": None, "tp": None, "X": None, "rawn": None, "raws": None}

                def make_out_pieces(i, lb, s, tok0, e_sw, e_pvs, ev_x, xs):
                    c = {}

                    def piece0():
                        ACT.wait(e_sw, fr["sqb"])
                        c["e_sq"] = ACT.sig(scalar.activation(out=sqb[:].rearrange("p k q -> p (k q)"), in_=o_un[:].rearrange("p k q -> p (k q)"), func=AF.Square))
                        PE.wait(c["e_sq"], fr["X"])
                        for grp in range(2):
                            for kk in range(4):
                                ins = tensor.matmul(X[:, grp * 128:(grp + 1) * 128], lhsT=ones[:], rhs=sqb[:, grp * 4 + kk, :],
                                                    start=(kk == 0), stop=(kk == 3), skip_group_check=True)
                        e_ss = PE.sig(ins)
                        fr["sqb"] = e_ss
                        ACT.wait(e_ss)
                        c["e_q"] = ACT.sig(scalar.activation(out=ab_s[:], in_=X[:, 0:256], func=AF.Ln, bias=epsc[:, 0:1], scale=1.0 / 512))
                        ACT.wait(c["e_q"])
                        e_ab = ACT.sig(scalar.activation(out=ab[:], in_=ab_s[:], func=AF.Exp, scale=-0.5))
                        POOL.wait(e_ab, e_sw, fr["obf"])
                        DVE.wait(e_ab, e_sw, fr["obf"])
                        evs_ = []
                        for k in range(8):
                            EE, eng_ = (DVE, vector) if k < 4 else (POOL, gpsimd)
                            ins = eng_.tensor_tensor(out=o_bf[:, k, :], in0=o_un[:, k, :], in1=ab[:, (k // 4) * 128:(k // 4 + 1) * 128], op=ALU.mult)
                            if k % 4 == 3:
                                evs_.append(EE.sig(ins))
                        c["e_ob"] = evs_
                        fr["oun"] = [c["e_sq"]] + evs_

                    def piece1(hf):
                        PE.wait(c["e_ob"], c["e_q"], c.get("e_x0"))
                        for k in range(8):
                            ins = tensor.matmul(X[:, 0:512], lhsT=o_bf[:, k, :], rhs=wo[:, k, hf * 512:(hf + 1) * 512],
                                                start=(k == 0), stop=(k == 7), skip_group_check=True)
                        e_op = PE.sig(ins)
                        DVE.wait(e_op, out_free[s], ev_x)
                        e_add = DVE.sig(vector.tensor_tensor(out=xmid[s][:, hf * 512:(hf + 1) * 512], in0=X[:, 0:512],
                                                             in1=xb[xs][:, hf * 512:(hf + 1) * 512], op=ALU.add))
                        if hf == 0:
                            c["e_x0"] = e_add
                        else:
                            fr["obf"] = e_op
                            c["e_xm"] = e_add
                            fr["X"] = e_add
                            xb_free[xs] = e_add

                    def piece2():
                        ACT.wait(c["e_xm"])
                        e = ACT.sig(scalar.activation(out=junk[:], in_=xmid[s][:], func=AF.Square, accum_out=st[:, 0:1]))
                        ACT.wait(e)
                        e = ACT.sig(scalar.activation(out=st[:, 1:2], in_=st[:, 0:1], func=AF.Ln, bias=epsc[:, 0:1], scale=1.0 / D))
                        ACT.wait(e)
                        e = ACT.sig(scalar.activation(out=st[:, 2:3], in_=st[:, 1:2], func=AF.Exp, scale=-0.5))
                        DVE.wait(e, fr["hm"])
                        c["e_hm"] = DVE.sig(vector.tensor_scalar(out=hm[:], in0=xmid[s][:], scalar1=st[:, 2:3], scalar2=None, op0=ALU.mult))

                    def piece3():
                        PE.wait(c["e_hm"], fr["X"])
                        for k in range(8):
                            ins = tensor.transpose(out=tpv[:, k, :], in_=hm[:, k * 128:(k + 1) * 128], identity=idb[:])
                        e_t = PE.sig(ins)
                        fr["hm"] = e_t
                        ACT.wait(e_t, out_free[s])
                        e_c = ACT.sig(scalar.copy(out=hmT[s][:], in_=tpv))
                        fr["X"] = e_c
                        POOL.wait(c["e_xm"], e_c, c["e_hm"])
                        stb[s].add(gpsimd.dma_start(out=xmid_scr[tok0:tok0 + 128, :], in_=xmid[s][:]))
                        stb[s].add(gpsimd.dma_start(out=hmT_v[:, :, 1 + tok0:1 + tok0 + 128], in_=hmT[s][:]))
                        out_free[s] = stb[s].ev()

                    return [piece0, lambda: piece1(0), lambda: piece1(1), piece2, piece3]

                pending = None
                pending3 = None
                for i, lb in enumerate(range(b0, b1)):
                    s = i % 2
                    tok0 = lb * 128
                    w0, nch = cfg.win[l][lb]
                    kc0 = 64 * w0; kc1 = 64 * (w0 + 2 * nch)
                    SP.wait(blk_free[s])
                    L = ldb[s]
                    L.add(sync.dma_start(out=kaT[s][:, :, 0:kc1 - kc0], in_=qk_v[:, 4:8, kc0:kc1]))
                    L.add(sync.dma_start(out=qaT[s][:], in_=qk_v[:, 0:4, tok0:tok0 + 128]))
                    L.add(sync.dma_start(out=qbT[s][:], in_=qk_v[:, 8:12, tok0:tok0 + 128]))
                    L.add(sync.dma_start(out=kbT[s][:], in_=qk_scr[12 * 128:13 * 128, tok0 - 128:tok0 + 256]))
                    L.add(sync.dma_start(out=vaw[s][:, 0:nch, :], in_=v_scr[kc0:kc1, 0:512].rearrange("(c p) f -> p c f", p=128)))
                    ev_ld = L.add(sync.dma_start(out=vbw[s][:], in_=v_scr[tok0 - 128:tok0 + 256, 512:768].rearrange("(c p) f -> p c f", p=128)))
                    xs = i % 3
                    SP.wait(xb_free[xs])
                    ev_x = ldxb[xs].add(sync.dma_start(out=xb[xs][:], in_=xsrc[tok0:tok0 + 128, :]))

                    units = [(e, c) for e in range(2) for c in range(nch)]
                    ev_exp = {}

                    def na_S(u):
                        e, c = units[u]
                        j, S, frb = S_rot.get()
                        PE.wait(frb, ev_ld)
                        s0 = 8 - (w0 - 2 * lb + 2 * c)
                        for pair in range(4):
                            tensor.matmul(S[:, pair * 128:(pair + 1) * 128],
                                          lhsT=kaT[s][64 * e:64 * e + 64, pair, c * 128:(c + 1) * 128],
                                          rhs=qaT[s][64 * e:64 * e + 64, pair, :],
                                          start=(pair == 0), stop=False, skip_group_check=True)
                        ins = tensor.matmul(S[:, 0:512], lhsT=idb[:], rhs=ttb[:, e * 4:(e + 1) * 4, s0 * 64:s0 * 64 + 128],
                                            start=False, stop=True, skip_group_check=True)
                        e_s = PE.sig(ins)
                        ACT.wait(e_s, fr["PTn"])
                        Sv = S[:, 0:512].rearrange("p (h a q) -> p h a q", h=4, a=2)
                        for p in range(2):
                            col = ((l * NLB + lb) * 6 + c) * 2 + p
                            ins = scalar.activation(out=PTn[:, c, e * 4:(e + 1) * 4, p * 64:(p + 1) * 64], in_=Sv[:, :, p, :],
                                                    func=AF.Exp, bias=mbna[:, col:col + 1])
                        e_x = ACT.sig(ins)
                        S_rot.rel(j, e_x)
                        ev_exp[u] = e_x

                    def na_PV(u):
                        e, c = units[u]
                        PE.wait(ev_exp[u], fr["PD"])
                        for pair in range(4):
                            bank = P[pair // 2]
                            col0 = (pair % 2) * 256 + e * 128
                            st_flag = (e == 0 and c == 0 and pair % 2 == 0)
                            tensor.matmul(bank[:, col0:col0 + 128], lhsT=vaw[s][:, c, pair * 128:(pair + 1) * 128],
                                          rhs=PTn[:, c, e * 4 + pair, :], start=st_flag, stop=(c == nch - 1), skip_group_check=True)
                        return tensor.matmul(Dk[e][:, 0:512], lhsT=ones[:], rhs=PTn[:, c, e * 4:(e + 1) * 4, :],
                                             start=(c == 0), stop=(c == nch - 1), skip_group_check=True)

                    nu = len(units)
                    LAG = 4
                    for u in range(nu):
                        na_S(u)
                        if u >= LAG:
                            na_PV(u - LAG)
                        if pending3 is not None and u == 1:
                            pending3[0]()
                        if pending3 is not None and u == 5:
                            pending3[1]()
                            pending3 = None
                        if pending is not None and u == 8:
                            pending[0]()
                    for u in range(nu - LAG, nu):
                        ins = na_PV(u)
                    e_pvn = PE.sig(ins)
                    fr["PTn"] = e_pvn
                    if pending is not None:
                        pending[1]()
                    ACT.wait(e_pvn, fr["rawn"])
                    scalar.copy(out=rawPn[:, 0, :], in_=P[0][:, 0:512])
                    scalar.copy(out=rawDn[:, 0, :], in_=Dk[0][:, 0:512])
                    scalar.copy(out=rawPn[:, 1, :], in_=P[1][:, 0:512])
                    e_ca = ACT.sig(scalar.copy(out=rawDn[:, 1, :], in_=Dk[1][:, 0:512]))
                    e_cd = e_ca
                    fr["PD"] = [e_ca]
                    DVE.wait(e_ca, e_cd)
                    e_r = DVE.sig(vector.reciprocal(out=rec_n[:].rearrange("p h q -> p (h q)"), in_=rawDn[:].rearrange("p e q -> p (e q)")))
                    POOL.wait(e_r, e_ca, e_cd, fr["oun"])
                    DVE.wait(e_r, fr["oun"])
                    evs_ = []
                    for e in range(2):
                        EE, eng_ = (DVE, vector) if e == 0 else (POOL, gpsimd)
                        for bk in range(2):
                            src = rawPn[64 * e:64 * e + 64, bk, :].rearrange("p (m t q) -> p m t q", m=2, t=2)[:, :, e, :]
                            ins = eng_.tensor_tensor(out=o_un[64 * e:64 * e + 64, 2 * bk:2 * bk + 2, :], in0=src,
                                                     in1=rec_n[64 * e:64 * e + 64, e * 4 + 2 * bk:e * 4 + 2 * bk + 2, :], op=ALU.mult)
                        evs_.append(EE.sig(ins))
                    e_na = evs_
                    fr["rawn"] = e_na

                    sunits = [(g, c) for g in range(2) for c in range(3)]
                    sw_exp = {}

                    def sw_S(u):
                        g, c = sunits[u]
                        j, S, frb = S_rot.get()
                        PE.wait(frb, ev_ld)
                        tensor.matmul(S[:, 0:512], lhsT=kbT[s][64 * g:64 * g + 64, c * 128:(c + 1) * 128],
                                      rhs=qbT[s][64 * g:64 * g + 64, :, :], start=True, stop=False, skip_group_check=True)
                        ins = tensor.matmul(S[:, 0:512], lhsT=idb[:], rhs=bswb[:, c, g * 4:(g + 1) * 4, :], start=False, stop=True, skip_group_check=True)
                        e_s = PE.sig(ins)
                        ACT.wait(e_s, fr["PTs"])
                        col = lb * 3 + c
                        e_x = ACT.sig(scalar.activation(out=PTs[:, c, g * 4:(g + 1) * 4, :], in_=S[:, 0:512].rearrange("p (h q) -> p h q", h=4),
                                                        func=AF.Exp, bias=mbsw[:, col:col + 1]))
                        S_rot.rel(j, e_x)
                        sw_exp[u] = e_x

                    def sw_PV(u):
                        g, c = sunits[u]
                        PE.wait(sw_exp[u], fr["PD"])
                        tensor.matmul(P[g][:, 0:512], lhsT=vbw[s][:, c, g * 128:(g + 1) * 128], rhs=PTs[:, c, g * 4:(g + 1) * 4, :],
                                      start=(c == 0), stop=(c == 2), skip_group_check=True)
                        ins = tensor.matmul(Dk[g][:, 0:512], lhsT=ones[:], rhs=PTs[:, c, g * 4:(g + 1) * 4, :],
                                            start=(c == 0), stop=False, skip_group_check=True)
                        if c == 2:
                            o0 = (l * 8 + 4 * g) * 128
                            ins = tensor.matmul(Dk[g][:, 0:512], lhsT=ones[0:1, :], rhs=esrow[0:1, o0:o0 + 512],
                                                start=False, stop=True, skip_group_check=True)
                        return ins

                    for u in range(6):
                        sw_S(u)
                        if pending is not None and u == 1:
                            pending[2]()
                    for u in range(6):
                        ins = sw_PV(u)
                    e_pvs = PE.sig(ins)
                    fr["PTs"] = e_pvs
                    ACT.wait(e_pvs, fr["raws"])
                    scalar.copy(out=rawPs[:, 0, :], in_=P[0][:, 0:512])
                    e_ca = ACT.sig(scalar.copy(out=rawDs[:, 0, :], in_=Dk[0][:, 0:512]))
                    DVE.wait(e_pvs, fr["raws"])
                    vector.tensor_copy(out=rawPs[:, 1, :], in_=P[1][:, 0:512])
                    e_cd = DVE.sig(vector.tensor_copy(out=rawDs[:, 1, :], in_=Dk[1][:, 0:512]))
                    pd_sw = [e_ca, e_cd]
                    DVE.wait(e_ca, e_cd)
                    e_r = DVE.sig(vector.reciprocal(out=rec_s[:].rearrange("p h q -> p (h q)"), in_=rawDs[:].rearrange("p g q -> p (g q)")))
                    POOL.wait(e_r, e_ca, e_cd, e_na)
                    DVE.wait(e_r, e_na)
                    rsv = rec_s[:].rearrange("p (g m t) q -> p g m t q", g=2, m=2)
                    evs_ = []
                    for hf in range(2):
                        EE, eng_ = (DVE, vector) if hf == 0 else (POOL, gpsimd)
                        for g in range(2):
                            src = rawPs[64 * hf:64 * hf + 64, g, :].rearrange("p (m t q) -> p m t q", m=2, t=2)[:, :, hf, :]
                            ins = eng_.tensor_tensor(out=o_un[64 * hf:64 * hf + 64, 4 + 2 * g:6 + 2 * g, :], in0=src,
                                                     in1=rsv[64 * hf:64 * hf + 64, g, :, hf, :], op=ALU.mult)
                        evs_.append(EE.sig(ins))
                    e_sw = evs_
                    fr["raws"] = e_sw
                    fr["PD"] = pd_sw
                    blk_free[s] = e_pvs
                    if pending is not None:
                        pending3 = (pending[3], pending[4])
                    pending = make_out_pieces(i, lb, s, tok0, e_sw, e_pvs, ev_x, xs)
                if pending3 is not None:
                    pending3[0]()
                    pending3[1]()
                for pc in pending:
                    pc()
                phase_end([stb[0].ev(), stb[1].ev()])

        def phase_D(l):
            tiles = cfg.tiles[l]
            phase_begin()
            with ExitStack() as es:
                wu = sb(es, "d_wu", [128, 8, 2 * DFF], BF16)
                wd = sb(es, "d_wd", [128, 22, D], BF16)
                with ExitStack() as es2:
                    NSTG = 4
                    stg = [sb(es2, f"d_stg{i}", [128, 2816], F32) for i in range(NSTG)]
                    stg_free = [None] * NSTG
                    ldw = [ds["w0"], ds["w1"], ds["w2"], ds["w3"]]
                    jobs = [("u", k, q) for k in range(8) for q in range(2)] + [("d", k, 0) for k in range(11)]
                    for n, (kind, k, q) in enumerate(jobs):
                        s = n % NSTG
                        EE = DVE if n % 2 == 0 else ACT
                        SP.wait(stg_free[s])
                        if kind == "u":
                            ev = ldw[s].add(sync.dma_start(out=stg[s][:, 0:2816], in_=w_up_d[l, k * 128:(k + 1) * 128, q * 2816:(q + 1) * 2816]))
                            EE.wait(ev)
                            gk = gc[:, G[f"ffn{l}"] + k:G[f"ffn{l}"] + k + 1]
                            if EE is DVE:
                                ins = vector.tensor_scalar(out=wu[:, k, q * 2816:(q + 1) * 2816], in0=stg[s][:, 0:2816], scalar1=gk, scalar2=None, op0=ALU.mult)
                            else:
                                ins = scalar.activation(out=wu[:, k, q * 2816:(q + 1) * 2816], in_=stg[s][:, 0:2816], func=AF.Identity, scale=gk)
                        else:
                            ev = ldw[s].add(sync.dma_start(out=stg[s][:, 0:2048].rearrange("p (c n) -> p c n", c=2),
                                                           in_=w_down_d[l, k * 256:(k + 1) * 256, :].rearrange("(c p) n -> p c n", p=128)))
                            EE.wait(ev)
                            dst = wd[:, 2 * k:2 * k + 2, :].rearrange("p c n -> p (c n)")
                            if EE is DVE:
                                ins = vector.tensor_copy(out=dst, in_=stg[s][:, 0:2048])
                            else:
                                ins = scalar.copy(out=dst, in_=stg[s][:, 0:2048])
                        stg_free[s] = EE.sig(ins)
                    wready = list(stg_free)
                    for E in ENGS:
                        E.wait(wready)
                hTt = [sb(es, f"d_hT{i}", [128, 8, 386], BF16) for i in range(2)]
                actT = sb(es, "d_act", [128, 22, 384], BF16)
                tg = [sb(es, f"d_tg{i}", [128, 384], F32) for i in range(2)]
                tv = [sb(es, f"d_tv{i}", [128, 384], F32) for i in range(2)]
                xm = [sb(es, f"d_xm{i}", [128, D], F32) for i in range(3)]
                xo = [sb(es, f"d_xo{i}", [128, D], F32) for i in range(2)]
                junk = sb(es, "d_junk", [128, D], BF16)
                st = sb(es, "d_st", [128, 4], F32)
                gfin = sb(es, "d_gfin", [128, D], F32) if l == 1 else None
                U = [ps(es, f"d_U{i}", [128, 512], F32) for i in range(4)]
                U_free = [None] * 4
                Y_rot = Rot([ps(es, f"d_Y{i}", [128, 512], F32) for i in range(4)])
                ev_gf = None
                if l == 1:
                    ds["c"].add(sync.dma_start(out=gfin[:], in_=gfin_d[:, :]))
                    ev_gf = ds["c"].ev()

                ldh = [ds["h0"], ds["h1"]]
                ldm = [ds["m0"], ds["m1"], ds["m2"]]
                sto = [ds["o0"], ds["o1"]]
                hTt_free = [None, None]
                xm_free = [None] * 3
                xo_free = [None, None]
                t_free = [None, None]
                xmc = 0; xoc = 0
                for t, (tb0, nb) in enumerate(tiles):
                    s = t % 2
                    T = 128 * nb; tok0 = tb0 * 128
                    SP.wait(hTt_free[s])
                    ev_h = ldh[s].add(sync.dma_start(out=hTt[s][:, :, 0:T + 2], in_=hmT_v[:, :, tok0:tok0 + T + 2]))
                    ev_fix = None
                    if tb0 in cfg.PB:
                        j = cfg.PB.index(tb0)
                        DVE.wait(ev_h)
                        ev_fix = DVE.sig(vector.tensor_scalar(out=hTt[s][:, :, 0:1], in0=hTt[s][:, :, 0:1],
                                                              scalar1=gc[:, G["keep"] + j:G["keep"] + j + 1], scalar2=None, op0=ALU.mult))
                    if (tb0 + nb) in cfg.PB:
                        j = cfg.PB.index(tb0 + nb)
                        DVE.wait(ev_h)
                        ev_fix = DVE.sig(vector.tensor_scalar(out=hTt[s][:, :, T + 1:T + 2], in0=hTt[s][:, :, T + 1:T + 2],
                                                              scalar1=gc[:, G["keep"] + j:G["keep"] + j + 1], scalar2=None, op0=ALU.mult))
                    xm_ev = []
                    xm_slot = []
                    for sbk in range(nb):
                        xs = xmc % 3; xmc += 1
                        SP.wait(xm_free[xs])
                        xm_ev.append(ldm[xs].add(sync.dma_start(out=xm[xs][:], in_=xmid_scr[tok0 + sbk * 128:tok0 + (sbk + 1) * 128, :])))
                        xm_slot.append(xs)
                    e_a = None
                    e_acts = []
                    for jj in range(22):
                        pr = jj % 2
                        e3 = {}
                        for wh, colbase, bi_, tt_ in (("g", jj * 128, 2 * pr, tg[pr]), ("v", DFF + jj * 128, 2 * pr + 1, tv[pr])):
                            bank = U[bi_]
                            PE.wait(U_free[bi_], ev_h, ev_fix)
                            for k in range(8):
                                ins = tensor.matmul(bank[:, 0:T + 2], lhsT=wu[:, k, colbase:colbase + 128], rhs=hTt[s][:, k, 0:T + 2],
                                                    start=(k == 0), stop=(k == 7))
                            e_u = PE.sig(ins)
                            m = colbase // 128
                            ACT.wait(e_u, t_free[pr])
                            e1 = ACT.sig(scalar.activation(out=tt_[:, 0:T], in_=bank[:, 1:T + 1], func=AF.Identity,
                                                           bias=gc[:, G[f"cb{l}"] + m:G[f"cb{l}"] + m + 1],
                                                           scale=gc[:, G[f"cw1{l}"] + m:G[f"cw1{l}"] + m + 1]))
                            DVE.wait(e1)
                            e2 = DVE.sig(vector.scalar_tensor_tensor(out=tt_[:, 0:T], in0=bank[:, 0:T], scalar=gc[:, G[f"cw0{l}"] + m:G[f"cw0{l}"] + m + 1],
                                                                     in1=tt_[:, 0:T], op0=ALU.mult, op1=ALU.add))
                            DVE.wait(e2)
                            e3[wh] = DVE.sig(vector.scalar_tensor_tensor(out=tt_[:, 0:T], in0=bank[:, 2:T + 2], scalar=gc[:, G[f"cw2{l}"] + m:G[f"cw2{l}"] + m + 1],
                                                                         in1=tt_[:, 0:T], op0=ALU.mult, op1=ALU.add))
                            U_free[bi_] = e3[wh]
                        ACT.wait(e3["g"])
                        e_s = ACT.sig(scalar.activation(out=tg[pr][:, 0:T], in_=tg[pr][:, 0:T], func=AF.Silu))
                        POOL.wait(e_s, e3["v"])
                        e_a = POOL.sig(gpsimd.tensor_tensor(out=actT[:, jj, 0:T], in0=tg[pr][:, 0:T], in1=tv[pr][:, 0:T], op=ALU.mult))
                        t_free[pr] = e_a
                        e_acts.append(e_a)
                    hTt_free[s] = e_u
                    for sbk in range(nb):
                        os_ = xoc % 2; xoc += 1
                        xs = xm_slot[sbk]
                        tok = tok0 + sbk * 128
                        for hf in range(2):
                            j, Y, fr = Y_rot.get()
                            PE.wait(fr)
                            for jj in range(22):
                                PE.wait(e_acts[jj])
                                ins = tensor.matmul(Y[:, 0:512], lhsT=actT[:, jj, sbk * 128:(sbk + 1) * 128], rhs=wd[:, jj, hf * 512:(hf + 1) * 512],
                                                    start=(jj == 0), stop=(jj == 21))
                            e_y = PE.sig(ins)
                            DVE.wait(e_y, xm_ev[sbk], xo_free[os_])
                            e_o = DVE.sig(vector.tensor_tensor(out=xo[os_][:, hf * 512:(hf + 1) * 512], in0=Y[:, 0:512],
                                                               in1=xm[xs][:, hf * 512:(hf + 1) * 512], op=ALU.add))
                            Y_rot.rel(j, e_o)
                        xm_free[xs] = e_o
                        if l == 0:
                            POOL.wait(e_o)
                            sto[os_].add(gpsimd.dma_start(out=x1_scr[tok:tok + 128, :], in_=xo[os_][:]))
                        else:
                            ACT.wait(e_o)
                            e = ACT.sig(scalar.activation(out=junk[:], in_=xo[os_][:], func=AF.Square, accum_out=st[:, 0:1]))
                            ACT.wait(e)
                            e = ACT.sig(scalar.activation(out=st[:, 1:2], in_=st[:, 0:1], func=AF.Sqrt, bias=EPS, scale=1.0 / D))
                            DVE.wait(e)
                            e = DVE.sig(vector.reciprocal(out=st[:, 2:3], in_=st[:, 1:2]))
                            DVE.wait(e)
                            e = DVE.sig(vector.tensor_scalar(out=xo[os_][:], in0=xo[os_][:], scalar1=st[:, 2:3], scalar2=None, op0=ALU.mult))
                            POOL.wait(e, ev_gf)
                            e = POOL.sig(gpsimd.tensor_tensor(out=xo[os_][:], in0=xo[os_][:], in1=gfin[:], op=ALU.mult))
                            POOL.wait(e)
                            yt = tok - HALO * 128
                            sto[os_].add(gpsimd.dma_start(out=y_d[yt:yt + 128, :], in_=xo[os_][:]))
                        xo_free[os_] = sto[os_].ev()
                phase_end([sto[0].ev(), sto[1].ev()])

        phase_A(0, xin)
        phase_BC(0, xin)
        phase_D(0)
        phase_A(1, x1_scr)
        phase_BC(1, x1_scr)
        phase_D(1)
        for E in ENGS:
            E.wait(state["phase_ev"])
    return nc


def run_cfg(cfg, xcat, params):
    nc = build_program(cfg)
    maps = host_inputs(cfg, xcat, params)
    res = run_bass_kernel_spmd(nc, maps, core_ids=list(range(cfg.NC)))
    return np.concatenate([np.asarray(r["y"]) for r in res.results], axis=0)


def kernel(x_prompt, x_sample, norm_mix, w_in, rpb, sinks, norm_grp, w_out, norm_ffn, w_up, conv_w, conv_b, w_down, norm_final):
    xp = np.asarray(x_prompt, np.float32)
    xs = np.asarray(x_sample, np.float32)
    T = xp.shape[1]
    xcat = np.concatenate([xp.reshape(-1, D), xs.reshape(-1, D)], axis=0)
    nseq = xp.shape[0] + xs.shape[0]
    cfg = Cfg(8, nseq, T // 128)
    params = dict(norm_mix=np.asarray(norm_mix), w_in=np.asarray(w_in), rpb=np.asarray(rpb), sinks=np.asarray(sinks),
                  norm_grp=np.asarray(norm_grp), w_out=np.asarray(w_out), norm_ffn=np.asarray(norm_ffn), w_up=np.asarray(w_up),
                  conv_w=np.asarray(conv_w), conv_b=np.asarray(conv_b), w_down=np.asarray(w_down), norm_final=np.asarray(norm_final))
    y = run_cfg(cfg, xcat, params)
    y = y.reshape(nseq, T, D)
    return (np.ascontiguousarray(y[:xp.shape[0]]), np.ascontiguousarray(y[xp.shape[0]:]))
```

```python
import numpy as np
from contextlib import ExitStack
import concourse.bass as bass
import concourse.mybir as mybir
from concourse.bass_utils import run_bass_kernel_spmd

F32 = mybir.dt.float32
BF16 = mybir.dt.bfloat16
AF = mybir.ActivationFunctionType
ALU = mybir.AluOpType
D = 1024
DFF = 2816
EPS = 1e-6
NEG = -30000.0
HALO = 6


class Cfg:
    def __init__(self, ncores, nseq, seq_blocks):
        self.NC, self.NSEQ, self.SB = ncores, nseq, seq_blocks
        self.TOTB = nseq * seq_blocks
        assert self.TOTB % ncores == 0
        self.BPC = self.TOTB // ncores
        self.NLB = self.BPC + 2 * HALO
        self.NTOK = self.NLB * 128
        pb = set()
        for c in range(ncores):
            for s in range(nseq + 1):
                o = s * seq_blocks - c * self.BPC
                if 0 <= o <= self.BPC:
                    pb.add(o + HALO)
        self.PB = sorted(pb)
        N = self.NLB
        self.rA = [(0, N), (3, N - 3)]
        self.rBC = [(2, N - 2), (5, N - 5)]
        self.rD = [(3, N - 3), (6, N - 6)]
        self.win = []
        for l in range(2):
            a0, a1 = self.rA[l]
            w = {}
            for lb in range(*self.rBC[l]):
                r0 = 2 * lb
                cand = (r0 - 4, 5)
                if (lb + 1) in self.PB:
                    cand = (r0 - 6, 6)
                elif lb in self.PB:
                    cand = (r0 - 4, 6)
                if cand[0] < 2 * a0 or cand[0] + 2 * cand[1] > 2 * a1:
                    cand = (r0 - 4, 5)
                assert cand[0] >= 2 * a0 and cand[0] + 2 * cand[1] <= 2 * a1
                w[lb] = cand
            self.win.append(w)
        self.tiles = []
        for l in range(2):
            d0, d1 = self.rD[l]
            cuts = sorted({d0, d1} | {p for p in self.PB if d0 < p < d1})
            tl = []
            for s0, s1 in zip(cuts[:-1], cuts[1:]):
                n = s1 - s0
                sizes = []
                while n > 0:
                    if n % 3 == 0 or n >= 5:
                        sizes.append(3); n -= 3
                    elif n >= 2:
                        sizes.append(2); n -= 2
                    else:
                        sizes.append(1); n -= 1
                b = s0
                for sz in sizes:
                    tl.append((b, sz)); b += sz
            self.tiles.append(tl)
        self.goff = {}
        o = 0
        for l in range(2):
            for nm, w in (("mix", 8), ("grp", 8), ("ffn", 8), ("cw0", 44), ("cw1", 44), ("cw2", 44), ("cb", 44)):
                self.goff[f"{nm}{l}"] = o; o += w
        self.goff["sink"] = o; o += 16
        self.goff["keep"] = o; o += len(self.PB)
        self.NG = o


def _vcol(v, n):
    return np.ascontiguousarray(np.asarray(v, np.float32).reshape(n, 128).T)


def build_tt(rpb_l):
    rpb_l = np.asarray(rpb_l, np.float32)
    tt = np.full((2, 64, 8, 16, 64), NEG, np.float32)
    qc = np.arange(64)
    cs = np.clip(qc - 8, 0, 48)
    kc = np.arange(64)
    valid = (kc[None, :] >= cs[:, None]) & (kc[None, :] < cs[:, None] + 16)
    dcc = np.clip(kc[None, :] - qc[:, None] + 15, 0, 30)
    for a in range(2):
        for s in range(16):
            off = 8 - s + a
            if abs(off) > 7:
                continue
            for hidx in range(8):
                e, pair = hidx // 4, hidx % 4
                h = 2 * pair + e
                slab = np.where(valid, rpb_l[h, off + 7][dcc], np.float32(NEG))
                tt[a, :, hidx, s, :] = slab.T
    return tt.reshape(128, 8, 1024)


def build_bsw():
    s = np.arange(128)[:, None, None, None]
    c = np.arange(3)[None, :, None, None]
    h = np.arange(8)[None, None, :, None]
    t = np.arange(128)[None, None, None, :]
    dist = np.abs(t - (s + 128 * (c - 1)))
    slope = np.exp2(-(h + 1.0))
    val = -(slope * dist)
    return np.where(dist <= 128, val, NEG).astype(np.float32)


def na_masks(cfg, core):
    NLB = cfg.NLB
    m = np.full((2, NLB, 6, 2, 2), NEG, np.float32)
    grow0 = 2 * (core * cfg.BPC - HALO)
    RPS = 2 * cfg.SB
    for l in range(2):
        for lb, (w0, nch) in cfg.win[l].items():
            for p in range(2):
                gq = grow0 + 2 * lb + p
                real = 0 <= gq < 2 * cfg.TOTB
                oks = np.zeros((6, 2), bool)
                for c in range(nch):
                    for a in range(2):
                        gk = grow0 + w0 + 2 * c + a
                        if real:
                            sq = gq // RPS
                            rs = int(np.clip(gq % RPS - 4, 0, RPS - 8)) + sq * RPS
                            oks[c, a] = rs <= gk < rs + 8
                        else:
                            oks[c, a] = -4 <= gk - gq <= 3
                if not oks.any():
                    for c in range(nch):
                        for a in range(2):
                            gk = grow0 + w0 + 2 * c + a
                            oks[c, a] = -4 <= gk - gq <= 3
                m[l, lb, :, p, :] = np.where(oks, 0.0, NEG)
    mm = np.transpose(m, (4, 0, 1, 2, 3)).reshape(2, -1)
    return np.ascontiguousarray(np.repeat(mm, 64, axis=0)).astype(np.float32)


def sw_masks(cfg, core):
    NLB = cfg.NLB
    m = np.zeros((NLB, 3), np.float32)
    for lb in range(NLB):
        gb = core * cfg.BPC - HALO + lb
        real = 0 <= gb < cfg.TOTB
        for c in range(3):
            gk = gb + c - 1
            if real:
                ok = (0 <= gk < cfg.TOTB) and (gk // cfg.SB == gb // cfg.SB)
            else:
                ok = True
            m[lb, c] = 0.0 if ok else NEG
    return np.ascontiguousarray(np.broadcast_to(m.reshape(1, -1), (128, NLB * 3))).astype(np.float32)


def host_inputs(cfg, xcat, p):
    shared = {
        "w_in": np.ascontiguousarray(p["w_in"], np.float32),
        "w_out": np.ascontiguousarray(p["w_out"], np.float32),
        "w_up": np.ascontiguousarray(p["w_up"], np.float32),
        "w_down": np.ascontiguousarray(p["w_down"], np.float32),
        "tt": np.stack([build_tt(p["rpb"][l]) for l in range(2)]),
        "bsw": build_bsw(),
        "ident": np.eye(128, dtype=np.float32),
        "gfin": np.ascontiguousarray(np.broadcast_to(np.asarray(p["norm_final"], np.float32)[None, :], (128, D))),
    }
    maps = []
    for c in range(cfg.NC):
        g = np.zeros((128, cfg.NG), np.float32)
        for l in range(2):
            g[:, cfg.goff[f"mix{l}"]:][:, :8] = _vcol(p["norm_mix"][l], 8)
            g[:, cfg.goff[f"grp{l}"]:][:, :8] = _vcol(p["norm_grp"][l], 8)
            g[:, cfg.goff[f"ffn{l}"]:][:, :8] = _vcol(p["norm_ffn"][l], 8)
            for j in range(3):
                g[:, cfg.goff[f"cw{j}{l}"]:][:, :44] = _vcol(p["conv_w"][l][j], 44)
            g[:, cfg.goff[f"cb{l}"]:][:, :44] = _vcol(p["conv_b"][l], 44)
        g[:, cfg.goff["sink"]:][:, :16] = np.asarray(p["sinks"], np.float32).reshape(1, 16)
        for j, pb in enumerate(cfg.PB):
            gb = c * cfg.BPC - HALO + pb
            realb = (0 <= gb <= cfg.TOTB) and (gb % cfg.SB == 0)
            g[:, cfg.goff["keep"] + j] = 0.0 if realb else 1.0
        xin = np.zeros((cfg.NTOK, D), np.float32)
        t0 = (c * cfg.BPC - HALO) * 128
        lo, hi = max(t0, 0), min(t0 + cfg.NTOK, xcat.shape[0])
        xin[lo - t0:hi - t0] = xcat[lo:hi]
        d = dict(shared)
        d.update({"xin": xin, "gcols": g, "mbna": na_masks(cfg, c), "mbsw": sw_masks(cfg, c)})
        maps.append(d)
    return maps


class Eng:
    def __init__(self, nc, eng, name):
        self.e = eng
        self.sem = nc.alloc_semaphore("pg_" + name)
        self.n = 0
        self.seen = {}

    def wait(self, *evs):
        for ev in evs:
            if ev is None:
                continue
            if isinstance(ev, list):
                self.wait(*ev)
                continue
            sem, v = ev
            k = id(sem)
            if self.seen.get(k, 0) >= v:
                continue
            self.seen[k] = v
            self.e.wait_ge(sem, v)

    def sig(self, ins):
        self.n += 1
        ins.then_inc(self.sem, 1)
        return (self.sem, self.n)


class DSem:
    def __init__(self, nc, name):
        self.sem = nc.alloc_semaphore("dm_" + name)
        self.n = 0

    def add(self, ins):
        ins.then_inc(self.sem, 16)
        self.n += 16
        return (self.sem, self.n)

    def ev(self):
        return (self.sem, self.n) if self.n else None


class Rot:
    def __init__(self, banks):
        self.b = banks
        self.free = [None] * len(banks)
        self.i = 0

    def get(self):
        j = self.i % len(self.b)
        self.i += 1
        return j, self.b[j], self.free[j]

    def rel(self, j, ev):
        self.free[j] = ev


def build_program(cfg):
    nc = bass.Bass("TRN2", target_bir_lowering=False)
    NLB, NTOK, BPC = cfg.NLB, cfg.NTOK, cfg.BPC
    G = cfg.goff

    def din(name, shape):
        return nc.dram_tensor(name, list(shape), F32, kind="ExternalInput").ap()

    xin = din("xin", [NTOK, D])
    gcols_d = din("gcols", [128, cfg.NG])
    gfin_d = din("gfin", [128, D])
    tt_d = din("tt", [2, 128, 8, 1024])
    bsw_d = din("bsw", [128, 3, 8, 128])
    mbna_d = din("mbna", [128, 2 * NLB * 12])
    mbsw_d = din("mbsw", [128, NLB * 3])
    ident_d = din("ident", [128, 128])
    w_in_d = din("w_in", [2, D, 2304])
    w_out_d = din("w_out", [2, D, D])
    w_up_d = din("w_up", [2, D, 2 * DFF])
    w_down_d = din("w_down", [2, DFF, D])
    y_d = nc.dram_tensor("y", [BPC * 128, D], F32, kind="ExternalOutput").ap()

    qk_scr = nc.dram_tensor("qk_scr", [13 * 128, NTOK], BF16).ap()
    v_scr = nc.dram_tensor("v_scr", [NTOK, 768], BF16).ap()
    xmid_scr = nc.dram_tensor("xmid_scr", [NTOK, D], F32).ap()
    hmT_scr = nc.dram_tensor("hmT_scr", [D, NTOK + 2], BF16).ap()
    x1_scr = nc.dram_tensor("x1_scr", [NTOK, D], F32).ap()

    PE = Eng(nc, nc.tensor, "pe")
    ACT = Eng(nc, nc.scalar, "act")
    DVE = Eng(nc, nc.vector, "dve")
    POOL = Eng(nc, nc.gpsimd, "pool")
    SP = Eng(nc, nc.sync, "sp")
    ENGS = [PE, ACT, DVE, POOL, SP]
    sync, scalar, vector, gpsimd, tensor = nc.sync, nc.scalar, nc.vector, nc.gpsimd, nc.tensor

    ds = {n: DSem(nc, n) for n in ["c", "w0", "w1", "w2", "w3", "x0", "x1", "q0", "q1", "b0", "b1", "s0", "s1",
                                   "h0", "h1", "m0", "m1", "m2", "o0", "o1"]}
    qk_v = qk_scr.rearrange("(c p) t -> p c t", p=128)
    hmT_v = hmT_scr.rearrange("(k p) t -> p k t", p=128)

    with ExitStack() as top:
        uid = [0]

        def sb(es, name, shape, dty):
            uid[0] += 1
            return es.enter_context(nc.sbuf_tensor(f"{name}_u{uid[0]}", list(shape), dty))

        def ps(es, name, shape, dty):
            uid[0] += 1
            return es.enter_context(nc.psum_tensor(f"{name}_u{uid[0]}", list(shape), dty))

        gc = sb(top, "gc", [128, cfg.NG], F32)
        mbna = sb(top, "mbna_sb", [128, 2 * NLB * 12], F32)
        mbsw = sb(top, "mbsw_sb", [128, NLB * 3], F32)
        idf = sb(top, "idf", [128, 128], F32)
        idb = sb(top, "idb", [128, 128], BF16)
        ones = sb(top, "ones", [128, 128], BF16)
        esink = sb(top, "esink", [128, 16], F32)
        dummy = sb(top, "k_dummy", [128, 2], F32)

        sync.dma_start(out=gc[:], in_=gcols_d[:, :]).then_inc(ds["c"].sem, 16)
        sync.dma_start(out=mbna[:], in_=mbna_d[:, :]).then_inc(ds["c"].sem, 16)
        sync.dma_start(out=mbsw[:], in_=mbsw_d[:, :]).then_inc(ds["c"].sem, 16)
        sync.dma_start(out=idf[:], in_=ident_d[:, :]).then_inc(ds["c"].sem, 16)
        ds["c"].n = 64
        ev_c = ds["c"].ev()
        DVE.wait(ev_c)
        vector.tensor_copy(out=idb[:], in_=idf[:])
        ev_const = DVE.sig(vector.memset(ones[:], 1.0))
        ACT.wait(ev_c)
        ev_sink = ACT.sig(scalar.activation(out=esink[:], in_=gc[:, G["sink"]:G["sink"] + 16], func=AF.Exp))
        esrow = sb(top, "esrow", [1, 2048], BF16)
        epsc = sb(top, "epsc", [128, 2], F32)
        vector.memset(epsc[:], EPS)
        DVE.wait(ev_sink, ev_const)
        for hh in range(16):
            ins = vector.tensor_scalar(out=esrow[0:1, hh * 128:(hh + 1) * 128], in0=ones[0:1, :], scalar1=esink[0:1, hh:hh + 1], scalar2=None, op0=ALU.mult)
        ev_esrow = DVE.sig(ins)
        for E in ENGS:
            E.wait(ev_c, ev_const, ev_sink, ev_esrow)

        state = {"phase_ev": None}

        def phase_begin():
            for E in ENGS:
                E.wait(state["phase_ev"])

        def phase_end(store_evs):
            POOL.wait(*store_evs)
            state["phase_ev"] = POOL.sig(gpsimd.memset(dummy[:], 0.0))

        def phase_A(l, xsrc):
            a0, a1 = cfg.rA[l]
            groups = [list(range(b, min(b + 4, a1))) for b in range(a0, a1, 4)]
            phase_begin()
            with ExitStack() as es:
                wres = sb(es, "a_wres", [128, 8, 2432], BF16)
                stg = [sb(es, f"a_stg{i}", [128, 2304], F32) for i in range(2)]
                xg = [sb(es, f"a_xg{i}", [128, 4, D], F32) for i in range(2)]
                hb = [sb(es, f"a_hb{i}", [128, D], BF16) for i in range(2)]
                hT = [sb(es, f"a_hT{i}", [128, 8, 512], BF16) for i in range(2)]
                qko = [sb(es, f"a_qko{i}", [128, 13, 512], BF16) for i in range(2)]
                vo = [sb(es, f"a_vo{i}", [128, 4, 768], BF16) for i in range(2)]
                junk = sb(es, "a_junk", [128, D], BF16)
                st = sb(es, "a_st", [128, 12], F32)
                tp = ps(es, "a_tp", [128, 8, 128], BF16)
                rot = Rot([ps(es, f"a_pm{i}", [128, 512], F32) for i in range(6)])

                stg_free = [None, None]
                ldw = [ds["w0"], ds["w1"]]
                for k in range(8):
                    s = k % 2
                    SP.wait(stg_free[s])
                    ev = ldw[s].add(sync.dma_start(out=stg[s][:], in_=w_in_d[l, k * 128:(k + 1) * 128, :]))
                    DVE.wait(ev)
                    gk = gc[:, G[f"mix{l}"] + k:G[f"mix{l}"] + k + 1]
                    src, dst = stg[s], wres
                    vector.tensor_scalar(out=dst[:, k, 0:512], in0=src[:, 0:512], scalar1=gk, scalar2=0.125, op0=ALU.mult, op1=ALU.mult)
                    vector.tensor_scalar(out=dst[:, k, 512:1024], in0=src[:, 512:1024], scalar1=gk, scalar2=None, op0=ALU.mult)
                    vector.tensor_scalar(out=dst[:, k, 1024:1536].rearrange("p (j a c) -> p j a c", j=4, a=2),
                                         in0=src[:, 1536:2048].rearrange("p (a j c) -> p j a c", a=2, j=4),
                                         scalar1=gk, scalar2=0.125, op0=ALU.mult, op1=ALU.mult)
                    vector.tensor_scalar(out=dst[:, k, 1536:1664], in0=src[:, 2048:2176], scalar1=gk, scalar2=None, op0=ALU.mult)
                    vector.tensor_scalar(out=dst[:, k, 1664:2176], in0=src[:, 1024:1536], scalar1=gk, scalar2=None, op0=ALU.mult)
                    dv = dst[:, k, 2176:2432].rearrange("p (g d c) -> p g d c", g=2, d=2)
                    sv = src[:, 2176:2304].rearrange("p (g c) -> p g c", g=2)
                    vector.tensor_scalar(out=dv[:, :, 0, :], in0=sv, scalar1=gk, scalar2=None, op0=ALU.mult)
                    ins = vector.tensor_scalar(out=dv[:, :, 1, :], in0=sv, scalar1=gk, scalar2=None, op0=ALU.mult)
                    stg_free[s] = DVE.sig(ins)
                wready = stg_free[1]
                PE.wait(wready)

                ldx = [ds["x0"], ds["x1"]]
                stq = [ds["q0"], ds["q1"]]
                xg_free = [None, None]; hb_free = [None, None]; hT_free = [None, None]
                out_free = [None, None]
                stat_free = [None] * 4
                store_evs = []
                A = {"bcount": 0, "tp_free": None}
                def stage1_begin(gi, blks):
                    s = gi % 2
                    nb = len(blks); T = 128 * nb; tok0 = blks[0] * 128
                    SP.wait(xg_free[s])
                    ev_x = ldx[s].add(sync.dma_start(out=xg[s][:, 0:nb, :],
                                                     in_=xsrc[tok0:tok0 + T, :].rearrange("(b p) d -> p b d", p=128)))
                    return {"s": s, "nb": nb, "T": T, "tok0": tok0, "evs_hT": [], "ev_x": ev_x, "done": 0}

                def stage1_block(cx, part="both"):
                    if cx is None or cx["done"] >= cx["nb"]:
                        return
                    if part in ("norm", "both") and cx.get("pend") is None:
                        stage1_norm(cx)
                    if part in ("tr", "both") and cx.get("pend") is not None:
                        stage1_tr(cx)

                def stage1_norm(cx):
                    bi = cx["done"]
                    s, nb, ev_x = cx["s"], cx["nb"], cx["ev_x"]
                    if True:
                        q = A["bcount"] % 4; sc = q * 3; hs = A["bcount"] % 2; A["bcount"] += 1
                        ACT.wait(ev_x, stat_free[q])
                        e = ACT.sig(scalar.activation(out=junk[:], in_=xg[s][:, bi, :], func=AF.Square, accum_out=st[:, sc:sc + 1]))
                        ACT.wait(e)
                        e = ACT.sig(scalar.activation(out=st[:, sc + 1:sc + 2], in_=st[:, sc:sc + 1], func=AF.Sqrt, bias=EPS, scale=1.0 / D))
                        DVE.wait(e)
                        e = DVE.sig(vector.reciprocal(out=st[:, sc + 2:sc + 3], in_=st[:, sc + 1:sc + 2]))
                        DVE.wait(e, hb_free[hs], ev_x)
                        e_h = DVE.sig(vector.tensor_scalar(out=hb[hs][:], in0=xg[s][:, bi, :], scalar1=st[:, sc + 2:sc + 3], scalar2=None, op0=ALU.mult))
                        stat_free[q] = e_h
                        if bi == nb - 1:
                            xg_free[s] = e_h
                        cx["pend"] = (bi, hs, e_h)

                def stage1_tr(cx):
                    bi, hs, e_h = cx["pend"]
                    cx["pend"] = None
                    cx["done"] += 1
                    s, evs_hT = cx["s"], cx["evs_hT"]
                    if True:
                        PE.wait(e_h, A["tp_free"])
                        for k in range(8):
                            ins = tensor.transpose(out=tp[:, k, :], in_=hb[hs][:, k * 128:(k + 1) * 128], identity=idb[:])
                        e_t = PE.sig(ins)
                        hb_free[hs] = e_t
                        ACT.wait(e_t, hT_free[s])
                        e_c = ACT.sig(scalar.copy(out=hT[s][:, :, bi * 128:(bi + 1) * 128], in_=tp[:]))
                        A["tp_free"] = e_c
                        evs_hT.append(e_c)

                def stage2(gi, blks, ctx, nxt):
                    s, nb, T, tok0, evs_hT = ctx["s"], ctx["nb"], ctx["T"], ctx["tok0"], ctx["evs_hT"]
                    evac = []
                    n_ev = 0
                    for cc in range(13):
                        if cc in (0, 3, 6, 9):
                            stage1_block(nxt, "norm")
                        elif cc in (2, 5, 8, 11):
                            stage1_block(nxt, "tr")
                        j, bank, fr = rot.get()
                        PE.wait(fr, *evs_hT)
                        for k in range(8):
                            ins = tensor.matmul(bank[:, 0:T], lhsT=wres[:, k, cc * 128:(cc + 1) * 128], rhs=hT[s][:, k, 0:T],
                                                start=(k == 0), stop=(k == 7))
                        e_m = PE.sig(ins)
                        E = ACT if n_ev % 2 == 0 else DVE
                        n_ev += 1
                        E.wait(e_m, out_free[s])
                        if E is ACT:
                            e_e = E.sig(scalar.copy(out=qko[s][:, cc, 0:T], in_=bank[:, 0:T]))
                        else:
                            e_e = E.sig(vector.tensor_copy(out=qko[s][:, cc, 0:T], in_=bank[:, 0:T]))
                        rot.rel(j, e_e)
                        evac.append(e_e)
                    for bi in range(nb):
                        for (c0, c1, w) in ((1664, 2176, 512), (2176, 2432, 256)):
                            j, bank, fr = rot.get()
                            PE.wait(fr)
                            for k in range(8):
                                ins = tensor.matmul(bank[:, 0:w], lhsT=hT[s][:, k, bi * 128:(bi + 1) * 128], rhs=wres[:, k, c0:c1],
                                                    start=(k == 0), stop=(k == 7))
                            e_m = PE.sig(ins)
                            E = ACT if n_ev % 2 == 0 else DVE
                            n_ev += 1
                            E.wait(e_m, out_free[s])
                            o0 = 0 if w == 512 else 512
                            if E is ACT:
                                e_e = E.sig(scalar.copy(out=vo[s][:, bi, o0:o0 + w], in_=bank[:, 0:w]))
                            else:
                                e_e = E.sig(vector.tensor_copy(out=vo[s][:, bi, o0:o0 + w], in_=bank[:, 0:w]))
                            rot.rel(j, e_e)
                            evac.append(e_e)
                    hT_free[s] = e_m
                    POOL.wait(*evac)
                    stq[s].add(gpsimd.dma_start(out=qk_v[:, :, tok0:tok0 + T], in_=qko[s][:, :, 0:T]))
                    stq[s].add(gpsimd.dma_start(out=v_scr[tok0:tok0 + T, :].rearrange("(b p) f -> p b f", p=128), in_=vo[s][:, 0:nb, :]))
                    out_free[s] = stq[s].ev()

                ctx_next = stage1_begin(0, groups[0])
                while ctx_next["done"] < ctx_next["nb"]:
                    stage1_block(ctx_next)
                for gi, blks in enumerate(groups):
                    ctx = ctx_next
                    ctx_next = stage1_begin(gi + 1, groups[gi + 1]) if gi + 1 < len(groups) else None
                    stage2(gi, blks, ctx, ctx_next)
                    while ctx_next is not None and ctx_next["done"] < ctx_next["nb"]:
                        stage1_block(ctx_next)
                phase_end([stq[0].ev(), stq[1].ev()])

        def phase_BC(l, xsrc):
            b0, b1 = cfg.rBC[l]
            phase_begin()
            with ExitStack() as es:
                wo = sb(es, "b_wo", [128, 8, D], BF16)
                ttb = sb(es, "b_tt", [128, 8, 1024], BF16)
                bswb = sb(es, "b_bsw", [128, 3, 8, 128], BF16)
                stg = [sb(es, f"b_stg{i}", [128, 1024], F32) for i in range(2)]
                kaT = [sb(es, f"b_kaT{i}", [128, 4, 768], BF16) for i in range(2)]
                qaT = [sb(es, f"b_qaT{i}", [128, 4, 128], BF16) for i in range(2)]
                qbT = [sb(es, f"b_qbT{i}", [128, 4, 128], BF16) for i in range(2)]
                kbT = [sb(es, f"b_kbT{i}", [128, 384], BF16) for i in range(2)]
                vaw = [sb(es, f"b_vaw{i}", [128, 6, 512], BF16) for i in range(2)]
                vbw = [sb(es, f"b_vbw{i}", [128, 3, 256], BF16) for i in range(2)]
                xb = [sb(es, f"b_xb{i}", [128, D], F32) for i in range(3)]
                xb_free = [None, None, None]
                ldxb = [ds["m0"], ds["m1"], ds["m2"]]
                PTn = sb(es, "b_PTn", [128, 6, 8, 128], BF16)
                PTs = sb(es, "b_PTs", [128, 3, 8, 128], BF16)
                rec_n = sb(es, "b_recn", [128, 8, 128], F32)
                rawPn = sb(es, "b_rawPn", [128, 2, 512], F32)
                rawDn = sb(es, "b_rawDn", [128, 2, 512], F32)
                rawPs = sb(es, "b_rawPs", [128, 2, 512], F32)
                rawDs = sb(es, "b_rawDs", [128, 2, 512], F32)
                rec_s = sb(es, "b_recs", [128, 8, 128], F32)
                o_un = sb(es, "b_oun", [128, 8, 128], F32)
                sqb = sb(es, "b_sqb", [128, 8, 128], BF16)
                ab_s = sb(es, "b_abs", [128, 256], F32)
                ab = sb(es, "b_ab", [128, 256], F32)
                o_bf = sb(es, "b_obf", [128, 8, 128], BF16)
                xmid = [sb(es, f"b_xmid{i}", [128, D], F32) for i in range(2)]
                junk = sb(es, "b_junk", [128, D], BF16)
                hm = sb(es, "b_hm", [128, D], BF16)
                hmT = [sb(es, f"b_hmT{i}", [128, 8, 128], BF16) for i in range(2)]
                st = sb(es, "b_st", [128, 4], F32)
                S_rot = Rot([ps(es, f"b_S{i}", [128, 512], F32) for i in range(2)])
                X = ps(es, "b_X", [128, 512], F32)
                P = [ps(es, f"b_P{i}", [128, 512], F32) for i in range(2)]
                Dk = [ps(es, f"b_D{i}", [128, 512], F32) for i in range(2)]
                tp = ps(es, "b_tp", [128, 8, 128], BF16)

                stg_free = [None, None]
                ldw = [ds["w0"], ds["w1"]]
                jobs = []
                for k in range(8):
                    jobs.append(("wo", k))
                for h in range(8):
                    jobs.append(("tt", h))
                for c in range(3):
                    jobs.append(("bsw", c))
                for n, (kind, k) in enumerate(jobs):
                    s = n % 2
                    SP.wait(stg_free[s])
                    if kind == "wo":
                        src_ap = w_out_d[l, k * 128:(k + 1) * 128, :]
                    elif kind == "tt":
                        src_ap = tt_d[l, :, k, :]
                    else:
                        src_ap = bsw_d[:, k, :, :].rearrange("p h q -> p (h q)")
                    ev = ldw[s].add(sync.dma_start(out=stg[s][:], in_=src_ap))
                    DVE.wait(ev)
                    if kind == "wo":
                        gk = gc[:, G[f"grp{l}"] + k:G[f"grp{l}"] + k + 1]
                        ins = vector.tensor_scalar(out=wo[:, k, :], in0=stg[s][:], scalar1=gk, scalar2=None, op0=ALU.mult)
                    elif kind == "tt":
                        ins = vector.tensor_copy(out=ttb[:, k, :], in_=stg[s][:])
                    else:
                        ins = vector.tensor_copy(out=bswb[:, k, :, :].rearrange("p h q -> p (h q)"), in_=stg[s][:])
                    stg_free[s] = DVE.sig(ins)
                wready = [stg_free[0], stg_free[1]]
                PE.wait(wready)

                ldb = [ds["b0"], ds["b1"]]
                stb = [ds["s0"], ds["s1"]]
                blk_free = [None, None]
                out_free = [None, None]
                fr = {"PTn": None, "PTs": None, "PD": None, "oun": None, "sqb": None, "obf": None, "hm": None, "tp": None, "X": None, "rawn": None, "raws": None}

                def make_out_pieces(i, lb, s, tok0, e_sw, e_pvs, ev_x, xs):
                    c = {}

                    def piece0a():
                        ACT.wait(e_sw, fr["sqb"])
                        c["e_sq"] = ACT.sig(scalar.activation(out=sqb[:].rearrange("p k q -> p (k q)"), in_=o_un[:].rearrange("p k q -> p (k q)"), func=AF.Square))

                    def piece0():
                        PE.wait(c["e_sq"], fr["X"])
                        for grp in range(2):
                            for kk in range(4):
                                ins = tensor.matmul(X[:, grp * 128:(grp + 1) * 128], lhsT=ones[:], rhs=sqb[:, grp * 4 + kk, :],
                                                    start=(kk == 0), stop=(kk == 3), skip_group_check=True)
                        e_ss = PE.sig(ins)
                        fr["sqb"] = e_ss
                        ACT.wait(e_ss)
                        c["e_q"] = ACT.sig(scalar.activation(out=ab_s[:], in_=X[:, 0:256], func=AF.Ln, bias=epsc[:, 0:1], scale=1.0 / 512))
                        ACT.wait(c["e_q"])
                        e_ab = ACT.sig(scalar.activation(out=ab[:], in_=ab_s[:], func=AF.Exp, scale=-0.5))
                        POOL.wait(e_ab, e_sw, fr["obf"])
                        DVE.wait(e_ab, e_sw, fr["obf"])
                        evs_ = []
                        for k in range(8):
                            EE, eng_ = (DVE, vector) if k < 4 else (POOL, gpsimd)
                            ins = eng_.tensor_tensor(out=o_bf[:, k, :], in0=o_un[:, k, :], in1=ab[:, (k // 4) * 128:(k // 4 + 1) * 128], op=ALU.mult)
                            if k % 4 == 3:
                                evs_.append(EE.sig(ins))
                        c["e_ob"] = evs_
                        fr["oun"] = [c["e_sq"]] + evs_

                    def piece1(hf):
                        PE.wait(c["e_ob"], c["e_q"], c.get("e_x0"))
                        for k in range(8):
                            ins = tensor.matmul(X[:, 0:512], lhsT=o_bf[:, k, :], rhs=wo[:, k, hf * 512:(hf + 1) * 512],
                                                start=(k == 0), stop=(k == 7), skip_group_check=True)
                        e_op = PE.sig(ins)
                        DVE.wait(e_op, out_free[s], ev_x)
                        e_add = DVE.sig(vector.tensor_tensor(out=xmid[s][:, hf * 512:(hf + 1) * 512], in0=X[:, 0:512],
                                                             in1=xb[xs][:, hf * 512:(hf + 1) * 512], op=ALU.add))
                        if hf == 0:
                            c["e_x0"] = e_add
                        else:
                            fr["obf"] = e_op
                            c["e_xm"] = e_add
                            fr["X"] = e_add
                            xb_free[xs] = e_add

                    def piece2():
                        ACT.wait(c["e_xm"])
                        e = ACT.sig(scalar.activation(out=junk[:], in_=xmid[s][:], func=AF.Square, accum_out=st[:, 0:1]))
                        ACT.wait(e)
                        e = ACT.sig(scalar.activation(out=st[:, 1:2], in_=st[:, 0:1], func=AF.Ln, bias=epsc[:, 0:1], scale=1.0 / D))
                        ACT.wait(e)
                        e = ACT.sig(scalar.activation(out=st[:, 2:3], in_=st[:, 1:2], func=AF.Exp, scale=-0.5))
                        DVE.wait(e, fr["hm"])
                        c["e_hm"] = DVE.sig(vector.tensor_scalar(out=hm[:], in0=xmid[s][:], scalar1=st[:, 2:3], scalar2=None, op0=ALU.mult))

                    def piece3():
                        PE.wait(c["e_hm"], fr["tp"])
                        for k in range(8):
                            ins = tensor.transpose(out=tp[:, k, :], in_=hm[:, k * 128:(k + 1) * 128], identity=idb[:])
                        e_t = PE.sig(ins)
                        fr["hm"] = e_t
                        ACT.wait(e_t, out_free[s])
                        e_c = ACT.sig(scalar.copy(out=hmT[s][:], in_=tp[:]))
                        fr["tp"] = e_c
                        POOL.wait(c["e_xm"], e_c, c["e_hm"])
                        stb[s].add(gpsimd.dma_start(out=xmid_scr[tok0:tok0 + 128, :], in_=xmid[s][:]))
                        stb[s].add(gpsimd.dma_start(out=hmT_v[:, :, 1 + tok0:1 + tok0 + 128], in_=hmT[s][:]))
                        out_free[s] = stb[s].ev()

                    return [piece0, lambda: piece1(0), lambda: piece1(1), piece2, piece3, piece0a]

                pending = None
                pending3 = None
                for i, lb in enumerate(range(b0, b1)):
                    s = i % 2
                    tok0 = lb * 128
                    w0, nch = cfg.win[l][lb]
                    kc0 = 64 * w0; kc1 = 64 * (w0 + 2 * nch)
                    SP.wait(blk_free[s])
                    L = ldb[s]
                    L.add(sync.dma_start(out=kaT[s][:, :, 0:kc1 - kc0], in_=qk_v[:, 4:8, kc0:kc1]))
                    L.add(sync.dma_start(out=qaT[s][:], in_=qk_v[:, 0:4, tok0:tok0 + 128]))
                    L.add(sync.dma_start(out=qbT[s][:], in_=qk_v[:, 8:12, tok0:tok0 + 128]))
                    L.add(sync.dma_start(out=kbT[s][:], in_=qk_scr[12 * 128:13 * 128, tok0 - 128:tok0 + 256]))
                    L.add(sync.dma_start(out=vaw[s][:, 0:nch, :], in_=v_scr[kc0:kc1, 0:512].rearrange("(c p) f -> p c f", p=128)))
                    ev_ld = L.add(sync.dma_start(out=vbw[s][:], in_=v_scr[tok0 - 128:tok0 + 256, 512:768].rearrange("(c p) f -> p c f", p=128)))
                    xs = i % 3
                    SP.wait(xb_free[xs])
                    ev_x = ldxb[xs].add(sync.dma_start(out=xb[xs][:], in_=xsrc[tok0:tok0 + 128, :]))

                    units = [(e, c) for e in range(2) for c in range(nch)]
                    ev_exp = {}

                    def na_S(u):
                        e, c = units[u]
                        j, S, frb = S_rot.get()
                        PE.wait(frb, ev_ld)
                        s0 = 8 - (w0 - 2 * lb + 2 * c)
                        for pair in range(4):
                            tensor.matmul(S[:, pair * 128:(pair + 1) * 128],
                                          lhsT=kaT[s][64 * e:64 * e + 64, pair, c * 128:(c + 1) * 128],
                                          rhs=qaT[s][64 * e:64 * e + 64, pair, :],
                                          start=(pair == 0), stop=False, skip_group_check=True)
                        ins = tensor.matmul(S[:, 0:512], lhsT=idb[:], rhs=ttb[:, e * 4:(e + 1) * 4, s0 * 64:s0 * 64 + 128],
                                            start=False, stop=True, skip_group_check=True)
                        e_s = PE.sig(ins)
                        ACT.wait(e_s, fr["PTn"])
                        Sv = S[:, 0:512].rearrange("p (h a q) -> p h a q", h=4, a=2)
                        for p in range(2):
                            col = ((l * NLB + lb) * 6 + c) * 2 + p
                            ins = scalar.activation(out=PTn[:, c, e * 4:(e + 1) * 4, p * 64:(p + 1) * 64], in_=Sv[:, :, p, :],
                                                    func=AF.Exp, bias=mbna[:, col:col + 1])
                        e_x = ACT.sig(ins)
                        S_rot.rel(j, e_x)
                        ev_exp[u] = e_x

                    def na_PV(u):
                        e, c = units[u]
                        PE.wait(ev_exp[u], fr["PD"])
                        for pair in range(4):
                            bank = P[pair // 2]
                            col0 = (pair % 2) * 256 + e * 128
                            st_flag = (e == 0 and c == 0 and pair % 2 == 0)
                            tensor.matmul(bank[:, col0:col0 + 128], lhsT=vaw[s][:, c, pair * 128:(pair + 1) * 128],
                                          rhs=PTn[:, c, e * 4 + pair, :], start=st_flag, stop=(c == nch - 1), skip_group_check=True)
                        return tensor.matmul(Dk[e][:, 0:512], lhsT=ones[:], rhs=PTn[:, c, e * 4:(e + 1) * 4, :],
                                             start=(c == 0), stop=(c == nch - 1), skip_group_check=True)

                    nu = len(units)
                    LAG = 4
                    for u in range(nu):
                        na_S(u)
                        if u >= LAG:
                            na_PV(u - LAG)
                        if pending3 is not None and u == 1:
                            pending3[0]()
                        if pending3 is not None and u == 5:
                            pending3[1]()
                            pending3 = None
                        if pending is not None and u == 6:
                            pending[5]()
                        if pending is not None and u == 8:
                            pending[0]()
                    for u in range(nu - LAG, nu):
                        ins = na_PV(u)
                    e_pvn = PE.sig(ins)
                    fr["PTn"] = e_pvn
                    if pending is not None:
                        pending[1]()
                    ACT.wait(e_pvn, fr["rawn"])
                    scalar.copy(out=rawPn[:, 0, :], in_=P[0][:, 0:512])
                    scalar.copy(out=rawDn[:, 0, :], in_=Dk[0][:, 0:512])
                    scalar.copy(out=rawPn[:, 1, :], in_=P[1][:, 0:512])
                    e_ca = ACT.sig(scalar.copy(out=rawDn[:, 1, :], in_=Dk[1][:, 0:512]))
                    e_cd = e_ca
                    fr["PD"] = [e_ca]
                    DVE.wait(e_ca, e_cd)
                    e_r = DVE.sig(vector.reciprocal(out=rec_n[:].rearrange("p h q -> p (h q)"), in_=rawDn[:].rearrange("p e q -> p (e q)")))
                    POOL.wait(e_r, e_ca, e_cd, fr["oun"])
                    DVE.wait(e_r, fr["oun"])
                    evs_ = []
                    for e in range(2):
                        EE, eng_ = (DVE, vector) if e == 0 else (POOL, gpsimd)
                        for bk in range(2):
                            src = rawPn[64 * e:64 * e + 64, bk, :].rearrange("p (m t q) -> p m t q", m=2, t=2)[:, :, e, :]
                            ins = eng_.tensor_tensor(out=o_un[64 * e:64 * e + 64, 2 * bk:2 * bk + 2, :], in0=src,
                                                     in1=rec_n[64 * e:64 * e + 64, e * 4 + 2 * bk:e * 4 + 2 * bk + 2, :], op=ALU.mult)
                        evs_.append(EE.sig(ins))
                    e_na = evs_
                    fr["rawn"] = e_na

                    sunits = [(g, c) for g in range(2) for c in range(3)]
                    sw_exp = {}

                    def sw_S(u):
                        g, c = sunits[u]
                        j, S, frb = S_rot.get()
                        PE.wait(frb, ev_ld)
                        tensor.matmul(S[:, 0:512], lhsT=kbT[s][64 * g:64 * g + 64, c * 128:(c + 1) * 128],
                                      rhs=qbT[s][64 * g:64 * g + 64, :, :], start=True, stop=False, skip_group_check=True)
                        ins = tensor.matmul(S[:, 0:512], lhsT=idb[:], rhs=bswb[:, c, g * 4:(g + 1) * 4, :], start=False, stop=True, skip_group_check=True)
                        e_s = PE.sig(ins)
                        ACT.wait(e_s, fr["PTs"])
                        col = lb * 3 + c
                        e_x = ACT.sig(scalar.activation(out=PTs[:, c, g * 4:(g + 1) * 4, :], in_=S[:, 0:512].rearrange("p (h q) -> p h q", h=4),
                                                        func=AF.Exp, bias=mbsw[:, col:col + 1]))
                        S_rot.rel(j, e_x)
                        sw_exp[u] = e_x

                    def sw_PV(u):
                        g, c = sunits[u]
                        PE.wait(sw_exp[u], fr["PD"])
                        tensor.matmul(P[g][:, 0:512], lhsT=vbw[s][:, c, g * 128:(g + 1) * 128], rhs=PTs[:, c, g * 4:(g + 1) * 4, :],
                                      start=(c == 0), stop=(c == 2), skip_group_check=True)
                        ins = tensor.matmul(Dk[g][:, 0:512], lhsT=ones[:], rhs=PTs[:, c, g * 4:(g + 1) * 4, :],
                                            start=(c == 0), stop=False, skip_group_check=True)
                        if c == 2:
                            o0 = (l * 8 + 4 * g) * 128
                            ins = tensor.matmul(Dk[g][:, 0:512], lhsT=ones[0:1, :], rhs=esrow[0:1, o0:o0 + 512],
                                                start=False, stop=True, skip_group_check=True)
                        return ins

                    for u in range(6):
                        sw_S(u)
                        if pending is not None and u == 1:
                            pending[2]()
                    for u in range(6):
                        ins = sw_PV(u)
                    e_pvs = PE.sig(ins)
                    fr["PTs"] = e_pvs
                    ACT.wait(e_pvs, fr["raws"])
                    scalar.copy(out=rawPs[:, 0, :], in_=P[0][:, 0:512])
                    e_ca = ACT.sig(scalar.copy(out=rawDs[:, 0, :], in_=Dk[0][:, 0:512]))
                    DVE.wait(e_pvs, fr["raws"])
                    vector.tensor_copy(out=rawPs[:, 1, :], in_=P[1][:, 0:512])
                    e_cd = DVE.sig(vector.tensor_copy(out=rawDs[:, 1, :], in_=Dk[1][:, 0:512]))
                    pd_sw = [e_ca, e_cd]
                    DVE.wait(e_ca, e_cd)
                    e_r = DVE.sig(vector.reciprocal(out=rec_s[:].rearrange("p h q -> p (h q)"), in_=rawDs[:].rearrange("p g q -> p (g q)")))
                    POOL.wait(e_r, e_ca, e_cd, e_na)
                    DVE.wait(e_r, e_na)
                    rsv = rec_s[:].rearrange("p (g m t) q -> p g m t q", g=2, m=2)
                    evs_ = []
                    for hf in range(2):
                        EE, eng_ = (DVE, vector) if hf == 0 else (POOL, gpsimd)
                        for g in range(2):
                            src = rawPs[64 * hf:64 * hf + 64, g, :].rearrange("p (m t q) -> p m t q", m=2, t=2)[:, :, hf, :]
                            ins = eng_.tensor_tensor(out=o_un[64 * hf:64 * hf + 64, 4 + 2 * g:6 + 2 * g, :], in0=src,
                                                     in1=rsv[64 * hf:64 * hf + 64, g, :, hf, :], op=ALU.mult)
                        evs_.append(EE.sig(ins))
                    e_sw = evs_
                    fr["raws"] = e_sw
                    fr["PD"] = pd_sw
                    blk_free[s] = e_pvs
                    if pending is not None:
                        pending3 = (pending[3], pending[4])
                    pending = make_out_pieces(i, lb, s, tok0, e_sw, e_pvs, ev_x, xs)
                if pending3 is not None:
                    pending3[0]()
                    pending3[1]()
                pending[5]()
                for pc in pending[:5]:
                    pc()
                phase_end([stb[0].ev(), stb[1].ev()])

        def phase_D(l):
            tiles = cfg.tiles[l]
            phase_begin()
            with ExitStack() as es:
                wu = sb(es, "d_wu", [128, 8, 2 * DFF], BF16)
                wd = sb(es, "d_wd", [128, 22, D], BF16)
                with ExitStack() as es2:
                    NSTG = 4
                    stg = [sb(es2, f"d_stg{i}", [128, 2816], F32) for i in range(NSTG)]
                    stg_free = [None] * NSTG
                    ldw = [ds["w0"], ds["w1"], ds["w2"], ds["w3"]]
                    jobs = [("u", k, q) for k in range(8) for q in range(2)] + [("d", k, 0) for k in range(11)]
                    for n, (kind, k, q) in enumerate(jobs):
                        s = n % NSTG
                        EE = DVE if n % 2 == 0 else ACT
                        SP.wait(stg_free[s])
                        if kind == "u":
                            ev = ldw[s].add(sync.dma_start(out=stg[s][:, 0:2816], in_=w_up_d[l, k * 128:(k + 1) * 128, q * 2816:(q + 1) * 2816]))
                            EE.wait(ev)
                            gk = gc[:, G[f"ffn{l}"] + k:G[f"ffn{l}"] + k + 1]
                            if EE is DVE:
                                ins = vector.tensor_scalar(out=wu[:, k, q * 2816:(q + 1) * 2816], in0=stg[s][:, 0:2816], scalar1=gk, scalar2=None, op0=ALU.mult)
                            else:
                                ins = scalar.activation(out=wu[:, k, q * 2816:(q + 1) * 2816], in_=stg[s][:, 0:2816], func=AF.Identity, scale=gk)
                        else:
                            ev = ldw[s].add(sync.dma_start(out=stg[s][:, 0:2048].rearrange("p (c n) -> p c n", c=2),
                                                           in_=w_down_d[l, k * 256:(k + 1) * 256, :].rearrange("(c p) n -> p c n", p=128)))
                            EE.wait(ev)
                            dst = wd[:, 2 * k:2 * k + 2, :].rearrange("p c n -> p (c n)")
                            if EE is DVE:
                                ins = vector.tensor_copy(out=dst, in_=stg[s][:, 0:2048])
                            else:
                                ins = scalar.copy(out=dst, in_=stg[s][:, 0:2048])
                        stg_free[s] = EE.sig(ins)
                    wready = list(stg_free)
                    for E in ENGS:
                        E.wait(wready)
                hTt = [sb(es, f"d_hT{i}", [128, 8, 386], BF16) for i in range(2)]
                actT = sb(es, "d_act", [128, 22, 384], BF16)
                tg = [sb(es, f"d_tg{i}", [128, 384], F32) for i in range(2)]
                tv = [sb(es, f"d_tv{i}", [128, 384], F32) for i in range(2)]
                xm = [sb(es, f"d_xm{i}", [128, D], F32) for i in range(3)]
                xo = [sb(es, f"d_xo{i}", [128, D], F32) for i in range(2)]
                junk = sb(es, "d_junk", [128, D], BF16)
                st = sb(es, "d_st", [128, 4], F32)
                gfin = sb(es, "d_gfin", [128, D], F32) if l == 1 else None
                U = [ps(es, f"d_U{i}", [128, 512], F32) for i in range(4)]
                U_free = [None] * 4
                Y_rot = Rot([ps(es, f"d_Y{i}", [128, 512], F32) for i in range(4)])
                ev_gf = None
                if l == 1:
                    ds["c"].add(sync.dma_start(out=gfin[:], in_=gfin_d[:, :]))
                    ev_gf = ds["c"].ev()

                ldh = [ds["h0"], ds["h1"]]
                ldm = [ds["m0"], ds["m1"], ds["m2"]]
                sto = [ds["o0"], ds["o1"]]
                hTt_free = [None, None]
                xm_free = [None] * 3
                xo_free = [None, None]
                t_free = [None, None]
                xmc = 0; xoc = 0
                for t, (tb0, nb) in enumerate(tiles):
                    s = t % 2
                    T = 128 * nb; tok0 = tb0 * 128
                    SP.wait(hTt_free[s])
                    ev_h = ldh[s].add(sync.dma_start(out=hTt[s][:, :, 0:T + 2], in_=hmT_v[:, :, tok0:tok0 + T + 2]))
                    ev_fix = None
                    if tb0 in cfg.PB:
                        j = cfg.PB.index(tb0)
                        DVE.wait(ev_h)
                        ev_fix = DVE.sig(vector.tensor_scalar(out=hTt[s][:, :, 0:1], in0=hTt[s][:, :, 0:1],
                                                              scalar1=gc[:, G["keep"] + j:G["keep"] + j + 1], scalar2=None, op0=ALU.mult))
                    if (tb0 + nb) in cfg.PB:
                        j = cfg.PB.index(tb0 + nb)
                        DVE.wait(ev_h)
                        ev_fix = DVE.sig(vector.tensor_scalar(out=hTt[s][:, :, T + 1:T + 2], in0=hTt[s][:, :, T + 1:T + 2],
                                                              scalar1=gc[:, G["keep"] + j:G["keep"] + j + 1], scalar2=None, op0=ALU.mult))
                    xm_ev = []
                    xm_slot = []
                    for sbk in range(nb):
                        xs = xmc % 3; xmc += 1
                        SP.wait(xm_free[xs])
                        xm_ev.append(ldm[xs].add(sync.dma_start(out=xm[xs][:], in_=xmid_scr[tok0 + sbk * 128:tok0 + (sbk + 1) * 128, :])))
                        xm_slot.append(xs)
                    e_a = None
                    e_acts = []
                    for jj in range(22):
                        pr = jj % 2
                        e3 = {}
                        for wh, colbase, bi_, tt_ in (("g", jj * 128, 2 * pr, tg[pr]), ("v", DFF + jj * 128, 2 * pr + 1, tv[pr])):
                            bank = U[bi_]
                            PE.wait(U_free[bi_], ev_h, ev_fix)
                            for k in range(8):
                                ins = tensor.matmul(bank[:, 0:T + 2], lhsT=wu[:, k, colbase:colbase + 128], rhs=hTt[s][:, k, 0:T + 2],
                                                    start=(k == 0), stop=(k == 7))
                            e_u = PE.sig(ins)
                            m = colbase // 128
                            ACT.wait(e_u, t_free[pr])
                            e1 = ACT.sig(scalar.activation(out=tt_[:, 0:T], in_=bank[:, 1:T + 1], func=AF.Identity,
                                                           bias=gc[:, G[f"cb{l}"] + m:G[f"cb{l}"] + m + 1],
                                                           scale=gc[:, G[f"cw1{l}"] + m:G[f"cw1{l}"] + m + 1]))
                            DVE.wait(e1)
                            e2 = DVE.sig(vector.scalar_tensor_tensor(out=tt_[:, 0:T], in0=bank[:, 0:T], scalar=gc[:, G[f"cw0{l}"] + m:G[f"cw0{l}"] + m + 1],
                                                                     in1=tt_[:, 0:T], op0=ALU.mult, op1=ALU.add))
                            DVE.wait(e2)
                            e3[wh] = DVE.sig(vector.scalar_tensor_tensor(out=tt_[:, 0:T], in0=bank[:, 2:T + 2], scalar=gc[:, G[f"cw2{l}"] + m:G[f"cw2{l}"] + m + 1],
                                                                         in1=tt_[:, 0:T], op0=ALU.mult, op1=ALU.add))
                            U_free[bi_] = e3[wh]
                        ACT.wait(e3["g"])
                        e_s = ACT.sig(scalar.activation(out=tg[pr][:, 0:T], in_=tg[pr][:, 0:T], func=AF.Silu))
                        POOL.wait(e_s, e3["v"])
                        e_a = POOL.sig(gpsimd.tensor_tensor(out=actT[:, jj, 0:T], in0=tg[pr][:, 0:T], in1=tv[pr][:, 0:T], op=ALU.mult))
                        t_free[pr] = e_a
                        e_acts.append(e_a)
                    hTt_free[s] = e_u
                    for sbk in range(nb):
                        os_ = xoc % 2; xoc += 1
                        xs = xm_slot[sbk]
                        tok = tok0 + sbk * 128
                        for hf in range(2):
                            j, Y, fr = Y_rot.get()
                            PE.wait(fr)
                            for jj in range(22):
                                PE.wait(e_acts[jj])
                                ins = tensor.matmul(Y[:, 0:512], lhsT=actT[:, jj, sbk * 128:(sbk + 1) * 128], rhs=wd[:, jj, hf * 512:(hf + 1) * 512],
                                                    start=(jj == 0), stop=(jj == 21))
                            e_y = PE.sig(ins)
                            DVE.wait(e_y, xm_ev[sbk], xo_free[os_])
                            e_o = DVE.sig(vector.tensor_tensor(out=xo[os_][:, hf * 512:(hf + 1) * 512], in0=Y[:, 0:512],
                                                               in1=xm[xs][:, hf * 512:(hf + 1) * 512], op=ALU.add))
                            Y_rot.rel(j, e_o)
                        xm_free[xs] = e_o
                        if l == 0:
                            POOL.wait(e_o)
                            sto[os_].add(gpsimd.dma_start(out=x1_scr[tok:tok + 128, :], in_=xo[os_][:]))
                        else:
                            ACT.wait(e_o)
                            e = ACT.sig(scalar.activation(out=junk[:], in_=xo[os_][:], func=AF.Square, accum_out=st[:, 0:1]))
                            ACT.wait(e)
                            e = ACT.sig(scalar.activation(out=st[:, 1:2], in_=st[:, 0:1], func=AF.Sqrt, bias=EPS, scale=1.0 / D))
                            DVE.wait(e)
                            e = DVE.sig(vector.reciprocal(out=st[:, 2:3], in_=st[:, 1:2]))
                            DVE.wait(e)
                            e = DVE.sig(vector.tensor_scalar(out=xo[os_][:], in0=xo[os_][:], scalar1=st[:, 2:3], scalar2=None, op0=ALU.mult))
                            POOL.wait(e, ev_gf)
                            e = POOL.sig(gpsimd.tensor_tensor(out=xo[os_][:], in0=xo[os_][:], in1=gfin[:], op=ALU.mult))
                            POOL.wait(e)
                            yt = tok - HALO * 128
                            sto[os_].add(gpsimd.dma_start(out=y_d[yt:yt + 128, :], in_=xo[os_][:]))
                        xo_free[os_] = sto[os_].ev()
                phase_end([sto[0].ev(), sto[1].ev()])

        phase_A(0, xin)
        phase_BC(0, xin)
        phase_D(0)
        phase_A(1, x1_scr)
        phase_BC(1, x1_scr)
        phase_D(1)
        for E in ENGS:
            E.wait(state["phase_ev"])
    return nc


def run_cfg(cfg, xcat, params):
    nc = build_program(cfg)
    maps = host_inputs(cfg, xcat, params)
    res = run_bass_kernel_spmd(nc, maps, core_ids=list(range(cfg.NC)))
    return np.concatenate([np.asarray(r["y"]) for r in res.results], axis=0)


def kernel(x_prompt, x_sample, norm_mix, w_in, rpb, sinks, norm_grp, w_out, norm_ffn, w_up, conv_w, conv_b, w_down, norm_final):
    xp = np.asarray(x_prompt, np.float32)
    xs = np.asarray(x_sample, np.float32)
    T = xp.shape[1]
    xcat = np.concatenate([xp.reshape(-1, D), xs.reshape(-1, D)], axis=0)
    nseq = xp.shape[0] + xs.shape[0]
    cfg = Cfg(8, nseq, T // 128)
    params = dict(norm_mix=np.asarray(norm_mix), w_in=np.asarray(w_in), rpb=np.asarray(rpb), sinks=np.asarray(sinks),
                  norm_grp=np.asarray(norm_grp), w_out=np.asarray(w_out), norm_ffn=np.asarray(norm_ffn), w_up=np.asarray(w_up),
                  conv_w=np.asarray(conv_w), conv_b=np.asarray(conv_b), w_down=np.asarray(w_down), norm_final=np.asarray(norm_final))
    y = run_cfg(cfg, xcat, params)
    y = y.reshape(nseq, T, D)
    return (np.ascontiguousarray(y[:xp.shape[0]]), np.ascontiguousarray(y[xp.shape[0]:]))
```

```python
import numpy as np
from contextlib import ExitStack
import concourse.bass as bass
import concourse.mybir as mybir
from concourse.bass_utils import run_bass_kernel_spmd

F32 = mybir.dt.float32
BF16 = mybir.dt.bfloat16
AF = mybir.ActivationFunctionType
ALU = mybir.AluOpType
D = 1024
DFF = 2816
EPS = 1e-6
NEG = -30000.0
HALO = 6


class Cfg:
    def __init__(self, ncores, nseq, seq_blocks):
        self.NC, self.NSEQ, self.SB = ncores, nseq, seq_blocks
        self.TOTB = nseq * seq_blocks
        assert self.TOTB % ncores == 0
        self.BPC = self.TOTB // ncores
        self.NLB = self.BPC + 2 * HALO
        self.NTOK = self.NLB * 128
        pb = set()
        for c in range(ncores):
            for s in range(nseq + 1):
                o = s * seq_blocks - c * self.BPC
                if 0 <= o <= self.BPC:
                    pb.add(o + HALO)
        self.PB = sorted(pb)
        N = self.NLB
        self.rA = [(0, N), (3, N - 3)]
        self.rBC = [(2, N - 2), (5, N - 5)]
        self.rD = [(3, N - 3), (6, N - 6)]
        self.win = []
        for l in range(2):
            a0, a1 = self.rA[l]
            w = {}
            for lb in range(*self.rBC[l]):
                r0 = 2 * lb
                cand = (r0 - 4, 5)
                if (lb + 1) in self.PB:
                    cand = (r0 - 6, 6)
                elif lb in self.PB:
                    cand = (r0 - 4, 6)
                if cand[0] < 2 * a0 or cand[0] + 2 * cand[1] > 2 * a1:
                    cand = (r0 - 4, 5)
                assert cand[0] >= 2 * a0 and cand[0] + 2 * cand[1] <= 2 * a1
                w[lb] = cand
            self.win.append(w)
        self.tiles = []
        for l in range(2):
            d0, d1 = self.rD[l]
            cuts = sorted({d0, d1} | {p for p in self.PB if d0 < p < d1})
            tl = []
            for s0, s1 in zip(cuts[:-1], cuts[1:]):
                n = s1 - s0
                sizes = []
                while n > 0:
                    if n % 3 == 0 or n >= 5:
                        sizes.append(3); n -= 3
                    elif n >= 2:
                        sizes.append(2); n -= 2
                    else:
                        sizes.append(1); n -= 1
                b = s0
                for sz in sizes:
                    tl.append((b, sz)); b += sz
            self.tiles.append(tl)
        self.goff = {}
        o = 0
        for l in range(2):
            for nm, w in (("mix", 8), ("grp", 8), ("ffn", 8), ("cw0", 44), ("cw1", 44), ("cw2", 44), ("cb", 44)):
                self.goff[f"{nm}{l}"] = o; o += w
        self.goff["sink"] = o; o += 16
        self.goff["keep"] = o; o += len(self.PB)
        self.NG = o


def _vcol(v, n):
    return np.ascontiguousarray(np.asarray(v, np.float32).reshape(n, 128).T)


def build_tt(rpb_l):
    rpb_l = np.asarray(rpb_l, np.float32)
    tt = np.full((2, 64, 8, 16, 64), NEG, np.float32)
    qc = np.arange(64)
    cs = np.clip(qc - 8, 0, 48)
    kc = np.arange(64)
    valid = (kc[None, :] >= cs[:, None]) & (kc[None, :] < cs[:, None] + 16)
    dcc = np.clip(kc[None, :] - qc[:, None] + 15, 0, 30)
    for a in range(2):
        for s in range(16):
            off = 8 - s + a
            if abs(off) > 7:
                continue
            for hidx in range(8):
                e, pair = hidx // 4, hidx % 4
                h = 2 * pair + e
                slab = np.where(valid, rpb_l[h, off + 7][dcc], np.float32(NEG))
                tt[a, :, hidx, s, :] = slab.T
    return tt.reshape(128, 8, 1024)


def build_bsw():
    s = np.arange(128)[:, None, None, None]
    c = np.arange(3)[None, :, None, None]
    h = np.arange(8)[None, None, :, None]
    t = np.arange(128)[None, None, None, :]
    dist = np.abs(t - (s + 128 * (c - 1)))
    slope = np.exp2(-(h + 1.0))
    val = -(slope * dist)
    return np.where(dist <= 128, val, NEG).astype(np.float32)


def na_masks(cfg, core):
    NLB = cfg.NLB
    m = np.full((2, NLB, 6, 2, 2), NEG, np.float32)
    grow0 = 2 * (core * cfg.BPC - HALO)
    RPS = 2 * cfg.SB
    for l in range(2):
        for lb, (w0, nch) in cfg.win[l].items():
            for p in range(2):
                gq = grow0 + 2 * lb + p
                real = 0 <= gq < 2 * cfg.TOTB
                oks = np.zeros((6, 2), bool)
                for c in range(nch):
                    for a in range(2):
                        gk = grow0 + w0 + 2 * c + a
                        if real:
                            sq = gq // RPS
                            rs = int(np.clip(gq % RPS - 4, 0, RPS - 8)) + sq * RPS
                            oks[c, a] = rs <= gk < rs + 8
                        else:
                            oks[c, a] = -4 <= gk - gq <= 3
                if not oks.any():
                    for c in range(nch):
                        for a in range(2):
                            gk = grow0 + w0 + 2 * c + a
                            oks[c, a] = -4 <= gk - gq <= 3
                m[l, lb, :, p, :] = np.where(oks, 0.0, NEG)
    mm = np.transpose(m, (4, 0, 1, 2, 3)).reshape(2, -1)
    return np.ascontiguousarray(np.repeat(mm, 64, axis=0)).astype(np.float32)


def sw_masks(cfg, core):
    NLB = cfg.NLB
    m = np.zeros((NLB, 3), np.float32)
    for lb in range(NLB):
        gb = core * cfg.BPC - HALO + lb
        real = 0 <= gb < cfg.TOTB
        for c in range(3):
            gk = gb + c - 1
            if real:
                ok = (0 <= gk < cfg.TOTB) and (gk // cfg.SB == gb // cfg.SB)
            else:
                ok = True
            m[lb, c] = 0.0 if ok else NEG
    return np.ascontiguousarray(np.broadcast_to(m.reshape(1, -1), (128, NLB * 3))).astype(np.float32)


def host_inputs(cfg, xcat, p):
    shared = {
        "w_in": np.ascontiguousarray(p["w_in"], np.float32),
        "w_out": np.ascontiguousarray(p["w_out"], np.float32),
        "w_up": np.ascontiguousarray(p["w_up"], np.float32),
        "w_down": np.ascontiguousarray(p["w_down"], np.float32),
        "tt": np.stack([build_tt(p["rpb"][l]) for l in range(2)]),
        "bsw": build_bsw(),
        "ident": np.eye(128, dtype=np.float32),
        "gfin": np.ascontiguousarray(np.broadcast_to(np.asarray(p["norm_final"], np.float32)[None, :], (128, D))),
    }
    maps = []
    for c in range(cfg.NC):
        g = np.zeros((128, cfg.NG), np.float32)
        for l in range(2):
            g[:, cfg.goff[f"mix{l}"]:][:, :8] = _vcol(p["norm_mix"][l], 8)
            g[:, cfg.goff[f"grp{l}"]:][:, :8] = _vcol(p["norm_grp"][l], 8)
            g[:, cfg.goff[f"ffn{l}"]:][:, :8] = _vcol(p["norm_ffn"][l], 8)
            for j in range(3):
                g[:, cfg.goff[f"cw{j}{l}"]:][:, :44] = _vcol(p["conv_w"][l][j], 44)
            g[:, cfg.goff[f"cb{l}"]:][:, :44] = _vcol(p["conv_b"][l], 44)
        g[:, cfg.goff["sink"]:][:, :16] = np.asarray(p["sinks"], np.float32).reshape(1, 16)
        for j, pb in enumerate(cfg.PB):
            gb = c * cfg.BPC - HALO + pb
            realb = (0 <= gb <= cfg.TOTB) and (gb % cfg.SB == 0)
            g[:, cfg.goff["keep"] + j] = 0.0 if realb else 1.0
        xin = np.zeros((cfg.NTOK, D), np.float32)
        t0 = (c * cfg.BPC - HALO) * 128
        lo, hi = max(t0, 0), min(t0 + cfg.NTOK, xcat.shape[0])
        xin[lo - t0:hi - t0] = xcat[lo:hi]
        d = dict(shared)
        d.update({"xin": xin, "gcols": g, "mbna": na_masks(cfg, c), "mbsw": sw_masks(cfg, c)})
        maps.append(d)
    return maps


class Eng:
    def __init__(self, nc, eng, name):
        self.e = eng
        self.sem = nc.alloc_semaphore("pg_" + name)
        self.n = 0
        self.seen = {}

    def wait(self, *evs):
        for ev in evs:
            if ev is None:
                continue
            if isinstance(ev, list):
                self.wait(*ev)
                continue
            sem, v = ev
            k = id(sem)
            if self.seen.get(k, 0) >= v:
                continue
            self.seen[k] = v
            self.e.wait_ge(sem, v)

    def sig(self, ins):
        self.n += 1
        ins.then_inc(self.sem, 1)
        return (self.sem, self.n)


class DSem:
    def __init__(self, nc, name):
        self.sem = nc.alloc_semaphore("dm_" + name)
        self.n = 0

    def add(self, ins):
        ins.then_inc(self.sem, 16)
        self.n += 16
        return (self.sem, self.n)

    def ev(self):
        return (self.sem, self.n) if self.n else None


class Rot:
    def __init__(self, banks):
        self.b = banks
        self.free = [None] * len(banks)
        self.i = 0

    def get(self):
        j = self.i % len(self.b)
        self.i += 1
        return j, self.b[j], self.free[j]

    def rel(self, j, ev):
        self.free[j] = ev


def build_program(cfg):
    nc = bass.Bass("TRN2", target_bir_lowering=False)
    NLB, NTOK, BPC = cfg.NLB, cfg.NTOK, cfg.BPC
    G = cfg.goff

    def din(name, shape):
        return nc.dram_tensor(name, list(shape), F32, kind="ExternalInput").ap()

    xin = din("xin", [NTOK, D])
    gcols_d = din("gcols", [128, cfg.NG])
    gfin_d = din("gfin", [128, D])
    tt_d = din("tt", [2, 128, 8, 1024])
    bsw_d = din("bsw", [128, 3, 8, 128])
    mbna_d = din("mbna", [128, 2 * NLB * 12])
    mbsw_d = din("mbsw", [128, NLB * 3])
    ident_d = din("ident", [128, 128])
    w_in_d = din("w_in", [2, D, 2304])
    w_out_d = din("w_out", [2, D, D])
    w_up_d = din("w_up", [2, D, 2 * DFF])
    w_down_d = din("w_down", [2, DFF, D])
    y_d = nc.dram_tensor("y", [BPC * 128, D], F32, kind="ExternalOutput").ap()

    qk_scr = nc.dram_tensor("qk_scr", [13 * 128, NTOK], BF16).ap()
    v_scr = nc.dram_tensor("v_scr", [NTOK, 768], BF16).ap()
    xmid_scr = nc.dram_tensor("xmid_scr", [NTOK, D], F32).ap()
    hmT_scr = nc.dram_tensor("hmT_scr", [D, NTOK + 2], BF16).ap()
    x1_scr = nc.dram_tensor("x1_scr", [NTOK, D], F32).ap()

    PE = Eng(nc, nc.tensor, "pe")
    ACT = Eng(nc, nc.scalar, "act")
    DVE = Eng(nc, nc.vector, "dve")
    POOL = Eng(nc, nc.gpsimd, "pool")
    SP = Eng(nc, nc.sync, "sp")
    ENGS = [PE, ACT, DVE, POOL, SP]
    sync, scalar, vector, gpsimd, tensor = nc.sync, nc.scalar, nc.vector, nc.gpsimd, nc.tensor

    ds = {n: DSem(nc, n) for n in ["c", "w0", "w1", "w2", "w3", "x0", "x1", "q0", "q1", "b0", "b1", "s0", "s1",
                                   "h0", "h1", "m0", "m1", "m2", "o0", "o1"]}
    qk_v = qk_scr.rearrange("(c p) t -> p c t", p=128)
    hmT_v = hmT_scr.rearrange("(k p) t -> p k t", p=128)

    with ExitStack() as top:
        uid = [0]

        def sb(es, name, shape, dty):
            uid[0] += 1
            return es.enter_context(nc.sbuf_tensor(f"{name}_u{uid[0]}", list(shape), dty))

        def ps(es, name, shape, dty):
            uid[0] += 1
            return es.enter_context(nc.psum_tensor(f"{name}_u{uid[0]}", list(shape), dty))

        gc = sb(top, "gc", [128, cfg.NG], F32)
        mbna = sb(top, "mbna_sb", [128, 2 * NLB * 12], F32)
        mbsw = sb(top, "mbsw_sb", [128, NLB * 3], F32)
        idf = sb(top, "idf", [128, 128], F32)
        idb = sb(top, "idb", [128, 128], BF16)
        ones = sb(top, "ones", [128, 128], BF16)
        esink = sb(top, "esink", [128, 16], F32)
        dummy = sb(top, "k_dummy", [128, 2], F32)

        sync.dma_start(out=gc[:], in_=gcols_d[:, :]).then_inc(ds["c"].sem, 16)
        sync.dma_start(out=mbna[:], in_=mbna_d[:, :]).then_inc(ds["c"].sem, 16)
        sync.dma_start(out=mbsw[:], in_=mbsw_d[:, :]).then_inc(ds["c"].sem, 16)
        sync.dma_start(out=idf[:], in_=ident_d[:, :]).then_inc(ds["c"].sem, 16)
        ds["c"].n = 64
        ev_c = ds["c"].ev()
        DVE.wait(ev_c)
        vector.tensor_copy(out=idb[:], in_=idf[:])
        ev_const = DVE.sig(vector.memset(ones[:], 1.0))
        ACT.wait(ev_c)
        ev_sink = ACT.sig(scalar.activation(out=esink[:], in_=gc[:, G["sink"]:G["sink"] + 16], func=AF.Exp))
        esrow = sb(top, "esrow", [1, 2048], BF16)
        epsc = sb(top, "epsc", [128, 2], F32)
        vector.memset(epsc[:], EPS)
        DVE.wait(ev_sink, ev_const)
        for hh in range(16):
            ins = vector.tensor_scalar(out=esrow[0:1, hh * 128:(hh + 1) * 128], in0=ones[0:1, :], scalar1=esink[0:1, hh:hh + 1], scalar2=None, op0=ALU.mult)
        ev_esrow = DVE.sig(ins)
        for E in ENGS:
            E.wait(ev_c, ev_const, ev_sink, ev_esrow)

        state = {"phase_ev": None}

        def phase_begin():
            for E in ENGS:
                E.wait(state["phase_ev"])

        def phase_end(store_evs):
            POOL.wait(*store_evs)
            state["phase_ev"] = POOL.sig(gpsimd.memset(dummy[:], 0.0))

        def phase_A(l, xsrc):
            a0, a1 = cfg.rA[l]
            groups = [list(range(b, min(b + 4, a1))) for b in range(a0, a1, 4)]
            phase_begin()
            with ExitStack() as es:
                wres = sb(es, "a_wres", [128, 8, 2432], BF16)
                stg = [sb(es, f"a_stg{i}", [128, 2304], F32) for i in range(2)]
                xg = [sb(es, f"a_xg{i}", [128, 4, D], F32) for i in range(2)]
                hb = [sb(es, f"a_hb{i}", [128, D], BF16) for i in range(2)]
                hT = [sb(es, f"a_hT{i}", [128, 8, 512], BF16) for i in range(2)]
                qko = [sb(es, f"a_qko{i}", [128, 13, 512], BF16) for i in range(2)]
                vo = [sb(es, f"a_vo{i}", [128, 4, 768], BF16) for i in range(2)]
                junk = sb(es, "a_junk", [128, D], BF16)
                st = sb(es, "a_st", [128, 12], F32)
                tp = ps(es, "a_tp", [128, 8, 128], BF16)
                rot = Rot([ps(es, f"a_pm{i}", [128, 512], F32) for i in range(6)])

                stg_free = [None, None]
                ldw = [ds["w0"], ds["w1"]]
                for k in range(8):
                    s = k % 2
                    SP.wait(stg_free[s])
                    ev = ldw[s].add(sync.dma_start(out=stg[s][:], in_=w_in_d[l, k * 128:(k + 1) * 128, :]))
                    DVE.wait(ev)
                    gk = gc[:, G[f"mix{l}"] + k:G[f"mix{l}"] + k + 1]
                    src, dst = stg[s], wres
                    vector.tensor_scalar(out=dst[:, k, 0:512], in0=src[:, 0:512], scalar1=gk, scalar2=0.125, op0=ALU.mult, op1=ALU.mult)
                    vector.tensor_scalar(out=dst[:, k, 512:1024], in0=src[:, 512:1024], scalar1=gk, scalar2=None, op0=ALU.mult)
                    vector.tensor_scalar(out=dst[:, k, 1024:1536].rearrange("p (j a c) -> p j a c", j=4, a=2),
                                         in0=src[:, 1536:2048].rearrange("p (a j c) -> p j a c", a=2, j=4),
                                         scalar1=gk, scalar2=0.125, op0=ALU.mult, op1=ALU.mult)
                    vector.tensor_scalar(out=dst[:, k, 1536:1664], in0=src[:, 2048:2176], scalar1=gk, scalar2=None, op0=ALU.mult)
                    vector.tensor_scalar(out=dst[:, k, 1664:2176], in0=src[:, 1024:1536], scalar1=gk, scalar2=None, op0=ALU.mult)
                    dv = dst[:, k, 2176:2432].rearrange("p (g d c) -> p g d c", g=2, d=2)
                    sv = src[:, 2176:2304].rearrange("p (g c) -> p g c", g=2)
                    vector.tensor_scalar(out=dv[:, :, 0, :], in0=sv, scalar1=gk, scalar2=None, op0=ALU.mult)
                    ins = vector.tensor_scalar(out=dv[:, :, 1, :], in0=sv, scalar1=gk, scalar2=None, op0=ALU.mult)
                    stg_free[s] = DVE.sig(ins)
                wready = stg_free[1]
                PE.wait(wready)

                ldx = [ds["x0"], ds["x1"]]
                stq = [ds["q0"], ds["q1"]]
                xg_free = [None, None]; hb_free = [None, None]; hT_free = [None, None]
                out_free = [None, None]
                stat_free = [None] * 4
                store_evs = []
                A = {"bcount": 0, "tp_free": None}
                def stage1_begin(gi, blks):
                    s = gi % 2
                    nb = len(blks); T = 128 * nb; tok0 = blks[0] * 128
                    SP.wait(xg_free[s])
                    ev_x = ldx[s].add(sync.dma_start(out=xg[s][:, 0:nb, :],
                                                     in_=xsrc[tok0:tok0 + T, :].rearrange("(b p) d -> p b d", p=128)))
                    return {"s": s, "nb": nb, "T": T, "tok0": tok0, "evs_hT": [], "ev_x": ev_x, "done": 0}

                def stage1_block(cx, part="both"):
                    if cx is None or cx["done"] >= cx["nb"]:
                        return
                    if part in ("norm", "both") and cx.get("pend") is None:
                        stage1_norm(cx)
                    if part in ("tr", "both") and cx.get("pend") is not None:
                        stage1_tr(cx)

                def stage1_norm(cx):
                    bi = cx["done"]
                    s, nb, ev_x = cx["s"], cx["nb"], cx["ev_x"]
                    if True:
                        q = A["bcount"] % 4; sc = q * 3; hs = A["bcount"] % 2; A["bcount"] += 1
                        ACT.wait(ev_x, stat_free[q])
                        e = ACT.sig(scalar.activation(out=junk[:], in_=xg[s][:, bi, :], func=AF.Square, accum_out=st[:, sc:sc + 1]))
                        ACT.wait(e)
                        e = ACT.sig(scalar.activation(out=st[:, sc + 1:sc + 2], in_=st[:, sc:sc + 1], func=AF.Sqrt, bias=EPS, scale=1.0 / D))
                        DVE.wait(e)
                        e = DVE.sig(vector.reciprocal(out=st[:, sc + 2:sc + 3], in_=st[:, sc + 1:sc + 2]))
                        DVE.wait(e, hb_free[hs], ev_x)
                        e_h = DVE.sig(vector.tensor_scalar(out=hb[hs][:], in0=xg[s][:, bi, :], scalar1=st[:, sc + 2:sc + 3], scalar2=None, op0=ALU.mult))
                        stat_free[q] = e_h
                        if bi == nb - 1:
                            xg_free[s] = e_h
                        cx["pend"] = (bi, hs, e_h)

                def stage1_tr(cx):
                    bi, hs, e_h = cx["pend"]
                    cx["pend"] = None
                    cx["done"] += 1
                    s, evs_hT = cx["s"], cx["evs_hT"]
                    if True:
                        PE.wait(e_h, A["tp_free"])
                        for k in range(8):
                            ins = tensor.transpose(out=tp[:, k, :], in_=hb[hs][:, k * 128:(k + 1) * 128], identity=idb[:])
                        e_t = PE.sig(ins)
                        hb_free[hs] = e_t
                        ACT.wait(e_t, hT_free[s])
                        e_c = ACT.sig(scalar.copy(out=hT[s][:, :, bi * 128:(bi + 1) * 128], in_=tp[:]))
                        A["tp_free"] = e_c
                        evs_hT.append(e_c)

                def stage2(gi, blks, ctx, nxt):
                    s, nb, T, tok0, evs_hT = ctx["s"], ctx["nb"], ctx["T"], ctx["tok0"], ctx["evs_hT"]
                    evac = []
                    n_ev = 0
                    for cc in range(13):
                        if cc in (0, 3, 6, 9):
                            stage1_block(nxt, "norm")
                        elif cc in (2, 5, 8, 11):
                            stage1_block(nxt, "tr")
                        j, bank, fr = rot.get()
                        PE.wait(fr, *evs_hT)
                        for k in range(8):
                            ins = tensor.matmul(bank[:, 0:T], lhsT=wres[:, k, cc * 128:(cc + 1) * 128], rhs=hT[s][:, k, 0:T],
                                                start=(k == 0), stop=(k == 7))
                        e_m = PE.sig(ins)
                        E = ACT if n_ev % 2 == 0 else DVE
                        n_ev += 1
                        E.wait(e_m, out_free[s])
                        if E is ACT:
                            e_e = E.sig(scalar.copy(out=qko[s][:, cc, 0:T], in_=bank[:, 0:T]))
                        else:
                            e_e = E.sig(vector.tensor_copy(out=qko[s][:, cc, 0:T], in_=bank[:, 0:T]))
                        rot.rel(j, e_e)
                        evac.append(e_e)
                    for bi in range(nb):
                        for (c0, c1, w) in ((1664, 2176, 512), (2176, 2432, 256)):
                            j, bank, fr = rot.get()
                            PE.wait(fr)
                            for k in range(8):
                                ins = tensor.matmul(bank[:, 0:w], lhsT=hT[s][:, k, bi * 128:(bi + 1) * 128], rhs=wres[:, k, c0:c1],
                                                    start=(k == 0), stop=(k == 7))
                            e_m = PE.sig(ins)
                            E = ACT if n_ev % 2 == 0 else DVE
                            n_ev += 1
                            E.wait(e_m, out_free[s])
                            o0 = 0 if w == 512 else 512
                            if E is ACT:
                                e_e = E.sig(scalar.copy(out=vo[s][:, bi, o0:o0 + w], in_=bank[:, 0:w]))
                            else:
                                e_e = E.sig(vector.tensor_copy(out=vo[s][:, bi, o0:o0 + w], in_=bank[:, 0:w]))
                            rot.rel(j, e_e)
                            evac.append(e_e)
                    hT_free[s] = e_m
                    POOL.wait(*evac)
                    stq[s].add(gpsimd.dma_start(out=qk_v[:, :, tok0:tok0 + T], in_=qko[s][:, :, 0:T]))
                    stq[s].add(gpsimd.dma_start(out=v_scr[tok0:tok0 + T, :].rearrange("(b p) f -> p b f", p=128), in_=vo[s][:, 0:nb, :]))
                    out_free[s] = stq[s].ev()

                ctx_next = stage1_begin(0, groups[0])
                while ctx_next["done"] < ctx_next["nb"]:
                    stage1_block(ctx_next)
                for gi, blks in enumerate(groups):
                    ctx = ctx_next
                    ctx_next = stage1_begin(gi + 1, groups[gi + 1]) if gi + 1 < len(groups) else None
                    stage2(gi, blks, ctx, ctx_next)
                    while ctx_next is not None and ctx_next["done"] < ctx_next["nb"]:
                        stage1_block(ctx_next)
                phase_end([stq[0].ev(), stq[1].ev()])

        def phase_BC(l, xsrc):
            b0, b1 = cfg.rBC[l]
            phase_begin()
            with ExitStack() as es:
                wo = sb(es, "b_wo", [128, 8, D], BF16)
                ttb = sb(es, "b_tt", [128, 8, 1024], BF16)
                bswb = sb(es, "b_bsw", [128, 3, 8, 128], BF16)
                stg = [sb(es, f"b_stg{i}", [128, 1024], F32) for i in range(2)]
                kaT = [sb(es, f"b_kaT{i}", [128, 4, 768], BF16) for i in range(2)]
                qaT = [sb(es, f"b_qaT{i}", [128, 4, 128], BF16) for i in range(2)]
                qbT = [sb(es, f"b_qbT{i}", [128, 4, 128], BF16) for i in range(2)]
                kbT = [sb(es, f"b_kbT{i}", [128, 384], BF16) for i in range(2)]
                vaw = [sb(es, f"b_vaw{i}", [128, 6, 512], BF16) for i in range(2)]
                vbw = [sb(es, f"b_vbw{i}", [128, 3, 256], BF16) for i in range(2)]
                xb = [sb(es, f"b_xb{i}", [128, D], F32) for i in range(3)]
                xb_free = [None, None, None]
                ldxb = [ds["m0"], ds["m1"], ds["m2"]]
                PTn = sb(es, "b_PTn", [128, 6, 8, 128], BF16)
                PTs = sb(es, "b_PTs", [128, 3, 8, 128], BF16)
                rec_n = sb(es, "b_recn", [128, 8, 128], F32)
                rawPn = sb(es, "b_rawPn", [128, 2, 512], F32)
                rawDn = sb(es, "b_rawDn", [128, 2, 512], F32)
                rawPs = sb(es, "b_rawPs", [128, 2, 512], F32)
                rawDs = sb(es, "b_rawDs", [128, 2, 512], F32)
                rec_s = sb(es, "b_recs", [128, 8, 128], F32)
                o_un = sb(es, "b_oun", [128, 8, 128], F32)
                sqb = sb(es, "b_sqb", [128, 8, 128], BF16)
                ab_s = sb(es, "b_abs", [128, 256], F32)
                ab = sb(es, "b_ab", [128, 256], F32)
                o_bf = sb(es, "b_obf", [128, 8, 128], BF16)
                xmid = [sb(es, f"b_xmid{i}", [128, D], F32) for i in range(2)]
                junk = sb(es, "b_junk", [128, D], BF16)
                hm = sb(es, "b_hm", [128, D], BF16)
                hmT = [sb(es, f"b_hmT{i}", [128, 8, 128], BF16) for i in range(2)]
                st = sb(es, "b_st", [128, 4], F32)
                S_rot = Rot([ps(es, f"b_S{i}", [128, 512], F32) for i in range(2)])
                X = ps(es, "b_X", [128, 512], F32)
                P = [ps(es, f"b_P{i}", [128, 512], F32) for i in range(2)]
                Dk = [ps(es, f"b_D{i}", [128, 512], F32) for i in range(2)]
                tp = ps(es, "b_tp", [128, 8, 128], BF16)

                stg_free = [None, None]
                ldw = [ds["w0"], ds["w1"]]
                jobs = []
                for k in range(8):
                    jobs.append(("wo", k))
                for h in range(8):
                    jobs.append(("tt", h))
                for c in range(3):
                    jobs.append(("bsw", c))
                for n, (kind, k) in enumerate(jobs):
                    s = n % 2
                    SP.wait(stg_free[s])
                    if kind == "wo":
                        src_ap = w_out_d[l, k * 128:(k + 1) * 128, :]
                    elif kind == "tt":
                        src_ap = tt_d[l, :, k, :]
                    else:
                        src_ap = bsw_d[:, k, :, :].rearrange("p h q -> p (h q)")
                    ev = ldw[s].add(sync.dma_start(out=stg[s][:], in_=src_ap))
                    DVE.wait(ev)
                    if kind == "wo":
                        gk = gc[:, G[f"grp{l}"] + k:G[f"grp{l}"] + k + 1]
                        ins = vector.tensor_scalar(out=wo[:, k, :], in0=stg[s][:], scalar1=gk, scalar2=None, op0=ALU.mult)
                    elif kind == "tt":
                        ins = vector.tensor_copy(out=ttb[:, k, :], in_=stg[s][:])
                    else:
                        ins = vector.tensor_copy(out=bswb[:, k, :, :].rearrange("p h q -> p (h q)"), in_=stg[s][:])
                    stg_free[s] = DVE.sig(ins)
                wready = [stg_free[0], stg_free[1]]
                PE.wait(wready)

                ldb = [ds["b0"], ds["b1"]]
                stb = [ds["s0"], ds["s1"]]
                blk_free = [None, None]
                out_free = [None, None]
                fr = {"PTn": None, "PTs": None, "PD": None, "oun": None, "sqb": None, "obf": None, "hm": None, "tp": None, "X": None, "rawn": None, "raws": None}

                def make_out_pieces(i, lb, s, tok0, e_sw, e_pvs, ev_x, xs):
                    c = {}

                    def piece0a():
                        ACT.wait(e_sw, fr["sqb"])
                        c["e_sq"] = ACT.sig(scalar.activation(out=sqb[:].rearrange("p k q -> p (k q)"), in_=o_un[:].rearrange("p k q -> p (k q)"), func=AF.Square))

                    def piece0():
                        PE.wait(c["e_sq"], fr["X"])
                        for grp in range(2):
                            for kk in range(4):
                                ins = tensor.matmul(X[:, grp * 128:(grp + 1) * 128], lhsT=ones[:], rhs=sqb[:, grp * 4 + kk, :],
                                                    start=(kk == 0), stop=(kk == 3), skip_group_check=True)
                        e_ss = PE.sig(ins)
                        fr["sqb"] = e_ss
                        ACT.wait(e_ss)
                        c["e_q"] = ACT.sig(scalar.activation(out=ab_s[:], in_=X[:, 0:256], func=AF.Ln, bias=epsc[:, 0:1], scale=1.0 / 512))
                        ACT.wait(c["e_q"])
                        e_ab = ACT.sig(scalar.activation(out=ab[:], in_=ab_s[:], func=AF.Exp, scale=-0.5))
                        POOL.wait(e_ab, e_sw, fr["obf"])
                        DVE.wait(e_ab, e_sw, fr["obf"])
                        evs_ = []
                        for k in range(8):
                            EE, eng_ = (DVE, vector) if k < 4 else (POOL, gpsimd)
                            ins = eng_.tensor_tensor(out=o_bf[:, k, :], in0=o_un[:, k, :], in1=ab[:, (k // 4) * 128:(k // 4 + 1) * 128], op=ALU.mult)
                            if k % 4 == 3:
                                evs_.append(EE.sig(ins))
                        c["e_ob"] = evs_
                        fr["oun"] = [c["e_sq"]] + evs_

                    def piece1(hf):
                        PE.wait(c["e_ob"], c["e_q"], c.get("e_x0"))
                        for k in range(8):
                            ins = tensor.matmul(X[:, 0:512], lhsT=o_bf[:, k, :], rhs=wo[:, k, hf * 512:(hf + 1) * 512],
                                                start=(k == 0), stop=(k == 7), skip_group_check=True)
                        e_op = PE.sig(ins)
                        DVE.wait(e_op, out_free[s], ev_x)
                        e_add = DVE.sig(vector.tensor_tensor(out=xmid[s][:, hf * 512:(hf + 1) * 512], in0=X[:, 0:512],
                                                             in1=xb[xs][:, hf * 512:(hf + 1) * 512], op=ALU.add))
                        if hf == 0:
                            c["e_x0"] = e_add
                        else:
                            fr["obf"] = e_op
                            c["e_xm"] = e_add
                            fr["X"] = e_add
                            xb_free[xs] = e_add

                    def piece2():
                        ACT.wait(c["e_xm"])
                        e = ACT.sig(scalar.activation(out=junk[:], in_=xmid[s][:], func=AF.Square, accum_out=st[:, 0:1]))
                        ACT.wait(e)
                        e = ACT.sig(scalar.activation(out=st[:, 1:2], in_=st[:, 0:1], func=AF.Ln, bias=epsc[:, 0:1], scale=1.0 / D))
                        ACT.wait(e)
                        e = ACT.sig(scalar.activation(out=st[:, 2:3], in_=st[:, 1:2], func=AF.Exp, scale=-0.5))
                        DVE.wait(e, fr["hm"])
                        c["e_hm"] = DVE.sig(vector.tensor_scalar(out=hm[:], in0=xmid[s][:], scalar1=st[:, 2:3], scalar2=None, op0=ALU.mult))

                    def piece3():
                        PE.wait(c["e_hm"], fr["tp"])
                        for k in range(8):
                            ins = tensor.transpose(out=tp[:, k, :], in_=hm[:, k * 128:(k + 1) * 128], identity=idb[:])
                        e_t = PE.sig(ins)
                        fr["hm"] = e_t
                        ACT.wait(e_t, out_free[s])
                        e_c = ACT.sig(scalar.copy(out=hmT[s][:], in_=tp[:]))
                        fr["tp"] = e_c
                        POOL.wait(c["e_xm"], e_c, c["e_hm"])
                        stb[s].add(gpsimd.dma_start(out=xmid_scr[tok0:tok0 + 128, :], in_=xmid[s][:]))
                        stb[s].add(gpsimd.dma_start(out=hmT_v[:, :, 1 + tok0:1 + tok0 + 128], in_=hmT[s][:]))
                        out_free[s] = stb[s].ev()

                    return [piece0, lambda: piece1(0), lambda: piece1(1), piece2, piece3, piece0a]

                pending = None
                pending3 = None
                for i, lb in enumerate(range(b0, b1)):
                    s = i % 2
                    tok0 = lb * 128
                    w0, nch = cfg.win[l][lb]
                    kc0 = 64 * w0; kc1 = 64 * (w0 + 2 * nch)
                    SP.wait(blk_free[s])
                    L = ldb[s]
                    L.add(sync.dma_start(out=kaT[s][:, :, 0:kc1 - kc0], in_=qk_v[:, 4:8, kc0:kc1]))
                    L.add(sync.dma_start(out=qaT[s][:], in_=qk_v[:, 0:4, tok0:tok0 + 128]))
                    L.add(sync.dma_start(out=qbT[s][:], in_=qk_v[:, 8:12, tok0:tok0 + 128]))
                    L.add(sync.dma_start(out=kbT[s][:], in_=qk_scr[12 * 128:13 * 128, tok0 - 128:tok0 + 256]))
                    L.add(sync.dma_start(out=vaw[s][:, 0:nch, :], in_=v_scr[kc0:kc1, 0:512].rearrange("(c p) f -> p c f", p=128)))
                    ev_ld = L.add(sync.dma_start(out=vbw[s][:], in_=v_scr[tok0 - 128:tok0 + 256, 512:768].rearrange("(c p) f -> p c f", p=128)))
                    xs = i % 3
                    SP.wait(xb_free[xs])
                    ev_x = ldxb[xs].add(sync.dma_start(out=xb[xs][:], in_=xsrc[tok0:tok0 + 128, :]))

                    units = [(e, c) for e in range(2) for c in range(nch)]
                    ev_exp = {}

                    def na_S(u):
                        e, c = units[u]
                        j, S, frb = S_rot.get()
                        PE.wait(frb, ev_ld)
                        s0 = 8 - (w0 - 2 * lb + 2 * c)
                        for pair in range(4):
                            tensor.matmul(S[:, pair * 128:(pair + 1) * 128],
                                          lhsT=kaT[s][64 * e:64 * e + 64, pair, c * 128:(c + 1) * 128],
                                          rhs=qaT[s][64 * e:64 * e + 64, pair, :],
                                          start=(pair == 0), stop=False, skip_group_check=True)
                        ins = tensor.matmul(S[:, 0:512], lhsT=idb[:], rhs=ttb[:, e * 4:(e + 1) * 4, s0 * 64:s0 * 64 + 128],
                                            start=False, stop=True, skip_group_check=True)
                        e_s = PE.sig(ins)
                        ACT.wait(e_s, fr["PTn"])
                        Sv = S[:, 0:512].rearrange("p (h a q) -> p h a q", h=4, a=2)
                        for p in range(2):
                            col = ((l * NLB + lb) * 6 + c) * 2 + p
                            ins = scalar.activation(out=PTn[:, c, e * 4:(e + 1) * 4, p * 64:(p + 1) * 64], in_=Sv[:, :, p, :],
                                                    func=AF.Exp, bias=mbna[:, col:col + 1])
                        e_x = ACT.sig(ins)
                        S_rot.rel(j, e_x)
                        ev_exp[u] = e_x

                    def na_PV(u):
                        e, c = units[u]
                        PE.wait(ev_exp[u], fr["PD"])
                        for pair in range(4):
                            bank = P[pair // 2]
                            col0 = (pair % 2) * 256 + e * 128
                            st_flag = (e == 0 and c == 0 and pair % 2 == 0)
                            tensor.matmul(bank[:, col0:col0 + 128], lhsT=vaw[s][:, c, pair * 128:(pair + 1) * 128],
                                          rhs=PTn[:, c, e * 4 + pair, :], start=st_flag, stop=(c == nch - 1), skip_group_check=True)
                        return tensor.matmul(Dk[e][:, 0:512], lhsT=ones[:], rhs=PTn[:, c, e * 4:(e + 1) * 4, :],
                                             start=(c == 0), stop=(c == nch - 1), skip_group_check=True)

                    nu = len(units)
                    LAG = 4
                    for u in range(nu):
                        na_S(u)
                        if u >= LAG:
                            na_PV(u - LAG)
                        if pending3 is not None and u == 1:
                            pending3[0]()
                        if pending3 is not None and u == 7:
                            pending3[1]()
                            pending3 = None
                        if pending is not None and u == 6:
                            pending[5]()
                        if pending is not None and u == 8:
                            pending[0]()
                    for u in range(nu - LAG, nu):
                        ins = na_PV(u)
                    e_pvn = PE.sig(ins)
                    fr["PTn"] = e_pvn
                    if pending is not None:
                        pending[1]()
                    ACT.wait(e_pvn, fr["rawn"])
                    scalar.copy(out=rawPn[:, 0, :], in_=P[0][:, 0:512])
                    scalar.copy(out=rawDn[:, 0, :], in_=Dk[0][:, 0:512])
                    scalar.copy(out=rawPn[:, 1, :], in_=P[1][:, 0:512])
                    e_ca = ACT.sig(scalar.copy(out=rawDn[:, 1, :], in_=Dk[1][:, 0:512]))
                    e_cd = e_ca
                    fr["PD"] = [e_ca]
                    DVE.wait(e_ca, e_cd)
                    e_r = DVE.sig(vector.reciprocal(out=rec_n[:].rearrange("p h q -> p (h q)"), in_=rawDn[:].rearrange("p e q -> p (e q)")))
                    POOL.wait(e_r, e_ca, e_cd, fr["oun"])
                    DVE.wait(e_r, fr["oun"])
                    evs_ = []
                    for e in range(2):
                        EE, eng_ = (DVE, vector) if e == 0 else (POOL, gpsimd)
                        for bk in range(2):
                            src = rawPn[64 * e:64 * e + 64, bk, :].rearrange("p (m t q) -> p m t q", m=2, t=2)[:, :, e, :]
                            ins = eng_.tensor_tensor(out=o_un[64 * e:64 * e + 64, 2 * bk:2 * bk + 2, :], in0=src,
                                                     in1=rec_n[64 * e:64 * e + 64, e * 4 + 2 * bk:e * 4 + 2 * bk + 2, :], op=ALU.mult)
                        evs_.append(EE.sig(ins))
                    e_na = evs_
                    fr["rawn"] = e_na

                    sunits = [(g, c) for g in range(2) for c in range(3)]
                    sw_exp = {}

                    def sw_S(u):
                        g, c = sunits[u]
                        j, S, frb = S_rot.get()
                        PE.wait(frb, ev_ld)
                        tensor.matmul(S[:, 0:512], lhsT=kbT[s][64 * g:64 * g + 64, c * 128:(c + 1) * 128],
                                      rhs=qbT[s][64 * g:64 * g + 64, :, :], start=True, stop=False, skip_group_check=True)
                        ins = tensor.matmul(S[:, 0:512], lhsT=idb[:], rhs=bswb[:, c, g * 4:(g + 1) * 4, :], start=False, stop=True, skip_group_check=True)
                        e_s = PE.sig(ins)
                        ACT.wait(e_s, fr["PTs"])
                        col = lb * 3 + c
                        e_x = ACT.sig(scalar.activation(out=PTs[:, c, g * 4:(g + 1) * 4, :], in_=S[:, 0:512].rearrange("p (h q) -> p h q", h=4),
                                                        func=AF.Exp, bias=mbsw[:, col:col + 1]))
                        S_rot.rel(j, e_x)
                        sw_exp[u] = e_x

                    def sw_PV(u):
                        g, c = sunits[u]
                        PE.wait(sw_exp[u], fr["PD"])
                        tensor.matmul(P[g][:, 0:512], lhsT=vbw[s][:, c, g * 128:(g + 1) * 128], rhs=PTs[:, c, g * 4:(g + 1) * 4, :],
                                      start=(c == 0), stop=(c == 2), skip_group_check=True)
                        ins = tensor.matmul(Dk[g][:, 0:512], lhsT=ones[:], rhs=PTs[:, c, g * 4:(g + 1) * 4, :],
                                            start=(c == 0), stop=False, skip_group_check=True)
                        if c == 2:
                            o0 = (l * 8 + 4 * g) * 128
                            ins = tensor.matmul(Dk[g][:, 0:512], lhsT=ones[0:1, :], rhs=esrow[0:1, o0:o0 + 512],
                                                start=False, stop=True, skip_group_check=True)
                        return ins

                    for u in range(6):
                        sw_S(u)
                        if pending is not None and u == 1:
                            pending[2]()
                    for u in range(6):
                        ins = sw_PV(u)
                    e_pvs = PE.sig(ins)
                    fr["PTs"] = e_pvs
                    ACT.wait(e_pvs, fr["raws"])
                    scalar.copy(out=rawPs[:, 0, :], in_=P[0][:, 0:512])
                    e_ca = ACT.sig(scalar.copy(out=rawDs[:, 0, :], in_=Dk[0][:, 0:512]))
                    DVE.wait(e_pvs, fr["raws"])
                    vector.tensor_copy(out=rawPs[:, 1, :], in_=P[1][:, 0:512])
                    e_cd = DVE.sig(vector.tensor_copy(out=rawDs[:, 1, :], in_=Dk[1][:, 0:512]))
                    pd_sw = [e_ca, e_cd]
                    DVE.wait(e_ca, e_cd)
                    e_r = DVE.sig(vector.reciprocal(out=rec_s[:].rearrange("p h q -> p (h q)"), in_=rawDs[:].rearrange("p g q -> p (g q)")))
                    POOL.wait(e_r, e_ca, e_cd, e_na)
                    DVE.wait(e_r, e_na)
                    rsv = rec_s[:].rearrange("p (g m t) q -> p g m t q", g=2, m=2)
                    evs_ = []
                    for hf in range(2):
                        EE, eng_ = (DVE, vector) if hf == 0 else (POOL, gpsimd)
                        for g in range(2):
                            src = rawPs[64 * hf:64 * hf + 64, g, :].rearrange("p (m t q) -> p m t q", m=2, t=2)[:, :, hf, :]
                            ins = eng_.tensor_tensor(out=o_un[64 * hf:64 * hf + 64, 4 + 2 * g:6 + 2 * g, :], in0=src,
                                                     in1=rsv[64 * hf:64 * hf + 64, g, :, hf, :], op=ALU.mult)
                        evs_.append(EE.sig(ins))
                    e_sw = evs_
                    fr["raws"] = e_sw
                    fr["PD"] = pd_sw
                    blk_free[s] = e_pvs
                    if pending is not None:
                        pending3 = (pending[3], pending[4])
                    pending = make_out_pieces(i, lb, s, tok0, e_sw, e_pvs, ev_x, xs)
                if pending3 is not None:
                    pending3[0]()
                    pending3[1]()
                pending[5]()
                for pc in pending[:5]:
                    pc()
                phase_end([stb[0].ev(), stb[1].ev()])

        def phase_D(l):
            tiles = cfg.tiles[l]
            phase_begin()
            with ExitStack() as es:
                wu = sb(es, "d_wu", [128, 8, 2 * DFF], BF16)
                wd = sb(es, "d_wd", [128, 22, D], BF16)
                with ExitStack() as es2:
                    NSTG = 4
                    stg = [sb(es2, f"d_stg{i}", [128, 2816], F32) for i in range(NSTG)]
                    stg_free = [None] * NSTG
                    ldw = [ds["w0"], ds["w1"], ds["w2"], ds["w3"]]
                    jobs = [("u", k, q) for k in range(8) for q in range(2)] + [("d", k, 0) for k in range(11)]
                    for n, (kind, k, q) in enumerate(jobs):
                        s = n % NSTG
                        EE = DVE if n % 2 == 0 else ACT
                        SP.wait(stg_free[s])
                        if kind == "u":
                            ev = ldw[s].add(sync.dma_start(out=stg[s][:, 0:2816], in_=w_up_d[l, k * 128:(k + 1) * 128, q * 2816:(q + 1) * 2816]))
                            EE.wait(ev)
                            gk = gc[:, G[f"ffn{l}"] + k:G[f"ffn{l}"] + k + 1]
                            if EE is DVE:
                                ins = vector.tensor_scalar(out=wu[:, k, q * 2816:(q + 1) * 2816], in0=stg[s][:, 0:2816], scalar1=gk, scalar2=None, op0=ALU.mult)
                            else:
                                ins = scalar.activation(out=wu[:, k, q * 2816:(q + 1) * 2816], in_=stg[s][:, 0:2816], func=AF.Identity, scale=gk)
                        else:
                            ev = ldw[s].add(sync.dma_start(out=stg[s][:, 0:2048].rearrange("p (c n) -> p c n", c=2),
                                                           in_=w_down_d[l, k * 256:(k + 1) * 256, :].rearrange("(c p) n -> p c n", p=128)))
                            EE.wait(ev)
                            dst = wd[:, 2 * k:2 * k + 2, :].rearrange("p c n -> p (c n)")
                            if EE is DVE:
                                ins = vector.tensor_copy(out=dst, in_=stg[s][:, 0:2048])
                            else:
                                ins = scalar.copy(out=dst, in_=stg[s][:, 0:2048])
                        stg_free[s] = EE.sig(ins)
                    wready = list(stg_free)
                    for E in ENGS:
                        E.wait(wready)
                hTt = [sb(es, f"d_hT{i}", [128, 8, 386], BF16) for i in range(2)]
                actT = sb(es, "d_act", [128, 22, 384], BF16)
                tg = [sb(es, f"d_tg{i}", [128, 384], F32) for i in range(2)]
                tv = [sb(es, f"d_tv{i}", [128, 384], F32) for i in range(2)]
                xm = [sb(es, f"d_xm{i}", [128, D], F32) for i in range(3)]
                xo = [sb(es, f"d_xo{i}", [128, D], F32) for i in range(2)]
                junk = sb(es, "d_junk", [128, D], BF16)
                st = sb(es, "d_st", [128, 4], F32)
                gfin = sb(es, "d_gfin", [128, D], F32) if l == 1 else None
                U = [ps(es, f"d_U{i}", [128, 512], F32) for i in range(4)]
                U_free = [None] * 4
                Y_rot = Rot([ps(es, f"d_Y{i}", [128, 512], F32) for i in range(4)])
                ev_gf = None
                if l == 1:
                    ds["c"].add(sync.dma_start(out=gfin[:], in_=gfin_d[:, :]))
                    ev_gf = ds["c"].ev()

                ldh = [ds["h0"], ds["h1"]]
                ldm = [ds["m0"], ds["m1"], ds["m2"]]
                sto = [ds["o0"], ds["o1"]]
                hTt_free = [None, None]
                xm_free = [None] * 3
                xo_free = [None, None]
                t_free = [None, None]
                xmc = 0; xoc = 0
                for t, (tb0, nb) in enumerate(tiles):
                    s = t % 2
                    T = 128 * nb; tok0 = tb0 * 128
                    SP.wait(hTt_free[s])
                    ev_h = ldh[s].add(sync.dma_start(out=hTt[s][:, :, 0:T + 2], in_=hmT_v[:, :, tok0:tok0 + T + 2]))
                    ev_fix = None
                    if tb0 in cfg.PB:
                        j = cfg.PB.index(tb0)
                        DVE.wait(ev_h)
                        ev_fix = DVE.sig(vector.tensor_scalar(out=hTt[s][:, :, 0:1], in0=hTt[s][:, :, 0:1],
                                                              scalar1=gc[:, G["keep"] + j:G["keep"] + j + 1], scalar2=None, op0=ALU.mult))
                    if (tb0 + nb) in cfg.PB:
                        j = cfg.PB.index(tb0 + nb)
                        DVE.wait(ev_h)
                        ev_fix = DVE.sig(vector.tensor_scalar(out=hTt[s][:, :, T + 1:T + 2], in0=hTt[s][:, :, T + 1:T + 2],
                                                              scalar1=gc[:, G["keep"] + j:G["keep"] + j + 1], scalar2=None, op0=ALU.mult))
                    xm_ev = []
                    xm_slot = []
                    for sbk in range(nb):
                        xs = xmc % 3; xmc += 1
                        SP.wait(xm_free[xs])
                        xm_ev.append(ldm[xs].add(sync.dma_start(out=xm[xs][:], in_=xmid_scr[tok0 + sbk * 128:tok0 + (sbk + 1) * 128, :])))
                        xm_slot.append(xs)
                    e_a = None
                    e_acts = []
                    for jj in range(22):
                        pr = jj % 2
                        e3 = {}
                        for wh, colbase, bi_, tt_ in (("g", jj * 128, 2 * pr, tg[pr]), ("v", DFF + jj * 128, 2 * pr + 1, tv[pr])):
                            bank = U[bi_]
                            PE.wait(U_free[bi_], ev_h, ev_fix)
                            for k in range(8):
                                ins = tensor.matmul(bank[:, 0:T + 2], lhsT=wu[:, k, colbase:colbase + 128], rhs=hTt[s][:, k, 0:T + 2],
                                                    start=(k == 0), stop=(k == 7))
                            e_u = PE.sig(ins)
                            m = colbase // 128
                            ACT.wait(e_u, t_free[pr])
                            e1 = ACT.sig(scalar.activation(out=tt_[:, 0:T], in_=bank[:, 1:T + 1], func=AF.Identity,
                                                           bias=gc[:, G[f"cb{l}"] + m:G[f"cb{l}"] + m + 1],
                                                           scale=gc[:, G[f"cw1{l}"] + m:G[f"cw1{l}"] + m + 1]))
                            DVE.wait(e1)
                            e2 = DVE.sig(vector.scalar_tensor_tensor(out=tt_[:, 0:T], in0=bank[:, 0:T], scalar=gc[:, G[f"cw0{l}"] + m:G[f"cw0{l}"] + m + 1],
                                                                     in1=tt_[:, 0:T], op0=ALU.mult, op1=ALU.add))
                            DVE.wait(e2)
                            e3[wh] = DVE.sig(vector.scalar_tensor_tensor(out=tt_[:, 0:T], in0=bank[:, 2:T + 2], scalar=gc[:, G[f"cw2{l}"] + m:G[f"cw2{l}"] + m + 1],
                                                                         in1=tt_[:, 0:T], op0=ALU.mult, op1=ALU.add))
                            U_free[bi_] = e3[wh]
                        ACT.wait(e3["g"])
                        e_s = ACT.sig(scalar.activation(out=tg[pr][:, 0:T], in_=tg[pr][:, 0:T], func=AF.Silu))
                        POOL.wait(e_s, e3["v"])
                        e_a = POOL.sig(gpsimd.tensor_tensor(out=actT[:, jj, 0:T], in0=tg[pr][:, 0:T], in1=tv[pr][:, 0:T], op=ALU.mult))
                        t_free[pr] = e_a
                        e_acts.append(e_a)
                    hTt_free[s] = e_u
                    for sbk in range(nb):
                        os_ = xoc % 2; xoc += 1
                        xs = xm_slot[sbk]
                        tok = tok0 + sbk * 128
                        for hf in range(2):
                            j, Y, fr = Y_rot.get()
                            PE.wait(fr)
                            for jj in range(22):
                                PE.wait(e_acts[jj])
                                ins = tensor.matmul(Y[:, 0:512], lhsT=actT[:, jj, sbk * 128:(sbk + 1) * 128], rhs=wd[:, jj, hf * 512:(hf + 1) * 512],
                                                    start=(jj == 0), stop=(jj == 21))
                            e_y = PE.sig(ins)
                            DVE.wait(e_y, xm_ev[sbk], xo_free[os_])
                            e_o = DVE.sig(vector.tensor_tensor(out=xo[os_][:, hf * 512:(hf + 1) * 512], in0=Y[:, 0:512],
                                                               in1=xm[xs][:, hf * 512:(hf + 1) * 512], op=ALU.add))
                            Y_rot.rel(j, e_o)
                        xm_free[xs] = e_o
                        if l == 0:
                            POOL.wait(e_o)
                            sto[os_].add(gpsimd.dma_start(out=x1_scr[tok:tok + 128, :], in_=xo[os_][:]))
                        else:
                            ACT.wait(e_o)
                            e = ACT.sig(scalar.activation(out=junk[:], in_=xo[os_][:], func=AF.Square, accum_out=st[:, 0:1]))
                            ACT.wait(e)
                            e = ACT.sig(scalar.activation(out=st[:, 1:2], in_=st[:, 0:1], func=AF.Sqrt, bias=EPS, scale=1.0 / D))
                            DVE.wait(e)
                            e = DVE.sig(vector.reciprocal(out=st[:, 2:3], in_=st[:, 1:2]))
                            DVE.wait(e)
                            e = DVE.sig(vector.tensor_scalar(out=xo[os_][:], in0=xo[os_][:], scalar1=st[:, 2:3], scalar2=None, op0=ALU.mult))
                            POOL.wait(e, ev_gf)
                            e = POOL.sig(gpsimd.tensor_tensor(out=xo[os_][:], in0=xo[os_][:], in1=gfin[:], op=ALU.mult))
                            POOL.wait(e)
                            yt = tok - HALO * 128
                            sto[os_].add(gpsimd.dma_start(out=y_d[yt:yt + 128, :], in_=xo[os_][:]))
                        xo_free[os_] = sto[os_].ev()
                phase_end([sto[0].ev(), sto[1].ev()])

        phase_A(0, xin)
        phase_BC(0, xin)
        phase_D(0)
        phase_A(1, x1_scr)
        phase_BC(1, x1_scr)
        phase_D(1)
        for E in ENGS:
            E.wait(state["phase_ev"])
    return nc


def run_cfg(cfg, xcat, params):
    nc = build_program(cfg)
    maps = host_inputs(cfg, xcat, params)
    res = run_bass_kernel_spmd(nc, maps, core_ids=list(range(cfg.NC)))
    return np.concatenate([np.asarray(r["y"]) for r in res.results], axis=0)


def kernel(x_prompt, x_sample, norm_mix, w_in, rpb, sinks, norm_grp, w_out, norm_ffn, w_up, conv_w, conv_b, w_down, norm_final):
    xp = np.asarray(x_prompt, np.float32)
    xs = np.asarray(x_sample, np.float32)
    T = xp.shape[1]
    xcat = np.concatenate([xp.reshape(-1, D), xs.reshape(-1, D)], axis=0)
    nseq = xp.shape[0] + xs.shape[0]
    cfg = Cfg(8, nseq, T // 128)
    params = dict(norm_mix=np.asarray(norm_mix), w_in=np.asarray(w_in), rpb=np.asarray(rpb), sinks=np.asarray(sinks),
                  norm_grp=np.asarray(norm_grp), w_out=np.asarray(w_out), norm_ffn=np.asarray(norm_ffn), w_up=np.asarray(w_up),
                  conv_w=np.asarray(conv_w), conv_b=np.asarray(conv_b), w_down=np.asarray(w_down), norm_final=np.asarray(norm_final))
    y = run_cfg(cfg, xcat, params)
    y = y.reshape(nseq, T, D)
    return (np.ascontiguousarray(y[:xp.shape[0]]), np.ascontiguousarray(y[xp.shape[0]:]))
```
